# Optimizing a Trainium2 kernel written in Bass

```python
import jax, jax.numpy as jnp
from jax import lax
import numpy as np

D_MODEL = 1024
BATCH = 4
SEQ = 4096
DEPTH = 2
DEC_BATCH = 32
DEC_SEQ = 4
PAST_LEN = 8192
PAGE_SIZE = 128

HEAD_DIM = 64
A_WIDTH = D_MODEL // 2
MOBA_HEADS = (D_MODEL // 2) // HEAD_DIM
C_WIDTH = D_MODEL // 2
C_GROUPS = 8
SB_HEADS = (D_MODEL // 2) // HEAD_DIM
CONV_W = 3
MOBA_BLOCK = 256
MOBA_TOPK = 3
MOBA_QB = 32
GMLP_CHUNK = 128
SB_QB = 128
N_EVEN = (DEPTH + 1) // 2
N_ODD = DEPTH // 2
RMS_EPS = 1e-6
NEG = -1e30

kernel_name = "hybrid_conv_moba_gmlp_stickbreak_step"


def rmsnorm(x, g):
    x32 = x.astype(jnp.float32)
    y = x32 * lax.rsqrt(jnp.mean(x32 * x32, axis=-1, keepdims=True) + RMS_EPS)
    return (y * g.astype(jnp.float32)).astype(x.dtype)


def _pad_rows(x, n, axis=1):
    if n == 0:
        return x
    widths = [(0, 0)] * x.ndim
    widths[axis] = (0, n)
    return jnp.pad(x, widths)


def _split(x, sizes):
    return jnp.split(x, list(np.cumsum(sizes)[:-1]), axis=-1)


def moba_attend(q, qpos, k, v):
    bsz, lq, nh, dh = q.shape
    t = k.shape[1]
    nb = max(-(-t // MOBA_BLOCK), MOBA_TOPK)
    kp = _pad_rows(k, nb * MOBA_BLOCK - t)
    vp = _pad_rows(v, nb * MOBA_BLOCK - t)
    kmean = kp.astype(jnp.float32).reshape(bsz, nb, MOBA_BLOCK, nh, dh).mean(axis=2)
    qb = min(lq, MOBA_QB)
    nq = -(-lq // qb)
    lp = nq * qb
    q_items = _pad_rows(q, lp - lq).reshape(bsz * nq, qb, nh, dh)
    p_items = jnp.tile(_pad_rows(qpos, lp - lq, axis=0).reshape(nq, qb), (bsz, 1))
    b_items = jnp.repeat(jnp.arange(bsz, dtype=jnp.int32), nq)
    blk_off = jnp.arange(MOBA_BLOCK, dtype=jnp.int32)
    head_idx = jnp.arange(nh, dtype=jnp.int32)[:, None, None]
    block_ids = jnp.arange(nb, dtype=jnp.int32)
    slot_ids = jnp.arange(MOBA_TOPK, dtype=jnp.int32)
    scale = dh ** -0.5

    def one_query(bi, qt, pt):
        qblk = pt // MOBA_BLOCK
        gate = jnp.einsum('hd,nhd->hn', qt.astype(jnp.float32), kmean[bi])
        gate = jnp.where(block_ids[None, :] < qblk, gate, NEG)
        _, sel = lax.top_k(gate, MOBA_TOPK)
        blocks = jnp.concatenate(
            [sel.astype(jnp.int32), jnp.full((nh, 1), qblk, jnp.int32)], axis=1)
        kidx = blocks[:, :, None] * MOBA_BLOCK + blk_off
        kg = kp[bi, kidx, head_idx]
        vg = vp[bi, kidx, head_idx]
        slot_ok = jnp.concatenate([slot_ids < qblk, jnp.ones((1,), bool)])
        mask = slot_ok[None, :, None] & (kidx <= pt)
        logits = jnp.einsum('hd,hskd->hsk', qt, kg).astype(jnp.float32) * scale
        logits = jnp.where(mask, logits, NEG).reshape(nh, -1)
        p = jax.nn.softmax(logits, axis=-1).reshape(kidx.shape)
        return jnp.einsum('hsk,hskd->hd', p.astype(vg.dtype), vg)

    def chunk(item):
        bi, qc, pc = item
        return jax.vmap(one_query, in_axes=(None, 0, 0))(bi, qc, pc)

    out = lax.map(chunk, (b_items, q_items, p_items))
    return out.reshape(bsz, lp, nh, dh)[:, :lq]


def stick_breaking_attend(q, qpos, k, v):
    bsz, lq, nh, dh = q.shape
    t = k.shape[1]
    qb = min(lq, SB_QB)
    nq = -(-lq // qb)
    lp = nq * qb
    qc = _pad_rows(q, lp - lq).reshape(bsz, nq, qb, nh, dh).swapaxes(0, 1)
    pc = _pad_rows(qpos, lp - lq, axis=0).reshape(nq, qb)
    kpos = jnp.arange(t, dtype=jnp.int32)
    scale = dh ** -0.5

    def block(item):
        qi, pi = item
        z = jnp.einsum('bqhd,bkhd->bhqk', qi, k).astype(jnp.float32) * scale
        strict = (kpos[None, :] < pi[:, None])[None, None]
        log_keep = jnp.where(strict, jax.nn.log_sigmoid(-z), 0.0)
        between = lax.cumsum(log_keep, axis=3, reverse=True) - log_keep
        w = jnp.where(strict, jnp.exp(jax.nn.log_sigmoid(z) + between), 0.0)
        return jnp.einsum('bhqk,bkhd->bqhd', w.astype(v.dtype), v)

    out = lax.map(block, (qc, pc))
    return out.swapaxes(0, 1).reshape(bsz, lp, nh, dh)[:, :lq]


def chunk_spatial_gate(v, w_s, b_s):
    bsz, l, cw = v.shape
    cs = min(l, GMLP_CHUNK)
    nc = -(-l // cs)
    lp = nc * cs
    vr = _pad_rows(v, lp - l).reshape(bsz, nc, cs, C_GROUPS, cw // C_GROUPS)
    tri = jnp.tril(jnp.ones((cs, cs), dtype=bool))
    wm = jnp.where(tri[None], w_s[:, :cs, :cs], 0)
    mixed = jnp.einsum('gts,bnsgc->bntgc', wm, vr) + b_s[:, :cs].T[None, None, :, :, None]
    return mixed.reshape(bsz, lp, cw)[:, :l]


def even_layer(x, qpos, conv_prev, k_past, v_past, g_pre, g_post, w_in, conv_w, w_out):
    bsz, l, _ = x.shape
    xn = rmsnorm(x, g_pre)
    bw = MOBA_HEADS * HEAD_DIM
    a_b, a_c, a_h, a_z, q, k, v, b_z = _split(
        xn @ w_in, [A_WIDTH] * 4 + [bw] * 4)
    u = a_c * a_h
    ue = jnp.concatenate([conv_prev.astype(u.dtype), u], axis=1)
    conv = ue[:, 0:l] * conv_w[0]
    for j in range(1, CONV_W):
        conv = conv + ue[:, j:j + l] * conv_w[j]
    y_a = a_b * conv * jax.nn.silu(a_z)
    qh = q.reshape(bsz, l, MOBA_HEADS, HEAD_DIM)
    kh = k.reshape(bsz, l, MOBA_HEADS, HEAD_DIM)
    vh = v.reshape(bsz, l, MOBA_HEADS, HEAD_DIM)
    if k_past is None:
        k_all, v_all = kh, vh
    else:
        k_all = jnp.concatenate([k_past.astype(kh.dtype), kh], axis=1)
        v_all = jnp.concatenate([v_past.astype(vh.dtype), vh], axis=1)
    y_b = moba_attend(qh, qpos, k_all, v_all).reshape(bsz, l, bw) * jax.nn.silu(b_z)
    out = jnp.concatenate([y_a, y_b], axis=-1) @ w_out
    return x + rmsnorm(out, g_post), ue[:, -(CONV_W - 1):], kh, vh


def odd_layer(x, qpos, k_past, v_past, g_pre, g_post, w_in, w_s, b_s, w_out):
    bsz, l, _ = x.shape
    xn = rmsnorm(x, g_pre)
    dw = SB_HEADS * HEAD_DIM
    c_u, c_v, c_z, q, k, v, d_z = _split(
        xn @ w_in, [C_WIDTH] * 3 + [dw] * 4)
    y_c = c_u * chunk_spatial_gate(c_v, w_s, b_s) * jax.nn.silu(c_z)
    qh = q.reshape(bsz, l, SB_HEADS, HEAD_DIM)
    kh = k.reshape(bsz, l, SB_HEADS, HEAD_DIM)
    vh = v.reshape(bsz, l, SB_HEADS, HEAD_DIM)
    if k_past is None:
        k_all, v_all = kh, vh
    else:
        k_all = jnp.concatenate([k_past.astype(kh.dtype), kh], axis=1)
        v_all = jnp.concatenate([v_past.astype(vh.dtype), vh], axis=1)
    y_d = stick_breaking_attend(qh, qpos, k_all, v_all).reshape(bsz, l, dw) * jax.nn.silu(d_z)
    out = jnp.concatenate([y_c, y_d], axis=-1) @ w_out
    open_start = ((l - 1) // GMLP_CHUNK) * GMLP_CHUNK
    return x + rmsnorm(out, g_post), kh, vh, c_v[:, open_start:]


def setup_inputs(seed: int = 0) -> dict:
    key = jax.random.key(seed)
    ks = jax.random.split(key, 20)
    f32 = jnp.float32
    n_pages = PAST_LEN // PAGE_SIZE
    n_used = DEC_BATCH * n_pages
    n_phys = n_used + max(1, n_used // 4)
    bw = MOBA_HEADS * HEAD_DIM
    dw = SB_HEADS * HEAD_DIM
    in_e = 4 * A_WIDTH + 4 * bw
    in_o = 3 * C_WIDTH + 4 * dw

    def nrm(k, shape, s=1.0):
        return jax.random.normal(k, shape, f32) * s

    page_table = jax.random.permutation(ks[7], n_phys)[:n_used].reshape(
        DEC_BATCH, n_pages).astype(jnp.int32)
    return {
        "x_prompt": nrm(ks[0], (BATCH, SEQ, D_MODEL)),
        "x_sample": nrm(ks[1], (DEC_BATCH, DEC_SEQ, D_MODEL)),
        "state_conv": nrm(ks[2], (N_EVEN, DEC_BATCH, CONV_W - 1, A_WIDTH)),
        "cache_k_moba": nrm(ks[3], (N_EVEN, n_phys, PAGE_SIZE, MOBA_HEADS, HEAD_DIM)),
        "cache_v_moba": nrm(ks[4], (N_EVEN, n_phys, PAGE_SIZE, MOBA_HEADS, HEAD_DIM)),
        "cache_k_sb": nrm(ks[5], (N_ODD, n_phys, PAGE_SIZE, SB_HEADS, HEAD_DIM)),
        "cache_v_sb": nrm(ks[6], (N_ODD, n_phys, PAGE_SIZE, SB_HEADS, HEAD_DIM)),
        "page_table": page_table,
        "norm_pre_e": 1.0 + nrm(ks[8], (N_EVEN, D_MODEL), 0.1),
        "norm_post_e": 1.0 + nrm(ks[9], (N_EVEN, D_MODEL), 0.1),
        "w_in_e": nrm(ks[10], (N_EVEN, D_MODEL, in_e), D_MODEL ** -0.5),
        "conv_w": nrm(ks[11], (N_EVEN, CONV_W, A_WIDTH), CONV_W ** -0.5),
        "w_out_e": nrm(ks[12], (N_EVEN, A_WIDTH + bw, D_MODEL), (A_WIDTH + bw) ** -0.5),
        "norm_pre_o": 1.0 + nrm(ks[13], (N_ODD, D_MODEL), 0.1),
        "norm_post_o": 1.0 + nrm(ks[14], (N_ODD, D_MODEL), 0.1),
        "w_in_o": nrm(ks[15], (N_ODD, D_MODEL, in_o), D_MODEL ** -0.5),
        "gmlp_w": nrm(ks[16], (N_ODD, C_GROUPS, GMLP_CHUNK, GMLP_CHUNK), GMLP_CHUNK ** -0.5),
        "gmlp_b": nrm(ks[17], (N_ODD, C_GROUPS, GMLP_CHUNK), 0.1),
        "w_out_o": nrm(ks[18], (N_ODD, C_WIDTH + dw, D_MODEL), (C_WIDTH + dw) ** -0.5),
    }


def reference(x_prompt, x_sample, state_conv, cache_k_moba, cache_v_moba, cache_k_sb, cache_v_sb,
              page_table, norm_pre_e, norm_post_e, w_in_e, conv_w, w_out_e,
              norm_pre_o, norm_post_o, w_in_o, gmlp_w, gmlp_b, w_out_o):
    n_dec, n_pages = page_table.shape
    past_len = n_pages * cache_k_moba.shape[2]
    pos_p = jnp.arange(x_prompt.shape[1], dtype=jnp.int32)
    pos_s = past_len + jnp.arange(x_sample.shape[1], dtype=jnp.int32)

    def gather_past(cache):
        return cache[page_table].reshape(n_dec, past_len, cache.shape[2], cache.shape[3])[:, :, :, :] if False else \
            cache[page_table].reshape(n_dec, past_len, cache.shape[2], cache.shape[3])

    hp, hs = x_prompt, x_sample
    conv_p, conv_s, kmb_p, vmb_p, kmb_s, vmb_s = [], [], [], [], [], []
    ksb_p, vsb_p, ksb_s, vsb_s, gv_p, gv_s = [], [], [], [], [], []
    for i in range(DEPTH):
        j = i // 2
        if i % 2 == 0:
            zero_conv = jnp.zeros((hp.shape[0], CONV_W - 1, A_WIDTH), hp.dtype)
            hp, c1, k1, v1 = even_layer(hp, pos_p, zero_conv, None, None, norm_pre_e[j],
                                        norm_post_e[j], w_in_e[j], conv_w[j], w_out_e[j])
            hs, c2, k2, v2 = even_layer(hs, pos_s, state_conv[j], gather_past(cache_k_moba[j]),
                                        gather_past(cache_v_moba[j]), norm_pre_e[j],
                                        norm_post_e[j], w_in_e[j], conv_w[j], w_out_e[j])
            conv_p.append(c1); conv_s.append(c2)
            kmb_p.append(k1); vmb_p.append(v1); kmb_s.append(k2); vmb_s.append(v2)
        else:
            hp, k1, v1, g1 = odd_layer(hp, pos_p, None, None, norm_pre_o[j], norm_post_o[j],
                                       w_in_o[j], gmlp_w[j], gmlp_b[j], w_out_o[j])
            hs, k2, v2, g2 = odd_layer(hs, pos_s, gather_past(cache_k_sb[j]),
                                       gather_past(cache_v_sb[j]), norm_pre_o[j], norm_post_o[j],
                                       w_in_o[j], gmlp_w[j], gmlp_b[j], w_out_o[j])
            ksb_p.append(k1); vsb_p.append(v1); ksb_s.append(k2); vsb_s.append(v2)
            gv_p.append(g1); gv_s.append(g2)

    y_prompt, y_sample = hp, hs
    conv_prompt, conv_sample = jnp.stack(conv_p), jnp.stack(conv_s)
    k_moba_prompt, v_moba_prompt = jnp.stack(kmb_p), jnp.stack(vmb_p)
    k_moba_sample, v_moba_sample = jnp.stack(kmb_s), jnp.stack(vmb_s)
    k_sb_prompt, v_sb_prompt = jnp.stack(ksb_p), jnp.stack(vsb_p)
    k_sb_sample, v_sb_sample = jnp.stack(ksb_s), jnp.stack(vsb_s)
    gmlp_v_prompt, gmlp_v_sample = jnp.stack(gv_p), jnp.stack(gv_s)
    return (y_prompt, y_sample, conv_prompt, conv_sample, k_moba_prompt, v_moba_prompt,
            k_moba_sample, v_moba_sample, k_sb_prompt, v_sb_prompt, k_sb_sample, v_sb_sample,
            gmlp_v_prompt, gmlp_v_sample)
```

```python
import types
import numpy as np
from contextlib import ExitStack
import concourse.bass as bass
import concourse.mybir as mybir
from concourse.bass_utils import run_bass_kernel_spmd

F32 = mybir.dt.float32
BF16 = mybir.dt.bfloat16
I32 = mybir.dt.int32
AF = mybir.ActivationFunctionType
ALU = mybir.AluOpType
AX = mybir.AxisListType

COMPUTE = ("pe", "act", "dve", "pool")
NDMA_SEMS = 6
SEM_EPOCH = 30000

D = 1024
SEQ = 4096
NT = SEQ // 128
EPS = 1e-6
SCALE = 0.125
NEGBIG = -30000.0


class Op:
    __slots__ = ("eng", "fn", "reads", "writes", "deps", "is_dma", "queue", "signal",
                 "sem", "val", "prewait", "dk")

    def __init__(self, eng, fn, reads, writes, is_dma=False, queue=None):
        self.eng = eng
        self.fn = fn
        self.reads = reads
        self.writes = writes
        self.deps = []
        self.is_dma = is_dma
        self.queue = queue
        self.signal = False
        self.sem = None
        self.val = 0
        self.prewait = None


def _freeze(fn):
    if fn.__closure__ is None:
        return fn
    cells = []
    for c in fn.__closure__:
        try:
            cells.append(types.CellType(c.cell_contents))
        except ValueError:
            cells.append(c)
    return types.FunctionType(fn.__code__, fn.__globals__, fn.__name__, fn.__defaults__, tuple(cells))


class StopBuild(Exception):
    pass


class Prog:
    stop_at = None

    def __init__(self, nc, same_engine_sync=True):
        self.nc = nc
        self.ops = []
        self.last_writer = {}
        self.readers = {}
        self.dma_writers = {}
        self.dcnt = {"sp": 0, "act": 0, "pool": 0}
        self.same_engine_sync = same_engine_sync

    def _add(self, op):
        deps = set()
        for r in op.reads:
            w = self.last_writer.get(r)
            if w is not None:
                deps.add(w)
            deps.update(self.dma_writers.get(r, {}).values())
        for w_ in op.writes:
            w = self.last_writer.get(w_)
            if w is not None:
                deps.add(w)
            deps.update(self.dma_writers.get(w_, {}).values())
            for rd in self.readers.get(w_, ()):
                deps.add(rd)
        i = len(self.ops)
        if self.stop_at is not None and i >= self.stop_at:
            raise StopBuild()
        op.fn = _freeze(op.fn)
        op.deps = sorted(deps)
        self.ops.append(op)
        for r in op.reads:
            self.readers.setdefault(r, []).append(i)
        for w_ in op.writes:
            self.last_writer[w_] = i
            self.readers[w_] = []
            if op.is_dma:
                self.dma_writers.setdefault(w_, {})[(op.queue, op.dk % NDMA_SEMS)] = i
            else:
                self.dma_writers.pop(w_, None)
        return i

    def barrier(self, fns):
        names = sorted(set(self.last_writer) | set(self.readers))
        for eng, fn in fns:
            if eng in ("sp",):
                self.dma(eng, fn, writes=names)
            else:
                self._add(Op(eng, fn, (), tuple(names)))

    def op(self, eng, fn, reads=(), writes=()):
        writes = tuple(writes) + tuple(r for r in reads if r[0] == "p" and r[1].isupper() and r not in writes)
        return self._add(Op(eng, fn, tuple(reads), tuple(writes)))

    def dma(self, queue, fn, reads=(), writes=()):
        o = Op("dma_" + queue, fn, tuple(reads), tuple(writes), is_dma=True, queue=queue)
        o.dk = self.dcnt[queue]
        self.dcnt[queue] += 1
        return self._add(o)

    def emit(self):
        nc = self.nc
        ops = self.ops
        for o in ops:
            for d in o.deps:
                p = ops[d]
                if p.is_dma:
                    p.signal = True
                    continue
                same = (p.eng == o.eng) and not o.is_dma
                if same and (p.eng == "pe" or not self.same_engine_sync):
                    continue
                p.signal = True
        for o in ops:
            if o.is_dma:
                o.signal = True
        with ExitStack() as st:
            sems = {e: st.enter_context(nc.semaphore("s_" + e)) for e in COMPUTE}
            dsem = {q: [st.enter_context(nc.semaphore("d_%s%d" % (q, k))) for k in range(NDMA_SEMS)]
                    for q in ("sp", "act", "pool")}
            cnt = {e: 0 for e in COMPUTE}
            ep = {e: 0 for e in COMPUTE}
            for o in ops:
                if o.is_dma:
                    k = o.dk
                    o.sem = dsem[o.queue][k % NDMA_SEMS]
                    o.val = 16 * (k // NDMA_SEMS + 1)
                    o.prewait = (o.sem, o.val - 16) if o.val > 16 else None
                elif o.signal:
                    if cnt[o.eng] >= SEM_EPOCH:
                        ep[o.eng] += 1
                        sems[o.eng] = st.enter_context(nc.semaphore("s_%s_%d" % (o.eng, ep[o.eng])))
                        cnt[o.eng] = 0
                    cnt[o.eng] += 1
                    o.sem = sems[o.eng]
                    o.val = cnt[o.eng]
            issue_eng = {"pe": "pe", "act": "act", "dve": "dve", "pool": "pool",
                         "dma_sp": "sp", "dma_act": "act", "dma_pool": "pool"}
            streams = {"pe": [], "act": [], "dve": [], "pool": [], "sp": []}
            for i, o in enumerate(ops):
                streams[issue_eng[o.eng]].append(i)
            block = st.enter_context(nc.Block())

            def run_stream(name, eng):
                waited = {}
                for i in streams[name]:
                    o = ops[i]
                    need = {}
                    for d in o.deps:
                        p = ops[d]
                        if p.sem is None:
                            continue
                        if need.get(p.sem, (0, None))[0] < p.val:
                            need[p.sem] = (p.val, p.sem)
                    if o.prewait is not None:
                        s, v = o.prewait
                        if need.get(s, (0, None))[0] < v:
                            need[s] = (v, s)
                    for key, (v, s) in need.items():
                        if waited.get(key, 0) < v:
                            eng.wait_ge(s, v)
                            waited[key] = v
                    ins = o.fn(eng)
                    if o.signal:
                        ins.then_inc(o.sem, 16 if o.is_dma else 1)
                if name == "sp":
                    last = {}
                    for o in ops:
                        if o.sem is not None and last.get(o.sem, (0,))[0] < o.val:
                            last[o.sem] = (o.val, o.sem)
                    for key, (v, s) in last.items():
                        if waited.get(key, 0) < v:
                            eng.wait_ge(s, v)

            @block.tensor
            def _(e):
                run_stream("pe", e)

            @block.scalar
            def _(e):
                run_stream("act", e)

            @block.vector
            def _(e):
                run_stream("dve", e)

            @block.gpsimd
            def _(e):
                run_stream("pool", e)

            @block.sync
            def _(e):
                run_stream("sp", e)


C_IDENT, C_TRI, C_ONES, C_CM, C_PEN, C_TRIL, C_OH = 0, 128, 256, 384, 896, 2944, 3072
C_PIDX, C_A, C_BM, C_BS, C_HM, C_OHC, C_REP, C_BD = 5120, 5121, 5153, 5185, 5217, 5225, 6249, 6377
C_W = 6505


def make_consts():
    c = np.zeros((128, C_W), np.float32)
    i = np.arange(128)
    c[:, C_IDENT:C_IDENT + 128] = np.eye(128)
    c[:, C_TRI:C_TRI + 128] = (i[:, None] >= i[None, :])
    c[:, C_ONES:C_ONES + 128] = 1.0
    t256 = np.arange(256)
    c[:, C_CM:C_CM + 256] = (i[:, None] <= t256[None, :])
    c[:, C_CM + 256:C_CM + 512] = (128 + i[:, None] <= t256[None, :])
    t512 = np.arange(512)
    for j in range(4):
        c[:, C_PEN + 512 * j:C_PEN + 512 * (j + 1)] = np.where(128 * j + i[:, None] < t512[None, :], 0.0, NEGBIG)
    c[:, C_TRIL:C_TRIL + 128] = (i[None, :] <= i[:, None])
    for n in range(16):
        c[n, C_OH + 128 * n:C_OH + 128 * (n + 1)] = 1.0
    col = np.arange(32)
    c[:, C_PIDX] = i
    c[:, C_A:C_A + 32] = (i[:, None] // 4 == col[None, :])
    c[:, C_BM:C_BM + 32] = (i[:, None] % 4 <= col[None, :] // 8)
    c[:, C_BS:C_BS + 32] = (i[:, None] % 4 < col[None, :] // 8)
    c[0:32, C_HM:C_HM + 8] = (col[:, None] % 8 == np.arange(8)[None, :])
    for n in range(32):
        c[:, C_OHC + 32 * n + n] = 1.0
    c[0:4, C_REP:C_REP + 128] = (np.arange(4)[:, None] == i[None, :] % 4)
    c[:, C_BD:C_BD + 128] = (i[:, None] // 4 == i[None, :] // 4) & (i[:, None] % 4 <= i[None, :] % 4)
    return c


def build(do_prompt=True, do_sample=True, n_phys=2560, same_engine_sync=True, nblk=8, stop_at=None, npr=4, nsq=32):
    nc = bass.Bass("TRN2", target_bir_lowering=False)
    dt_in = {}

    def din(name, shape, dt=F32):
        dt_in[name] = nc.dram_tensor(name, shape, dt, kind="ExternalInput").ap()
        return dt_in[name]

    def dout(name, shape, dt=F32):
        return nc.dram_tensor(name, shape, dt, kind="ExternalOutput").ap()

    consts = din("consts", [128, C_W])
    xp = din("xp", [npr * SEQ, D])
    npre_e = din("norm_pre_e", [D]); npost_e = din("norm_post_e", [D])
    npre_o = din("norm_pre_o", [D]); npost_o = din("norm_post_o", [D])
    w_in_e = din("w_in_e", [D, 4096]); w_out_e = din("w_out_e", [D, D])
    w_in_o = din("w_in_o", [D, 3584]); w_out_o = din("w_out_o", [D, D])
    conv_w = din("conv_w", [3, 512])
    gmlp_w = din("gmlp_w", [8, 128, 128]); gmlp_b = din("gmlp_b", [8, 128])

    o_y = dout("y_prompt", [npr * SEQ, D])
    o_conv = dout("conv_prompt", [npr * 2, 512])
    o_kmb = dout("k_moba_prompt", [npr * SEQ, 512]); o_vmb = dout("v_moba_prompt", [npr * SEQ, 512])
    o_ksb = dout("k_sb_prompt", [npr * SEQ, 512]); o_vsb = dout("v_sb_prompt", [npr * SEQ, 512])
    o_gv = dout("gmlp_v_prompt", [npr * 128, 512])
    x1d = nc.dram_tensor("x1_scratch", [npr * SEQ, D], F32, kind="Internal").ap()
    if do_sample:
        xs = din("xs", [128, D])
        stc = din("stc", [64, 512])
        ptab = din("ptab", [nsq * 64], I32)
        ck_m = din("ck_m", [n_phys * 128, 512]); cv_m = din("cv_m", [n_phys * 128, 512])
        ck_s = din("ck_s", [n_phys * 128, 512]); cv_s = din("cv_s", [n_phys * 128, 512])
        o_ys = dout("y_sample", [128, D])
        o_convs = dout("conv_sample", [64, 512])
        o_kms = dout("k_moba_sample", [128, 512]); o_vms = dout("v_moba_sample", [128, 512])
        o_kss = dout("k_sb_sample", [128, 512]); o_vss = dout("v_sb_sample", [128, 512])
        o_gvs = dout("gmlp_v_sample", [128, 512])
        xs1d = nc.dram_tensor("xs1_scratch", [128, D], F32, kind="Internal").ap()
        ybd = nc.dram_tensor("yb_scratch", [128, 512], F32, kind="Internal").ap()

    with ExitStack() as st:
        def sb(name, shape, dt):
            return st.enter_context(nc.sbuf_tensor(name, shape, dt))

        def ps(name, shape, dt):
            return st.enter_context(nc.psum_tensor(name, shape, dt))

        P = Prog(nc, same_engine_sync=same_engine_sync)

        ident = sb("ident", [128, 128], BF16)
        identf = sb("identf", [128, 128], F32)
        tri = sb("tri", [128, 128], BF16)
        onesm = sb("onesm", [128, 128], BF16)
        cm = sb("cm", [128, 512], F32)
        pen = sb("pen", [128, 2048], BF16)
        tril = sb("tril", [128, 128], F32)
        oneh = sb("oneh", [16, 2048], BF16)
        P.dma("pool", lambda e: e.dma_start(out=ident[:], in_=consts[:, C_IDENT:C_IDENT + 128]), writes=["ident"])
        P.dma("sp", lambda e: e.dma_start(out=identf[:], in_=consts[:, C_IDENT:C_IDENT + 128]), writes=["identf"])
        P.dma("pool", lambda e: e.dma_start(out=tri[:], in_=consts[:, C_TRI:C_TRI + 128]), writes=["tri"])
        P.dma("pool", lambda e: e.dma_start(out=onesm[:], in_=consts[:, C_ONES:C_ONES + 128]), writes=["onesm"])
        P.dma("sp", lambda e: e.dma_start(out=cm[:], in_=consts[:, C_CM:C_CM + 512]), writes=["cm"])
        P.dma("pool", lambda e: e.dma_start(out=pen[:], in_=consts[:, C_PEN:C_PEN + 2048]), writes=["pen"])
        P.dma("sp", lambda e: e.dma_start(out=tril[:], in_=consts[:, C_TRIL:C_TRIL + 128]), writes=["tril"])
        P.dma("pool", lambda e: e.dma_start(out=oneh[:], in_=consts[0:16, C_OH:C_OH + 2048]), writes=["oneh"])

        wout = sb("wout", [128, 8, D], BF16)
        NWB = 4
        wbuf = [sb("wbuf%d" % i, [128, 8, 128], BF16) for i in range(NWB)]
        wtok = [sb("wtok%d" % i, [128, 8, 512], BF16) for i in range(1)]
        arena = sb("arena", [128, 16384], F32)
        KT = arena[:, 0:8192].bitcast(BF16).rearrange("p (c k) -> p c k", c=4)
        V = arena[:, 8192:16384].bitcast(BF16).rearrange("p (t n) -> p t n", t=NT)
        xt = sb("xt", [128, D], F32)
        xn = sb("xn", [128, D], BF16)
        xnT = sb("xnT", [128, 8, 512], BF16)
        gpre = sb("gpre", [128, 8], F32)
        gpost = sb("gpost", [128, D], F32)
        ssq = sb("ssq", [128, 1], F32)
        rstd = sb("rstd", [128, 1], F32)
        junk = sb("junk", [128, D], F32)
        qT = sb("qT", [128, 4, 512], BF16)
        qT32 = sb("qT32", [128, 4, 512], F32)
        yT = sb("yT", [128, 8, 512], BF16)
        zs = sb("zs", [128, 4, 512], F32)
        tA = sb("tA", [128, 512], F32)
        tB = sb("tB", [128, 512], F32)
        tC = sb("tC", [128, 512], F32)
        ubuf = sb("ubuf", [128, 514], F32)
        ucarry = sb("ucarry", [128, 4, 2], F32)
        cwT = sb("cwT", [128, 4, 3], F32)
        tokf = [sb("tokf%d" % i, [128, 512], F32) for i in range(2)]
        E = [sb("E%d" % i, [128, 512], F32) for i in range(2)]
        L = [sb("L%d" % i, [128, 512], BF16) for i in range(2)]
        X = [sb("X%d" % i, [128, 512], F32) for i in range(2)]
        PT = [sb("PT%d" % i, [128, 512], BF16) for i in range(2)]
        Lsum = sb("Lsum", [128, 512], F32)
        Lsumb = sb("Lsumb", [128, 512], BF16)
        kmT = sb("kmT", [128, 4, 16], F32)
        gsb = sb("gsb", [128, 16], F32)
        m8 = sb("m8", [128, 8], F32)
        self_ = sb("sel", [128, 16], F32)
        selT = sb("selT", [16, 256], BF16)
        rden = sb("rden", [128, 256], F32)
        onecol = sb("onecol", [128, 1], F32)
        wmT = sb("wmT", [128, 8, 128], BF16)
        wnat = sb("wnat", [128, 128], F32)
        bB = sb("bB", [128, 4, 128], F32)
        cvb = sb("cvb", [128, 512], BF16)

        pAB = ps("pAB", [128, 1024], F32)
        pT = ps("pT", [128, 1024], BF16)
        pS = [ps("pS%d" % i, [128, 512], F32) for i in range(2)]
        pC = ps("pC", [128, 512], F32)
        pACC = [ps("pACC%d" % i, [128, 512], F32) for i in range(2)]

        epsc = sb("epsc", [128, 1], F32)
        P.op("dve", lambda e: e.memset(epsc[:], EPS), writes=["epsc"])
        P.op("dve", lambda e: e.memset(onecol[:], 1.0), writes=["onecol"])
        P.op("dve", lambda e: e.memset(gsb[:], -1e30), writes=["gsb"])

        ctr = {"w": 0, "wt": 0, "ab": 0, "s": 0, "acc": 0, "tok": 0, "e": 0}

        def load_g(npre, npost):
            P.dma("sp", lambda e: e.dma_start(out=gpre[:], in_=npre.rearrange("(c p) -> p c", p=128),
                                              allow_slow_non_contiguous=True), writes=["gpre"])
            P.dma("sp", lambda e: e.dma_start(out=gpost[:], in_=npost.partition_broadcast(128)), writes=["gpost"])

        def load_wout(w):
            P.dma("pool", lambda e: e.dma_start(out=wout[:], in_=w.rearrange("(c p) n -> p c n", p=128)),
                  writes=["wout"])

        def norm_transpose(src_rows, col0, rname=None):
            P.dma("sp", lambda e: e.dma_start(out=xt[:], in_=src_rows), reads=([rname] if rname else []), writes=["xt"])
            P.op("act", lambda e: e.activation(out=junk[:], in_=xt[:], func=AF.Square, accum_out=ssq[:]),
                 reads=["xt"], writes=["junk", "ssq"])
            P.op("act", lambda e: e.activation(out=rstd[:], in_=ssq[:], func=AF.Ln, scale=1.0 / D, bias=epsc[:, 0:1]),
                 reads=["ssq", "epsc"], writes=["rstd"])
            P.op("act", lambda e: e.activation(out=rstd[:], in_=rstd[:], func=AF.Exp, scale=-0.5),
                 reads=["rstd"], writes=["rstd"])
            P.op("dve", lambda e: e.tensor_scalar(out=xn[:], in0=xt[:], scalar1=rstd[:, 0:1], scalar2=None,
                                                  op0=ALU.mult), reads=["xt", "rstd"], writes=["xn"])
            for c in range(8):
                P.op("pe", lambda e, c=c: e.transpose(pT[:, c * 128:(c + 1) * 128], xn[:, c * 128:(c + 1) * 128],
                                                      ident[:]), reads=["xn", "ident"], writes=["pT"])
            P.op("dve", lambda e: e.tensor_tensor(
                out=xnT[:, :, col0:col0 + 128], in0=pT[:].rearrange("p (c t) -> p c t", c=8),
                in1=gpre[:].unsqueeze(2).to_broadcast([128, 8, 128]), op=ALU.mult),
                reads=["pT", "gpre"], writes=["xnT"])

        def proj_feat(w_in, col0, ntok=512):
            wb = wbuf[ctr["w"] % NWB]
            wn = "wbuf%d" % (ctr["w"] % NWB)
            ctr["w"] += 1
            P.dma("pool", lambda e: e.dma_start(
                out=wb[:], in_=w_in[:, col0:col0 + 128].rearrange("(c p) n -> p c n", p=128)), writes=[wn])
            half = ctr["ab"] % 2
            ctr["ab"] += 1
            pn = "pAB%d" % half
            dst = pAB[:, half * 512:half * 512 + ntok]
            for kc in range(8):
                P.op("pe", lambda e, kc=kc: e.matmul(dst, wb[:, kc, :], xnT[:, kc, 0:ntok],
                                                     start=(kc == 0), stop=(kc == 7)),
                     reads=[wn, "xnT"], writes=[pn])
            return dst, pn

        def load_wtok(w_in, col0):
            i = 0
            P.dma("pool", lambda e: e.dma_start(
                out=wtok[i][:], in_=w_in[:, col0:col0 + 512].rearrange("(c p) n -> p c n", p=128)),
                writes=["wtok%d" % i])
            return wtok[i], "wtok%d" % i

        def proj_tok(wt, wtn, tcol0):
            half = ctr["ab"] % 2
            ctr["ab"] += 1
            pn = "pAB%d" % half
            dst = pAB[:, half * 512:(half + 1) * 512]
            for kc in range(8):
                P.op("pe", lambda e, kc=kc: e.matmul(dst, xnT[:, kc, tcol0:tcol0 + 128], wt[:, kc, :],
                                                     start=(kc == 0), stop=(kc == 7)),
                     reads=[wtn, "xnT"], writes=[pn])
            return dst, pn

        def out_proj_tile(blk, ti, src_rows, dst_rows, rname=None, wname="dram_res"):
            tc0 = ti * 128
            for nh in range(2):
                for c in range(8):
                    P.op("pe", lambda e, c=c, nh=nh: e.matmul(
                        pAB[:, nh * 512:(nh + 1) * 512], yT[:, c, tc0:tc0 + 128], wout[:, c, nh * 512:(nh + 1) * 512],
                        start=(c == 0), stop=(c == 7)), reads=["yT", "wout"], writes=["pAB%d" % nh])
            P.dma("sp", lambda e: e.dma_start(out=xt[:], in_=src_rows), reads=([rname] if rname else []), writes=["xt"])
            P.op("act", lambda e: e.activation(out=junk[:], in_=pAB[:], func=AF.Square, accum_out=ssq[:]),
                 reads=["pAB0", "pAB1"], writes=["junk", "ssq"])
            P.op("act", lambda e: e.activation(out=rstd[:], in_=ssq[:], func=AF.Ln, scale=1.0 / D, bias=epsc[:, 0:1]),
                 reads=["ssq", "epsc"], writes=["rstd"])
            P.op("act", lambda e: e.activation(out=rstd[:], in_=rstd[:], func=AF.Exp, scale=-0.5),
                 reads=["rstd"], writes=["rstd"])
            P.op("dve", lambda e: e.scalar_tensor_tensor(out=junk[:], in0=pAB[:], scalar=rstd[:, 0:1], in1=gpost[:],
                                                         op0=ALU.mult, op1=ALU.mult),
                 reads=["pAB0", "pAB1", "rstd", "gpost"], writes=["junk"])
            P.op("dve", lambda e: e.tensor_tensor(out=xt[:], in0=xt[:], in1=junk[:], op=ALU.add),
                 reads=["xt", "junk"], writes=["xt"])
            P.dma("sp", lambda e: e.dma_start(out=dst_rows, in_=xt[:]), reads=["xt"], writes=[wname])

        def tok_store(psrc, pn, dram_rows, extra=None):
            i = ctr["tok"] % 2
            ctr["tok"] += 1
            tn = "tokf%d" % i
            P.op("act", lambda e: e.activation(out=tokf[i][:], in_=psrc, func=AF.Copy), reads=[pn], writes=[tn])
            P.dma("sp", lambda e: e.dma_start(out=dram_rows, in_=tokf[i][:]), reads=[tn], writes=["dram_kv"])
            return tokf[i], tn

        def layer0_setup():
            load_g(npre_e, npost_e)
            load_wout(w_out_e)
            for j in range(3):
                P.dma("sp", lambda e, j=j: e.dma_start(out=cwT[:, :, j], in_=conv_w[j].rearrange("(c p) -> p c", p=128),
                                                       allow_slow_non_contiguous=True), writes=["cwT"])

        def layer0(sq):
            R0 = sq * SEQ
            P.op("dve", lambda e: e.memset(ucarry[:], 0.0), writes=["ucarry"])
            P.op("dve", lambda e: e.memset(gsb[:], -1e30), writes=["gsb"])
            for blk in range(nblk):
                r0 = blk * 512
                for ti in range(4):
                    norm_transpose(xp[R0 + r0 + ti * 128:R0 + r0 + (ti + 1) * 128, :], ti * 128)
                wt, wtn = load_wtok(w_in_e, 2048 + 512)
                for ti in range(4):
                    g = blk * 4 + ti
                    psrc, pn = proj_tok(wt, wtn, ti * 128)
                    tf, tn = tok_store(psrc, pn, o_kmb[R0 + g * 128:R0 + (g + 1) * 128, :])
                    for p in range(4):
                        P.op("pe", lambda e, p=p, ti=ti, tf=tf: e.matmul(
                            pC[:, 256 + 4 * (ti % 2) + p:256 + 4 * (ti % 2) + p + 1], tf[:, p * 128:(p + 1) * 128], onecol[:],
                            start=True, stop=True), reads=[tn, "onecol"], writes=["pC"])
                    if ti % 2 == 1:
                        P.op("dve", lambda e, g=g: e.tensor_copy(out=kmT[:, :, g // 2], in_=pC[:, 256:260]),
                             reads=["pC"], writes=["kmT"])
                        P.op("dve", lambda e, g=g: e.tensor_tensor(out=kmT[:, :, g // 2], in0=kmT[:, :, g // 2],
                                                                   in1=pC[:, 260:264], op=ALU.add),
                             reads=["pC", "kmT"], writes=["kmT"])
                wt, wtn = load_wtok(w_in_e, 2048 + 1024)
                for ti in range(4):
                    g = blk * 4 + ti
                    psrc, pn = proj_tok(wt, wtn, ti * 128)
                    P.op("dve", lambda e, g=g, psrc=psrc: e.tensor_copy(out=V[:, g, :], in_=psrc),
                         reads=[pn], writes=["V%d" % g])
                    tok_store(psrc, pn, o_vmb[R0 + g * 128:R0 + (g + 1) * 128, :])
                for p in range(4):
                    psrc, pn = proj_feat(w_in_e, 2048 + p * 128)
                    P.op("act", lambda e, p=p, psrc=psrc: e.activation(out=qT[:, p, :], in_=psrc, func=AF.Copy),
                         reads=[pn], writes=["qT"])
                    P.op("dve", lambda e, p=p, psrc=psrc: e.tensor_copy(out=qT32[:, p, :], in_=psrc),
                         reads=[pn], writes=["qT32"])
                for p in range(4):
                    psrc, pn = proj_feat(w_in_e, 2048 + 512 + p * 128)
                    P.op("act", lambda e, p=p, psrc=psrc: e.activation(out=KT[:, p, r0:r0 + 512], in_=psrc, func=AF.Copy),
                         reads=[pn], writes=["KT%d" % blk])
                for p in range(4):
                    psrc, pn = proj_feat(w_in_e, 2048 + 1536 + p * 128)
                    P.op("act", lambda e, p=p, psrc=psrc: e.activation(out=zs[:, p, :], in_=psrc, func=AF.Silu),
                         reads=[pn], writes=["zs"])
                for c in range(4):
                    psrc, pn = proj_feat(w_in_e, 512 + c * 128)
                    P.op("act", lambda e, psrc=psrc: e.activation(out=tA[:], in_=psrc, func=AF.Copy),
                         reads=[pn], writes=["tA"])
                    psrc, pn = proj_feat(w_in_e, 1024 + c * 128)
                    P.op("dve", lambda e, c=c: e.tensor_copy(out=ubuf[:, 0:2], in_=ucarry[:, c, :]),
                         reads=["ucarry"], writes=["ubuf"])
                    P.op("dve", lambda e, psrc=psrc: e.tensor_tensor(out=ubuf[:, 2:514], in0=psrc, in1=tA[:], op=ALU.mult),
                         reads=[pn, "tA"], writes=["ubuf"])
                    P.op("dve", lambda e, c=c: e.tensor_copy(out=ucarry[:, c, :], in_=ubuf[:, 512:514]),
                         reads=["ubuf"], writes=["ucarry"])
                    P.op("dve", lambda e, c=c: e.tensor_scalar(out=tB[:], in0=ubuf[:, 0:512], scalar1=cwT[:, c, 0:1],
                                                               scalar2=None, op0=ALU.mult),
                         reads=["ubuf", "cwT"], writes=["tB"])
                    P.op("dve", lambda e, c=c: e.scalar_tensor_tensor(out=tB[:], in0=ubuf[:, 1:513], scalar=cwT[:, c, 1:2],
                                                                      in1=tB[:], op0=ALU.mult, op1=ALU.add),
                         reads=["ubuf", "cwT", "tB"], writes=["tB"])
                    P.op("dve", lambda e, c=c: e.scalar_tensor_tensor(out=tB[:], in0=ubuf[:, 2:514], scalar=cwT[:, c, 2:3],
                                                                      in1=tB[:], op0=ALU.mult, op1=ALU.add),
                         reads=["ubuf", "cwT", "tB"], writes=["tB"])
                    psrc, pn = proj_feat(w_in_e, 1536 + c * 128)
                    P.op("act", lambda e, psrc=psrc: e.activation(out=tC[:], in_=psrc, func=AF.Silu),
                         reads=[pn], writes=["tC"])
                    P.op("dve", lambda e: e.tensor_tensor(out=tB[:], in0=tB[:], in1=tC[:], op=ALU.mult),
                         reads=["tB", "tC"], writes=["tB"])
                    psrc, pn = proj_feat(w_in_e, 0 + c * 128)
                    P.op("dve", lambda e, c=c, psrc=psrc: e.tensor_tensor(out=yT[:, c, :], in0=psrc, in1=tB[:], op=ALU.mult),
                         reads=[pn, "tB"], writes=["yT"])
                if blk == nblk - 1:
                    for j in range(2):
                        P.dma("sp", lambda e, j=j: e.dma_start(out=o_conv[sq * 2 + j].rearrange("(c p) -> p c", p=128), in_=ucarry[:, :, j],
                                                               allow_slow_non_contiguous=True), reads=["ucarry"], writes=["o_conv"])
                for half in range(2):
                    g = blk * 2 + half
                    q0 = half * 256
                    for h in range(8):
                        p, hr = h // 2, (h % 2) * 64
                        need_sel = g >= 4
                        if need_sel:
                            for qt in range(2):
                                P.op("pe", lambda e, qt=qt: e.matmul(
                                    pC[:, 0:g], qT32[hr:hr + 64, p, q0 + qt * 128:q0 + (qt + 1) * 128],
                                    kmT[hr:hr + 64, p, 0:g], start=True, stop=True),
                                    reads=["qT32", "kmT"], writes=["pC"])
                                P.op("dve", lambda e: e.tensor_copy(out=gsb[:, 0:g], in_=pC[:, 0:g]),
                                     reads=["pC"], writes=["gsb"])
                                P.op("dve", lambda e: e.max(out=m8[:], in_=gsb[:]), reads=["gsb"], writes=["m8"])
                                P.op("dve", lambda e: e.tensor_scalar(out=self_[:], in0=gsb[:], scalar1=m8[:, 2:3],
                                                                      scalar2=None, op0=ALU.is_ge),
                                     reads=["gsb", "m8"], writes=["sel"])
                                P.op("pe", lambda e: e.transpose(pC[0:16, 128:256], self_[:], identf[:]),
                                     reads=["sel", "identf"], writes=["pC"])
                                P.op("dve", lambda e, qt=qt: e.tensor_copy(out=selT[:, qt * 128:(qt + 1) * 128],
                                                                           in_=pC[0:16, 128:256]),
                                     reads=["pC"], writes=["selT"])
                        acc = pACC[ctr["acc"] % 2]
                        accn = "pACC%d" % (ctr["acc"] % 2)
                        ctr["acc"] += 1
                        nkt = 2 * (g + 1)
                        for kt in range(nkt):
                            n = kt // 2
                            si = ctr["s"] % 2
                            ctr["s"] += 1
                            pSn = "pS%d" % si
                            P.op("pe", lambda e, kt=kt, si=si: e.matmul(
                                pS[si][:, 0:256], KT[hr:hr + 64, p, kt * 128:(kt + 1) * 128], qT[hr:hr + 64, p, q0:q0 + 256],
                                start=True, stop=True), reads=["KT%d" % (kt // 4), "qT"], writes=[pSn])
                            ei = ctr["e"] % 2
                            ctr["e"] += 1
                            if n == g:
                                P.op("act", lambda e, si=si, ei=ei: e.activation(out=E[ei][:, 0:256], in_=pS[si][:, 0:256],
                                                                                func=AF.Exp, scale=SCALE),
                                     reads=[pSn], writes=["E%d" % ei])
                                j = kt % 2
                                P.op("dve", lambda e, ei=ei, j=j: e.tensor_tensor(out=PT[ei][:, 0:256], in0=E[ei][:, 0:256],
                                                                                  in1=cm[:, j * 256:(j + 1) * 256], op=ALU.mult),
                                     reads=["E%d" % ei, "cm"], writes=["PT%d" % ei])
                            elif need_sel:
                                if kt % 2 == 0:
                                    P.op("pe", lambda e, n=n: e.matmul(pC[:, 256:512], oneh[:, n * 128:(n + 1) * 128], selT[:],
                                                                       start=True, stop=True),
                                         reads=["oneh", "selT"], writes=["pC"])
                                P.op("act", lambda e, si=si, ei=ei: e.activation(out=E[ei][:, 0:256], in_=pS[si][:, 0:256],
                                                                                func=AF.Exp, scale=SCALE),
                                     reads=[pSn], writes=["E%d" % ei])
                                P.op("dve", lambda e, ei=ei: e.tensor_tensor(out=PT[ei][:, 0:256], in0=E[ei][:, 0:256],
                                                                             in1=pC[:, 256:512], op=ALU.mult),
                                     reads=["E%d" % ei, "pC"], writes=["PT%d" % ei])
                            else:
                                P.op("act", lambda e, si=si, ei=ei: e.activation(out=PT[ei][:, 0:256], in_=pS[si][:, 0:256],
                                                                                func=AF.Exp, scale=SCALE),
                                     reads=[pSn], writes=["PT%d" % ei])
                            P.op("pe", lambda e, kt=kt, ei=ei, acc=acc: e.matmul(
                                acc[:, 0:256], V[:, kt, p * 128:(p + 1) * 128], PT[ei][:, 0:256],
                                start=(kt == 0), stop=(kt == nkt - 1), skip_group_check=True), reads=["V%d" % kt, "PT%d" % ei], writes=[accn])
                            P.op("pe", lambda e, kt=kt, ei=ei, acc=acc: e.matmul(
                                acc[:, 256:512], onesm[:], PT[ei][:, 0:256],
                                start=False, stop=(kt == nkt - 1), skip_group_check=True), reads=["onesm", "PT%d" % ei], writes=[accn])
                        P.op("dve", lambda e, acc=acc: e.reciprocal(out=rden[hr:hr + 64, :], in_=acc[hr:hr + 64, 256:512]),
                             reads=[accn], writes=["rden"])
                        P.op("dve", lambda e, acc=acc: e.tensor_tensor(out=rden[hr:hr + 64, :], in0=acc[hr:hr + 64, 0:256],
                                                                       in1=rden[hr:hr + 64, :], op=ALU.mult),
                             reads=[accn, "rden"], writes=["rden"])
                        P.op("dve", lambda e: e.tensor_tensor(out=yT[hr:hr + 64, 4 + p, q0:q0 + 256], in0=rden[hr:hr + 64, :],
                                                              in1=zs[hr:hr + 64, p, q0:q0 + 256], op=ALU.mult),
                             reads=["rden", "zs"], writes=["yT"])
                for ti in range(4):
                    rr = r0 + ti * 128
                    out_proj_tile(blk, ti, xp[R0 + rr:R0 + rr + 128, :], x1d[R0 + rr:R0 + rr + 128, :], wname="x1d")

        def layer1_setup():
            load_g(npre_o, npost_o)
            load_wout(w_out_o)
            for g in range(8):
                P.dma("sp", lambda e, g=g: e.dma_start(out=wnat[:], in_=gmlp_w[g]), writes=["wnat"])
                P.op("dve", lambda e: e.tensor_tensor(out=wnat[:], in0=wnat[:], in1=tril[:], op=ALU.mult),
                     reads=["wnat", "tril"], writes=["wnat"])
                P.op("pe", lambda e: e.transpose(pC[:, 0:128], wnat[:], identf[:]), reads=["wnat", "identf"], writes=["pC"])
                P.op("dve", lambda e, g=g: e.tensor_copy(out=wmT[:, g, :], in_=pC[:, 0:128]), reads=["pC"], writes=["wmT"])
                P.dma("sp", lambda e, g=g: e.dma_start(out=bB[(g % 2) * 64:(g % 2) * 64 + 64, g // 2, :],
                                                       in_=gmlp_b[g].partition_broadcast(64)), writes=["bB"])

        def layer1(sq):
            R0 = sq * SEQ
            for blk in range(nblk):
                r0 = blk * 512
                for ti in range(4):
                    norm_transpose(x1d[R0 + r0 + ti * 128:R0 + r0 + (ti + 1) * 128, :], ti * 128, rname="x1d")
                wt, wtn = load_wtok(w_in_o, 1536 + 512)
                for ti in range(4):
                    g = blk * 4 + ti
                    psrc, pn = proj_tok(wt, wtn, ti * 128)
                    tok_store(psrc, pn, o_ksb[R0 + g * 128:R0 + (g + 1) * 128, :])
                wt, wtn = load_wtok(w_in_o, 1536 + 1024)
                for ti in range(4):
                    g = blk * 4 + ti
                    psrc, pn = proj_tok(wt, wtn, ti * 128)
                    P.op("dve", lambda e, g=g, psrc=psrc: e.tensor_copy(out=V[:, g, :], in_=psrc),
                         reads=[pn], writes=["V%d" % g])
                    tok_store(psrc, pn, o_vsb[R0 + g * 128:R0 + (g + 1) * 128, :])
                for p in range(4):
                    psrc, pn = proj_feat(w_in_o, 1536 + p * 128)
                    P.op("act", lambda e, p=p, psrc=psrc: e.activation(out=qT[:, p, :], in_=psrc, func=AF.Copy),
                         reads=[pn], writes=["qT"])
                for p in range(4):
                    psrc, pn = proj_feat(w_in_o, 1536 + 512 + p * 128)
                    P.op("act", lambda e, p=p, psrc=psrc: e.activation(out=KT[:, p, r0:r0 + 512], in_=psrc, func=AF.Copy),
                         reads=[pn], writes=["KT%d" % blk])
                for p in range(4):
                    psrc, pn = proj_feat(w_in_o, 1536 + 1536 + p * 128)
                    P.op("act", lambda e, p=p, psrc=psrc: e.activation(out=zs[:, p, :], in_=psrc, func=AF.Silu),
                         reads=[pn], writes=["zs"])
                wt, wtn = load_wtok(w_in_o, 512)
                for ti in range(4):
                    g = blk * 4 + ti
                    psrc, pn = proj_tok(wt, wtn, ti * 128)
                    P.op("dve", lambda e, psrc=psrc: e.tensor_copy(out=cvb[:], in_=psrc), reads=[pn], writes=["cvb"])
                    if g == NT - 1:
                        tok_store(psrc, pn, o_gv[sq * 128:(sq + 1) * 128, :])
                    for p in range(4):
                        for j in range(2):
                            gg = 2 * p + j
                            P.op("pe", lambda e, p=p, j=j, gg=gg: e.matmul(
                                pC[:, j * 128:(j + 1) * 128], cvb[:, p * 128:(p + 1) * 128], wmT[:, gg, :],
                                start=True, stop=True), reads=["cvb", "wmT"], writes=["pC"])
                        for j in range(2):
                            P.op("dve", lambda e, p=p, j=j, ti=ti: e.tensor_tensor(
                                out=tA[j * 64:(j + 1) * 64, p * 128:(p + 1) * 128],
                                in0=pC[j * 64:(j + 1) * 64, j * 128:(j + 1) * 128],
                                in1=bB[j * 64:(j + 1) * 64, p, :], op=ALU.add), reads=["pC", "bB"], writes=["tA"])
                    P.op("pool", lambda e, ti=ti: e.tensor_copy(
                        out=mixT[:, :, ti * 128:(ti + 1) * 128], in_=tA[:].rearrange("q (p t) -> q p t", p=4)),
                        reads=["tA"], writes=["qT32"])
                for c in range(4):
                    psrc, pn = proj_feat(w_in_o, 1024 + c * 128)
                    P.op("act", lambda e, psrc=psrc: e.activation(out=tC[:], in_=psrc, func=AF.Silu),
                         reads=[pn], writes=["tC"])
                    P.op("dve", lambda e, c=c: e.tensor_tensor(out=tC[:], in0=tC[:], in1=mixT[:, c, :], op=ALU.mult),
                         reads=["tC", "qT32"], writes=["tC"])
                    psrc, pn = proj_feat(w_in_o, 0 + c * 128)
                    P.op("dve", lambda e, c=c, psrc=psrc: e.tensor_tensor(out=yT[:, c, :], in0=psrc, in1=tC[:], op=ALU.mult),
                         reads=[pn, "tC"], writes=["yT"])
                nkt = 4 * (blk + 1)
                for h in range(8):
                    p, hr = h // 2, (h % 2) * 64
                    acc = pACC[ctr["acc"] % 2]
                    accn = "pACC%d" % (ctr["acc"] % 2)
                    ctr["acc"] += 1
                    for idx, kt in enumerate(range(nkt - 1, -1, -1)):
                        si = ctr["s"] % 2
                        ctr["s"] += 1
                        pSn = "pS%d" % si
                        ei = ctr["e"] % 2
                        ctr["e"] += 1
                        P.op("pe", lambda e, kt=kt, si=si: e.matmul(
                            pS[si][:], KT[hr:hr + 64, p, kt * 128:(kt + 1) * 128], qT[hr:hr + 64, p, :],
                            start=True, stop=True), reads=["KT%d" % (kt // 4), "qT"], writes=[pSn])
                        if kt >= nkt - 4:
                            j = kt - (nkt - 4)
                            P.op("dve", lambda e, si=si, ei=ei, j=j: e.scalar_tensor_tensor(
                                out=X[ei][:], in0=pS[si][:], scalar=SCALE, in1=pen[:, j * 512:(j + 1) * 512],
                                op0=ALU.mult, op1=ALU.add), reads=[pSn, "pen"], writes=["X%d" % ei])
                            P.op("act", lambda e, ei=ei: e.activation(out=E[ei][:], in_=X[ei][:], func=AF.Exp),
                                 reads=["X%d" % ei], writes=["E%d" % ei])
                        else:
                            P.op("act", lambda e, si=si, ei=ei: e.activation(out=E[ei][:], in_=pS[si][:], func=AF.Exp, scale=SCALE),
                                 reads=[pSn], writes=["E%d" % ei])
                        P.op("act", lambda e, ei=ei: e.activation(out=L[ei][:], in_=E[ei][:], func=AF.Ln, bias=1.0),
                             reads=["E%d" % ei], writes=["L%d" % ei])
                        P.op("pe", lambda e, ei=ei, idx=idx: e.matmul(pC[:], tri[:], L[ei][:], start=True, stop=(idx == 0)),
                             reads=["tri", "L%d" % ei], writes=["pC"])
                        if idx > 0:
                            P.op("pe", lambda e: e.matmul(pC[:], onesm[:], Lsumb[:], start=False, stop=True),
                                 reads=["onesm", "Lsumb"], writes=["pC"])
                        P.op("act", lambda e, ei=ei: e.activation(out=X[ei][:], in_=pC[:], func=AF.Exp, scale=-1.0),
                             reads=["pC"], writes=["X%d" % ei])
                        P.op("dve", lambda e, ei=ei: e.tensor_tensor(out=PT[ei][:], in0=E[ei][:], in1=X[ei][:], op=ALU.mult),
                             reads=["E%d" % ei, "X%d" % ei], writes=["PT%d" % ei])
                        P.op("pe", lambda e, kt=kt, ei=ei, acc=acc, idx=idx: e.matmul(
                            acc[:], V[:, kt, p * 128:(p + 1) * 128], PT[ei][:],
                            start=(idx == 0), stop=(idx == nkt - 1)), reads=["V%d" % kt, "PT%d" % ei], writes=[accn])
                        if idx < nkt - 1:
                            if idx == 0:
                                P.op("pool", lambda e, ei=ei: e.tensor_copy(out=Lsum[:], in_=L[ei][:]),
                                     reads=["L%d" % ei], writes=["Lsum"])
                            else:
                                P.op("pool", lambda e, ei=ei: e.tensor_tensor(out=Lsum[:], in0=Lsum[:], in1=L[ei][:], op=ALU.add),
                                     reads=["L%d" % ei, "Lsum"], writes=["Lsum"])
                            P.op("pool", lambda e: e.tensor_copy(out=Lsumb[:], in_=Lsum[:]), reads=["Lsum"], writes=["Lsumb"])
                    P.op("dve", lambda e, acc=acc: e.tensor_tensor(out=yT[hr:hr + 64, 4 + p, :], in0=acc[hr:hr + 64, :],
                                                                   in1=zs[hr:hr + 64, p, :], op=ALU.mult),
                         reads=[accn, "zs"], writes=["yT"])
                for ti in range(4):
                    rr = r0 + ti * 128
                    out_proj_tile(blk, ti, x1d[R0 + rr:R0 + rr + 128, :], o_y[R0 + rr:R0 + rr + 128, :], rname="x1d")

        if do_sample:
            pidx = sb("pidx", [128, 1], F32)
            Acon = sb("Acon", [128, 32], F32)
            BMc = sb("BMc", [128, 32], F32)
            BSc = sb("BSc", [128, 32], F32)
            HMc = sb("HMc", [32, 8], F32)
            ohc = sb("ohc", [128, 1024], BF16)
            repc = sb("repc", [4, 128], F32)
            bdm = sb("bdm", [128, 128], F32)
            trif = sb("trif", [128, 128], F32)
            onesf = sb("onesf", [128, 128], F32)
            for tl, nm, c0, w, q in ((pidx, "pidx", C_PIDX, 1, "sp"), (Acon, "Acon", C_A, 32, "sp"), (BMc, "BMc", C_BM, 32, "sp"),
                                     (BSc, "BSc", C_BS, 32, "sp"), (ohc, "ohc", C_OHC, 1024, "pool"), (bdm, "bdm", C_BD, 128, "sp"),
                                     (trif, "trif", C_TRI, 128, "sp"), (onesf, "onesf", C_ONES, 128, "sp")):
                P.dma(q, lambda e, tl=tl, c0=c0, w=w: e.dma_start(out=tl[:], in_=consts[:, c0:c0 + w], allow_slow_non_contiguous=True), writes=[nm])
            P.dma("sp", lambda e: e.dma_start(out=HMc[:], in_=consts[0:32, C_HM:C_HM + 8]), writes=["HMc"])
            P.dma("sp", lambda e: e.dma_start(out=repc[:], in_=consts[0:4, C_REP:C_REP + 128]), writes=["repc"])
            ueb = sb("ueb", [128, 4, 32, 6], F32)
            QF = sb("QF", [128, 4, 4, 8], F32)
            QB = sb("QB", [128, 4, 4, 8], BF16)
            kssb = junk[0:32, 512:1024]
            ksT = sb("ksT", [128, 4, 32], F32)
            gs = sb("gs", [32, 32], F32)
            m8s = sb("m8s", [32, 8], F32)
            sels = sb("sels", [32, 32], F32)
            selTs = sb("selTs", [32, 32], F32)
            D3 = Lsum[:].bitcast(BF16)[0:32, :].rearrange("n (a b) -> n a b", a=32)
            PTsum = sb("PTsum", [128, 32], F32)
            rdn = sb("rdn", [32, 1], F32)
            ytmp = junk[0:32, 0:512]
            ysb = sb("ysb", [32, 64], F32)
            cvs = Lsumb
            BDg = wmT
            w4 = sb("w4", [4, 4], F32)
            m1 = sb("m1", [4, 128], F32)
            bBs = bB[:].rearrange("p c (s t) -> p c s t", t=4)
            barc = sb("barc", [128, 1], F32)
            NCOL = 65 * 32
            sE = arena[:, 0:2080]
            sL = arena[:, 2080:4160]
            sCw = arena[:, 4160:6240]
            sTA = arena[:, 6240:8320]
            sPT = arena[:, 8320:9360].bitcast(BF16)
            sIdx = arena[:, 9360:11408].bitcast(I32)
            sPtf = arena[:, 11408:13456]
            sPti = arena[:, 13456:15504].bitcast(I32)
            Kpg = [E[0][:].bitcast(BF16)[:, 0:512], E[0][:].bitcast(BF16)[:, 512:1024]]
            Vpg = [E[1][:].bitcast(BF16)[:, 0:512], E[1][:].bitcast(BF16)[:, 512:1024]]
            KTp = [X[0][:].bitcast(BF16)[:, 0:512], X[0][:].bitcast(BF16)[:, 512:1024]]

            def do_barrier():
                P.barrier([
                    ("pe", lambda e: e.matmul(pC[0:1, 0:1], onecol[:], onecol[:], start=True, stop=True)),
                    ("act", lambda e: e.activation(out=barc[:], in_=onecol[:], func=AF.Copy)),
                    ("dve", lambda e: e.memset(barc[:], 0.0)),
                    ("pool", lambda e: e.memset(barc[:], 0.0)),
                    ("sp", lambda e: e.dma_start(out=barc[0:1, 0:1], in_=consts[0:1, 0:1])),
                    ("pe", lambda e: e.matmul(pC[0:1, 0:1], onecol[:], onecol[:], start=True, stop=True)),
                    ("act", lambda e: e.activation(out=barc[:], in_=onecol[:], func=AF.Copy)),
                    ("dve", lambda e: e.memset(barc[:], 0.0)),
                    ("pool", lambda e: e.memset(barc[:], 0.0)),
                ])

            def sample_phase(layer):
                do_barrier()
                w_in = w_in_e if layer == 0 else w_in_o
                src = xs if layer == 0 else xs1d
                dst = xs1d if layer == 0 else o_ys
                okd, ovd = (o_kms, o_vms) if layer == 0 else (o_kss, o_vss)
                ckd, cvd = (ck_m, cv_m) if layer == 0 else (ck_s, cv_s)
                qoff = 2048 if layer == 0 else 1536
                norm_transpose(src[0:128, :], 0, rname=("xs1d" if layer == 1 else None))
                nidx = nsq * 64
                P.dma("sp", lambda e: e.dma_start(out=sPti[:, 0:nidx], in_=ptab.partition_broadcast(128)), writes=["sPti"])
                P.op("dve", lambda e: e.tensor_copy(out=sPtf[:, 0:nidx], in_=sPti[:, 0:nidx]), reads=["sPti"], writes=["sPtf"])
                P.op("dve", lambda e: e.tensor_scalar(out=sPtf[:, 0:nidx], in0=sPtf[:, 0:nidx], scalar1=128.0, scalar2=pidx[:, 0:1],
                                                      op0=ALU.mult, op1=ALU.add), reads=["sPtf", "pidx"], writes=["sPtf"])
                P.op("dve", lambda e: e.tensor_copy(out=sIdx[:, 0:nidx], in_=sPtf[:, 0:nidx]), reads=["sPtf"], writes=["sIdx"])
                wt, wtn = load_wtok(w_in, qoff + 512)
                psrc, pn = proj_tok(wt, wtn, 0)
                tok_store(psrc, pn, okd[:, :])
                wt, wtn = load_wtok(w_in, qoff + 1024)
                psrc, pn = proj_tok(wt, wtn, 0)
                P.op("dve", lambda e, psrc=psrc: e.tensor_copy(out=cvb[:], in_=psrc), reads=[pn], writes=["cvb"])
                tok_store(psrc, pn, ovd[:, :])
                for p in range(4):
                    psrc, pn = proj_feat(w_in, qoff + p * 128, ntok=128)
                    P.op("dve", lambda e, p=p, psrc=psrc: e.tensor_copy(out=qT32[:, p, 0:128], in_=psrc), reads=[pn], writes=["qT32"])
                for p in range(4):
                    psrc, pn = proj_feat(w_in, qoff + 512 + p * 128, ntok=128)
                    P.op("act", lambda e, p=p, psrc=psrc: e.activation(out=qT[:, p, 0:128], in_=psrc, func=AF.Copy), reads=[pn], writes=["qT"])
                for p in range(4):
                    psrc, pn = proj_feat(w_in, qoff + 1536 + p * 128, ntok=128)
                    P.op("act", lambda e, p=p, psrc=psrc: e.activation(out=zs[:, p, 0:128], in_=psrc, func=AF.Silu), reads=[pn], writes=["zs"])
                if layer == 0:
                    P.dma("sp", lambda e: e.dma_start(out=tokf[0][0:64, :], in_=stc), writes=["tokf0"])
                    for c in range(4):
                        P.op("pe", lambda e, c=c: e.transpose(pC[:, 0:64], tokf[0][0:64, c * 128:(c + 1) * 128], identf[0:64, 0:64]),
                             reads=["tokf0", "identf"], writes=["pC"])
                        P.op("dve", lambda e, c=c: e.tensor_copy(out=ueb[:, c, :, 0:2], in_=pC[:, 0:64].rearrange("p (s j) -> p s j", j=2)),
                             reads=["pC"], writes=["ueb"])
                    for c in range(4):
                        psrc, pn = proj_feat(w_in, 512 + c * 128, ntok=128)
                        P.op("act", lambda e, psrc=psrc: e.activation(out=tA[:, 0:128], in_=psrc, func=AF.Copy), reads=[pn], writes=["tA"])
                        psrc, pn = proj_feat(w_in, 1024 + c * 128, ntok=128)
                        P.op("dve", lambda e, c=c, psrc=psrc: e.tensor_tensor(
                            out=ueb[:, c, :, 2:6], in0=psrc.rearrange("p (s t) -> p s t", t=4),
                            in1=tA[:, 0:128].rearrange("p (s t) -> p s t", t=4), op=ALU.mult), reads=[pn, "tA"], writes=["ueb"])
                        tBv = tB[:, 0:128].rearrange("p (s t) -> p s t", t=4)
                        P.op("dve", lambda e, c=c, tBv=tBv: e.tensor_scalar(out=tBv, in0=ueb[:, c, :, 0:4], scalar1=cwT[:, c, 0:1],
                                                                            scalar2=None, op0=ALU.mult), reads=["ueb", "cwT"], writes=["tB"])
                        for j in (1, 2):
                            P.op("dve", lambda e, c=c, j=j, tBv=tBv: e.scalar_tensor_tensor(
                                out=tBv, in0=ueb[:, c, :, j:j + 4], scalar=cwT[:, c, j:j + 1], in1=tBv, op0=ALU.mult, op1=ALU.add),
                                reads=["ueb", "cwT", "tB"], writes=["tB"])
                        psrc, pn = proj_feat(w_in, 1536 + c * 128, ntok=128)
                        P.op("act", lambda e, psrc=psrc: e.activation(out=tC[:, 0:128], in_=psrc, func=AF.Silu), reads=[pn], writes=["tC"])
                        P.op("dve", lambda e: e.tensor_tensor(out=tB[:, 0:128], in0=tB[:, 0:128], in1=tC[:, 0:128], op=ALU.mult),
                             reads=["tB", "tC"], writes=["tB"])
                        psrc, pn = proj_feat(w_in, 0 + c * 128, ntok=128)
                        P.op("dve", lambda e, c=c, psrc=psrc: e.tensor_tensor(out=yT[:, c, 0:128], in0=psrc, in1=tB[:, 0:128], op=ALU.mult),
                             reads=[pn, "tB"], writes=["yT"])
                        P.op("dve", lambda e, c=c: e.tensor_copy(out=tA[:, 0:64].rearrange("p (s j) -> p s j", j=2), in_=ueb[:, c, :, 4:6]),
                             reads=["ueb"], writes=["tA"])
                        P.op("pe", lambda e, c=c: e.transpose(pC[0:64, 128:256], tA[:, 0:64], identf[:]),
                             reads=["tA", "identf"], writes=["pC"])
                        P.op("dve", lambda e, c=c: e.tensor_copy(out=tokf[1][0:64, c * 128:(c + 1) * 128], in_=pC[0:64, 128:256]),
                             reads=["pC"], writes=["tokf1"])
                    P.dma("sp", lambda e: e.dma_start(out=o_convs[:, :], in_=tokf[1][0:64, :]), reads=["tokf1"], writes=["o_convs"])
                else:
                    for g in range(8):
                        P.dma("sp", lambda e, g=g: e.dma_start(out=w4[:], in_=gmlp_w[g, 0:4, 0:4]), writes=["w4"])
                        P.op("pe", lambda e: e.matmul(pC[0:4, 0:128], w4[:], repc[:], start=True, stop=True),
                             reads=["w4", "repc"], writes=["pC"])
                        P.op("dve", lambda e: e.tensor_copy(out=m1[:], in_=pC[0:4, 0:128]), reads=["pC"], writes=["m1"])
                        P.op("pe", lambda e: e.matmul(pC[:, 128:256], repc[:], m1[:], start=True, stop=True),
                             reads=["repc", "m1"], writes=["pC"])
                        P.op("dve", lambda e, g=g: e.tensor_tensor(out=BDg[:, g, :], in0=pC[:, 128:256], in1=bdm[:], op=ALU.mult),
                             reads=["pC", "bdm"], writes=["wmT"])
                        P.dma("sp", lambda e, g=g: e.dma_start(
                            out=bBs[(g % 2) * 64:(g % 2) * 64 + 64, g // 2, :, :],
                            in_=gmlp_b[g, 0:4].partition_broadcast(64).unsqueeze(1).to_broadcast([64, 32, 4])), writes=["bB"])
                    wt, wtn = load_wtok(w_in, 512)
                    psrc, pn = proj_tok(wt, wtn, 0)
                    P.op("dve", lambda e, psrc=psrc: e.tensor_copy(out=cvs[:], in_=psrc), reads=[pn], writes=["Lsumb"])
                    tok_store(psrc, pn, o_gvs[:, :])
                    for p in range(4):
                        for j in range(2):
                            P.op("pe", lambda e, p=p, j=j: e.matmul(pC[:, j * 128:(j + 1) * 128], cvs[:, p * 128:(p + 1) * 128], BDg[:, 2 * p + j, :],
                                                                    start=True, stop=True), reads=["Lsumb", "wmT"], writes=["pC"])
                        for j in range(2):
                            P.op("dve", lambda e, p=p, j=j: e.tensor_tensor(
                                out=tA[j * 64:(j + 1) * 64, p * 128:(p + 1) * 128], in0=pC[j * 64:(j + 1) * 64, j * 128:(j + 1) * 128],
                                in1=bBs[j * 64:(j + 1) * 64, p, :, :].rearrange("q s t -> q (s t)"), op=ALU.add),
                                reads=["pC", "bB"], writes=["tA"])
                    for c in range(4):
                        psrc, pn = proj_feat(w_in, 1024 + c * 128, ntok=128)
                        P.op("act", lambda e, psrc=psrc: e.activation(out=tC[:, 0:128], in_=psrc, func=AF.Silu), reads=[pn], writes=["tC"])
                        P.op("dve", lambda e, c=c: e.tensor_tensor(out=tC[:, 0:128], in0=tC[:, 0:128], in1=tA[:, c * 128:(c + 1) * 128], op=ALU.mult),
                             reads=["tC", "tA"], writes=["tC"])
                        psrc, pn = proj_feat(w_in, 0 + c * 128, ntok=128)
                        P.op("dve", lambda e, c=c, psrc=psrc: e.tensor_tensor(out=yT[:, c, 0:128], in0=psrc, in1=tC[:, 0:128], op=ALU.mult),
                             reads=[pn, "tC"], writes=["yT"])
                P.op("dve", lambda e: e.memset(ytmp[:], 0.0), writes=["junk"])
                for r4 in range(4):
                    P.dma("sp", lambda e, r4=r4: e.dma_start(out=ybd[r4 * 32:(r4 + 1) * 32, :], in_=ytmp[:]), reads=["junk"], writes=["ybd"])
                P.op("dve", lambda e: e.memset(QF[:], 0.0), writes=["QF"])
                bank = 0
                for s_ in range(nsq):
                    for p in range(4):
                        for j in range(2):
                            h = 2 * p + j
                            P.op("dve", lambda e, p=p, j=j, h=h, s_=s_: e.tensor_copy(
                                out=QF[j * 64:(j + 1) * 64, p, :, h], in_=qT32[j * 64:(j + 1) * 64, p, s_ * 4:(s_ + 1) * 4]),
                                reads=["qT32"], writes=["QF"])
                    P.op("dve", lambda e: e.tensor_copy(out=QB[:], in_=QF[:]), reads=["QF"], writes=["QB"])
                    for p in range(65):
                        if p < 64:
                            kp, kn = Kpg[p % 2], "sK%d" % (p % 2)
                            col = s_ * 64 + p
                            P.dma("pool", lambda e, kp=kp, col=col: e.indirect_dma_start(
                                out=kp, out_offset=None, in_=ckd, in_offset=bass.IndirectOffsetOnAxis(ap=sIdx[:, col:col + 1], axis=0)),
                                reads=["sIdx"], writes=[kn])
                            for c in range(4):
                                P.op("pe", lambda e, c=c, kp=kp: e.transpose(pT[:, c * 128:(c + 1) * 128], kp[:, c * 128:(c + 1) * 128], ident[:]),
                                     reads=[kn, "ident"], writes=["pT"])
                            ktp, ktn = KTp[p % 2], "sKT%d" % (p % 2)
                            if p % 2 == 0:
                                P.op("act", lambda e, ktp=ktp: e.activation(out=ktp, in_=pT[:, 0:512], func=AF.Copy), reads=["pT"], writes=[ktn])
                            else:
                                P.op("dve", lambda e, ktp=ktp: e.tensor_copy(out=ktp, in_=pT[:, 0:512]), reads=["pT"], writes=[ktn])
                            if layer == 0:
                                n = p // 2
                                P.op("pe", lambda e, n=n, kp=kp, p=p: e.matmul(pACC[1][0:32, :], ohc[:, n * 32:(n + 1) * 32], kp,
                                                                               start=(p == 0), stop=(p == 63)),
                                     reads=["ohc", kn], writes=["pACC1"])
                        pcol = (p % 16) * 32
                        pSn = "pS%d" % bank
                        for c in range(4):
                            if p < 64:
                                P.op("pe", lambda e, c=c, ktp=ktp, bank=bank, pcol=pcol: e.matmul(
                                    pS[bank][:, pcol:pcol + 32], ktp[:, c * 128:(c + 1) * 128], QB[:, c, :, :].rearrange("p a b -> p (a b)"),
                                    start=(c == 0), stop=(c == 3)), reads=[ktn, "QB"], writes=[pSn])
                            else:
                                P.op("pe", lambda e, c=c, bank=bank, pcol=pcol: e.matmul(
                                    pS[bank][:, pcol:pcol + 32], qT[:, c, 0:128], QB[:, c, :, :].rearrange("p a b -> p (a b)"),
                                    start=(c == 0), stop=(c == 3)), reads=["qT", "QB"], writes=[pSn])
                        if p % 16 == 15 or p == 64:
                            c0 = (p // 16) * 512
                            wdt = pcol + 32
                            P.op("act", lambda e, bank=bank, c0=c0, wdt=wdt: e.activation(out=sE[:, c0:c0 + wdt], in_=pS[bank][:, 0:wdt],
                                                                                         func=AF.Exp, scale=SCALE),
                                 reads=[pSn], writes=["sE"])
                            bank ^= 1
                    P.op("dve", lambda e, s_=s_: e.scalar_tensor_tensor(
                        out=sE[:, 2048:2080], in0=sE[:, 2048:2080], scalar=Acon[:, s_:s_ + 1], in1=(BMc if layer == 0 else BSc)[:],
                        op0=ALU.mult, op1=ALU.mult), reads=["sE", "Acon", "BMc", "BSc"], writes=["sE"])
                    if layer == 0:
                        P.op("act", lambda e: e.activation(out=kssb[:], in_=pACC[1][0:32, :], func=AF.Copy), reads=["pACC1"], writes=["junk"])
                        for c in range(4):
                            P.op("pe", lambda e, c=c: e.transpose(pC[:, c * 32:(c + 1) * 32], kssb[:, c * 128:(c + 1) * 128], identf[0:32, 0:32]),
                                 reads=["junk", "identf"], writes=["pC"])
                        P.op("dve", lambda e: e.tensor_copy(out=ksT[:], in_=pC[:, 0:128].rearrange("p (c n) -> p c n", c=4)),
                             reads=["pC"], writes=["ksT"])
                        for c in range(4):
                            P.op("pe", lambda e, c=c: e.matmul(pC[0:32, 128:160], QF[:, c, :, :].rearrange("p a b -> p (a b)"), ksT[:, c, :],
                                                               start=(c == 0), stop=(c == 3)), reads=["QF", "ksT"], writes=["pC"])
                        P.op("dve", lambda e: e.tensor_copy(out=gs[:], in_=pC[0:32, 128:160]), reads=["pC"], writes=["gs"])
                        P.op("dve", lambda e: e.max(out=m8s[:], in_=gs[:]), reads=["gs"], writes=["m8s"])
                        P.op("dve", lambda e: e.tensor_scalar(out=sels[:], in0=gs[:], scalar1=m8s[:, 2:3], scalar2=None, op0=ALU.is_ge),
                             reads=["gs", "m8s"], writes=["sels"])
                        P.op("pe", lambda e: e.transpose(pC[0:32, 160:192], sels[:], identf[0:32, 0:32]), reads=["sels", "identf"], writes=["pC"])
                        P.op("dve", lambda e: e.tensor_copy(out=selTs[:], in_=pC[0:32, 160:192]), reads=["pC"], writes=["selTs"])
                        P.op("dve", lambda e: e.tensor_tensor(out=D3[:], in0=identf[0:32, 0:32].unsqueeze(2).to_broadcast([32, 32, 32]),
                                                              in1=selTs[:].unsqueeze(1).to_broadcast([32, 32, 32]), op=ALU.mult),
                             reads=["identf", "selTs"], writes=["Lsum"])
                        for hf in range(2):
                            P.op("pe", lambda e, hf=hf: e.matmul(pS[bank][:], onesm[0:32, :], D3[:, hf * 16:(hf + 1) * 16, :].rearrange("n a b -> n (a b)"),
                                                                 start=True, stop=True), reads=["onesm", "Lsum"], writes=["pS%d" % bank])
                            for j in range(2):
                                Ev = sE[:, hf * 1024:(hf + 1) * 1024].rearrange("k (n j c) -> k n j c", j=2, c=32)
                                Pv = sPT[:, hf * 1024:(hf + 1) * 1024].rearrange("k (n j c) -> k n j c", j=2, c=32)
                                P.op("dve", lambda e, j=j, Ev=Ev, Pv=Pv, bank=bank: e.tensor_tensor(
                                    out=Pv[:, :, j, :], in0=Ev[:, :, j, :], in1=pS[bank][:].rearrange("k (n c) -> k n c", c=32), op=ALU.mult),
                                    reads=["sE", "pS%d" % bank], writes=["sPT"])
                            bank ^= 1
                        P.op("dve", lambda e: e.tensor_copy(out=sPT[:, 2048:2080], in_=sE[:, 2048:2080]), reads=["sE"], writes=["sPT"])
                        P.op("dve", lambda e: e.tensor_reduce(out=PTsum[:], in_=sPT[:, 0:NCOL].rearrange("k (p c) -> k c p", c=32),
                                                              axis=AX.X, op=ALU.add), reads=["sPT"], writes=["PTsum"])
                        P.op("pe", lambda e: e.matmul(pC[0:32, 200:201], PTsum[:], onecol[:], start=True, stop=True),
                             reads=["PTsum", "onecol"], writes=["pC"])
                        P.op("dve", lambda e: e.reciprocal(out=rdn[:], in_=pC[0:32, 200:201]), reads=["pC"], writes=["rdn"])
                    else:
                        P.op("act", lambda e: e.activation(out=sL[:, 0:NCOL], in_=sE[:, 0:NCOL], func=AF.Ln, bias=1.0), reads=["sE"], writes=["sL"])
                        for k in range(5):
                            c0 = k * 512
                            wdt = min(512, NCOL - c0)
                            P.op("pe", lambda e, c0=c0, wdt=wdt: e.matmul(pC[:, 0:wdt], trif[:], sL[:, c0:c0 + wdt], start=True, stop=True),
                                 reads=["trif", "sL"], writes=["pC"])
                            P.op("dve", lambda e, c0=c0, wdt=wdt: e.tensor_copy(out=sCw[:, c0:c0 + wdt], in_=pC[:, 0:wdt]), reads=["pC"], writes=["sCw"])
                            P.op("pe", lambda e, c0=c0, wdt=wdt: e.matmul(pC[:, 0:wdt], onesf[:], sL[:, c0:c0 + wdt], start=True, stop=True),
                                 reads=["onesf", "sL"], writes=["pC"])
                            P.op("dve", lambda e, c0=c0, wdt=wdt: e.tensor_copy(out=sTA[:, c0:c0 + wdt], in_=pC[:, 0:wdt]), reads=["pC"], writes=["sTA"])
                        P.op("dve", lambda e: e.tensor_copy(out=sL[:, 0:2048], in_=sTA[:, 32:2080]), reads=["sTA"], writes=["sL"])
                        P.op("dve", lambda e: e.memset(sL[:, 2048:2080], 0.0), reads=["sTA"], writes=["sL"])
                        bufs = [(sL, "sL"), (sTA, "sTA")]
                        cur = 0
                        st_ = 1
                        while st_ < 65:
                            a, an = bufs[cur]
                            b, bn = bufs[1 - cur]
                            nh = (65 - st_) * 32
                            P.op("dve", lambda e, a=a, b=b, nh=nh, st_=st_: e.tensor_tensor(out=b[:, 0:nh], in0=a[:, 0:nh], in1=a[:, st_ * 32:NCOL], op=ALU.add),
                                 reads=[an], writes=[bn])
                            P.op("dve", lambda e, a=a, b=b, nh=nh: e.tensor_copy(out=b[:, nh:NCOL], in_=a[:, nh:NCOL]), reads=[an], writes=[bn])
                            cur = 1 - cur
                            st_ *= 2
                        r_, rn = bufs[cur]
                        o_, on = bufs[1 - cur]
                        P.op("dve", lambda e, r_=r_: e.tensor_tensor(out=sCw[:, 0:NCOL], in0=sCw[:, 0:NCOL], in1=r_[:, 0:NCOL], op=ALU.add),
                             reads=["sCw", rn], writes=["sCw"])
                        P.op("act", lambda e, o_=o_: e.activation(out=o_[:, 0:NCOL], in_=sCw[:, 0:NCOL], func=AF.Exp, scale=-1.0),
                             reads=["sCw"], writes=[on])
                        P.op("dve", lambda e, o_=o_: e.tensor_tensor(out=sPT[:, 0:NCOL], in0=sE[:, 0:NCOL], in1=o_[:, 0:NCOL], op=ALU.mult),
                             reads=["sE", on], writes=["sPT"])
                    for p in range(64):
                        vp, vn = Vpg[p % 2], "sV%d" % (p % 2)
                        col = s_ * 64 + p
                        P.dma("pool", lambda e, vp=vp, col=col: e.indirect_dma_start(
                            out=vp, out_offset=None, in_=cvd, in_offset=bass.IndirectOffsetOnAxis(ap=sIdx[:, col:col + 1], axis=0)),
                            reads=["sIdx"], writes=[vn])
                        P.op("pe", lambda e, vp=vp, p=p: e.matmul(pACC[0][0:32, :], sPT[:, p * 32:(p + 1) * 32], vp, start=(p == 0), stop=False),
                             reads=["sPT", vn], writes=["pACC0"])
                    P.op("pe", lambda e: e.matmul(pACC[0][0:32, :], sPT[:, 2048:2080], cvb[:], start=False, stop=True),
                         reads=["sPT", "cvb"], writes=["pACC0"])
                    P.op("dve", lambda e: e.tensor_tensor(out=ytmp[:].rearrange("r (h d) -> r h d", h=8),
                                                          in0=pACC[0][0:32, :].rearrange("r (h d) -> r h d", h=8),
                                                          in1=HMc[:].unsqueeze(2).to_broadcast([32, 8, 64]), op=ALU.mult),
                         reads=["pACC0", "HMc"], writes=["junk"])
                    P.op("dve", lambda e: e.tensor_reduce(out=ysb[:], in_=ytmp[:].rearrange("r (h d) -> r d h", h=8), axis=AX.X, op=ALU.add),
                         reads=["junk"], writes=["ysb"])
                    if layer == 0:
                        P.op("dve", lambda e: e.tensor_scalar(out=ysb[:], in0=ysb[:], scalar1=rdn[:, 0:1], scalar2=None, op0=ALU.mult),
                             reads=["ysb", "rdn"], writes=["ysb"])
                    P.dma("sp", lambda e, s_=s_: e.dma_start(out=ybd[s_ * 4:(s_ + 1) * 4, :].rearrange("q (h d) -> (q h) d", d=64), in_=ysb[:]),
                          reads=["ysb"], writes=["ybd"])
                P.dma("sp", lambda e: e.dma_start(out=tokf[0][:], in_=ybd[:, :]), reads=["ybd"], writes=["tokf0"])
                for c in range(4):
                    P.op("pe", lambda e, c=c: e.transpose(pC[:, c * 128:(c + 1) * 128], tokf[0][:, c * 128:(c + 1) * 128], identf[:]),
                         reads=["tokf0", "identf"], writes=["pC"])
                P.op("dve", lambda e: e.tensor_tensor(out=yT[:, 4:8, 0:128], in0=pC[:].rearrange("p (c t) -> p c t", c=4),
                                                      in1=zs[:, :, 0:128], op=ALU.mult), reads=["pC", "zs"], writes=["yT"])
                out_proj_tile(0, 0, src[0:128, :], dst[0:128, :], rname=("xs1d" if layer == 1 else None),
                              wname=("xs1d" if layer == 0 else "dram_res"))
                do_barrier()

        mixT = qT32

        P.stop_at = stop_at
        try:
            layer0_setup()
            if do_prompt:
                for sq in range(npr):
                    layer0(sq)
            if do_sample:
                sample_phase(0)
            layer1_setup()
            if do_prompt:
                for sq in range(npr):
                    layer1(sq)
            if do_sample:
                sample_phase(1)
        except StopBuild:
            pass
        try:
            print("sbuf bytes remaining", nc.sbuf_bytes_remaining, "n_ops", len(P.ops))
        except Exception:
            pass
        P.emit()
    return nc


NCORES = 1


def kernel(**inputs):
    f32 = np.float32
    n = NCORES
    npr, nsq = 4 // n, 32 // n
    consts = make_consts()
    shared = {k: np.ascontiguousarray(inputs[k][0], dtype=f32) for k in
              ("norm_pre_e", "norm_post_e", "norm_pre_o", "norm_post_o", "w_in_e", "w_out_e",
               "w_in_o", "w_out_o", "conv_w", "gmlp_w", "gmlp_b")}
    n_phys = inputs["cache_k_moba"].shape[1]
    pools = {"ck_m": "cache_k_moba", "cv_m": "cache_v_moba", "ck_s": "cache_k_sb", "cv_s": "cache_v_sb"}
    for k, src in pools.items():
        shared[k] = np.ascontiguousarray(inputs[src][0], dtype=f32).reshape(n_phys * 128, 512)
    in_maps = []
    for c in range(n):
        m = dict(shared)
        m["consts"] = consts
        m["xp"] = np.ascontiguousarray(inputs["x_prompt"][c * npr:(c + 1) * npr], dtype=f32).reshape(npr * SEQ, D)
        xs = np.zeros((128, D), f32)
        xs[:4 * nsq] = np.asarray(inputs["x_sample"][c * nsq:(c + 1) * nsq], dtype=f32).reshape(4 * nsq, D)
        m["xs"] = xs
        stc = np.zeros((64, 512), f32)
        stc[:2 * nsq] = np.asarray(inputs["state_conv"][0, c * nsq:(c + 1) * nsq], dtype=f32).reshape(2 * nsq, 512)
        m["stc"] = stc
        m["ptab"] = np.ascontiguousarray(inputs["page_table"][c * nsq:(c + 1) * nsq], dtype=np.int32).reshape(-1)
        in_maps.append(m)
    nc = build(do_prompt=True, do_sample=True, n_phys=n_phys, npr=npr, nsq=nsq)
    res = run_bass_kernel_spmd(nc, in_maps, core_ids=list(range(n))).results

    def cat(name, rows, shape):
        return np.concatenate([np.asarray(r[name])[:rows] for r in res], axis=0).reshape(shape).astype(f32)

    nb, nd = 4, 32
    y_prompt = cat("y_prompt", npr * SEQ, (nb, SEQ, D))
    y_sample = cat("y_sample", 4 * nsq, (nd, 4, D))
    conv_prompt = cat("conv_prompt", npr * 2, (1, nb, 2, 512))
    conv_sample = cat("conv_sample", 2 * nsq, (1, nd, 2, 512))
    kvp = lambda name: cat(name, npr * SEQ, (1, nb, SEQ, 8, 64))
    kvs = lambda name: cat(name, 4 * nsq, (1, nd, 4, 8, 64))
    return (y_prompt, y_sample, conv_prompt, conv_sample,
            kvp("k_moba_prompt"), kvp("v_moba_prompt"), kvs("k_moba_sample"), kvs("v_moba_sample"),
            kvp("k_sb_prompt"), kvp("v_sb_prompt"), kvs("k_sb_sample"), kvs("v_sb_sample"),
            cat("gmlp_v_prompt", npr * 128, (1, nb, 128, 512)), cat("gmlp_v_sample", 4 * nsq, (1, nd, 4, 512)))
```

```python
import types
import numpy as np
from contextlib import ExitStack
import concourse.bass as bass
import concourse.mybir as mybir
from concourse.bass_utils import run_bass_kernel_spmd

F32 = mybir.dt.float32
BF16 = mybir.dt.bfloat16
I32 = mybir.dt.int32
AF = mybir.ActivationFunctionType
ALU = mybir.AluOpType
AX = mybir.AxisListType

COMPUTE = ("pe", "act", "dve", "pool")
NDMA_SEMS = 6
SEM_EPOCH = 30000
SKIP = {}

D = 1024
SEQ = 4096
NT = SEQ // 128
EPS = 1e-6
SCALE = 0.125
NEGBIG = -30000.0


class Op:
    __slots__ = ("eng", "fn", "reads", "writes", "deps", "is_dma", "queue", "signal",
                 "sem", "val", "prewait", "dk")

    def __init__(self, eng, fn, reads, writes, is_dma=False, queue=None):
        self.eng = eng
        self.fn = fn
        self.reads = reads
        self.writes = writes
        self.deps = []
        self.is_dma = is_dma
        self.queue = queue
        self.signal = False
        self.sem = None
        self.val = 0
        self.prewait = None


def _freeze(fn):
    if fn.__closure__ is None:
        return fn
    cells = []
    for c in fn.__closure__:
        try:
            cells.append(types.CellType(c.cell_contents))
        except ValueError:
            cells.append(c)
    return types.FunctionType(fn.__code__, fn.__globals__, fn.__name__, fn.__defaults__, tuple(cells))


class StopBuild(Exception):
    pass


class Prog:
    stop_at = None

    def __init__(self, nc, same_engine_sync=True):
        self.nc = nc
        self.ops = []
        self.last_writer = {}
        self.readers = {}
        self.dma_writers = {}
        self.dcnt = {"sp": 0, "act": 0, "pool": 0}
        self.same_engine_sync = same_engine_sync

    def _add(self, op):
        deps = set()
        for r in op.reads:
            w = self.last_writer.get(r)
            if w is not None:
                deps.add(w)
            deps.update(self.dma_writers.get(r, {}).values())
        for w_ in op.writes:
            w = self.last_writer.get(w_)
            if w is not None:
                deps.add(w)
            deps.update(self.dma_writers.get(w_, {}).values())
            for rd in self.readers.get(w_, ()):
                deps.add(rd)
        i = len(self.ops)
        if self.stop_at is not None and i >= self.stop_at:
            raise StopBuild()
        op.fn = _freeze(op.fn)
        op.deps = sorted(deps)
        self.ops.append(op)
        for r in op.reads:
            self.readers.setdefault(r, []).append(i)
        for w_ in op.writes:
            self.last_writer[w_] = i
            self.readers[w_] = []
            if op.is_dma:
                self.dma_writers.setdefault(w_, {})[(op.queue, op.dk % NDMA_SEMS)] = i
            else:
                self.dma_writers.pop(w_, None)
        return i

    def barrier(self, fns):
        names = sorted(set(self.last_writer) | set(self.readers))
        for eng, fn in fns:
            if eng in ("sp",):
                self.dma(eng, fn, writes=names)
            else:
                self._add(Op(eng, fn, (), tuple(names)))

    def op(self, eng, fn, reads=(), writes=()):
        writes = tuple(writes) + tuple(r for r in reads if r[0] == "p" and r[1].isupper() and r not in writes)
        return self._add(Op(eng, fn, tuple(reads), tuple(writes)))

    def dma(self, queue, fn, reads=(), writes=()):
        o = Op("dma_" + queue, fn, tuple(reads), tuple(writes), is_dma=True, queue=queue)
        o.dk = self.dcnt[queue]
        self.dcnt[queue] += 1
        return self._add(o)

    def emit(self):
        nc = self.nc
        ops = self.ops
        for o in ops:
            for d in o.deps:
                p = ops[d]
                if p.is_dma:
                    p.signal = True
                    continue
                same = (p.eng == o.eng) and not o.is_dma
                if same and (p.eng == "pe" or not self.same_engine_sync):
                    continue
                p.signal = True
        for o in ops:
            if o.is_dma:
                o.signal = True
        with ExitStack() as st:
            sems = {e: st.enter_context(nc.semaphore("s_" + e)) for e in COMPUTE}
            dsem = {q: [st.enter_context(nc.semaphore("d_%s%d" % (q, k))) for k in range(NDMA_SEMS)]
                    for q in ("sp", "act", "pool")}
            cnt = {e: 0 for e in COMPUTE}
            ep = {e: 0 for e in COMPUTE}
            for o in ops:
                if o.is_dma:
                    k = o.dk
                    o.sem = dsem[o.queue][k % NDMA_SEMS]
                    o.val = 16 * (k // NDMA_SEMS + 1)
                    o.prewait = (o.sem, o.val - 16) if o.val > 16 else None
                elif o.signal:
                    if cnt[o.eng] >= SEM_EPOCH:
                        ep[o.eng] += 1
                        sems[o.eng] = st.enter_context(nc.semaphore("s_%s_%d" % (o.eng, ep[o.eng])))
                        cnt[o.eng] = 0
                    cnt[o.eng] += 1
                    o.sem = sems[o.eng]
                    o.val = cnt[o.eng]
            issue_eng = {"pe": "pe", "act": "act", "dve": "dve", "pool": "pool",
                         "dma_sp": "sp", "dma_act": "act", "dma_pool": "pool"}
            streams = {"pe": [], "act": [], "dve": [], "pool": [], "sp": []}
            for i, o in enumerate(ops):
                streams[issue_eng[o.eng]].append(i)
            block = st.enter_context(nc.Block())

            def run_stream(name, eng):
                waited = {}
                for i in streams[name]:
                    o = ops[i]
                    need = {}
                    for d in o.deps:
                        p = ops[d]
                        if p.sem is None:
                            continue
                        if need.get(p.sem, (0, None))[0] < p.val:
                            need[p.sem] = (p.val, p.sem)
                    if o.prewait is not None:
                        s, v = o.prewait
                        if need.get(s, (0, None))[0] < v:
                            need[s] = (v, s)
                    for key, (v, s) in need.items():
                        if waited.get(key, 0) < v:
                            eng.wait_ge(s, v)
                            waited[key] = v
                    ins = o.fn(eng)
                    if o.signal:
                        ins.then_inc(o.sem, 16 if o.is_dma else 1)
                if name == "sp":
                    last = {}
                    for o in ops:
                        if o.sem is not None and last.get(o.sem, (0,))[0] < o.val:
                            last[o.sem] = (o.val, o.sem)
                    for key, (v, s) in last.items():
                        if waited.get(key, 0) < v:
                            eng.wait_ge(s, v)

            @block.tensor
            def _(e):
                run_stream("pe", e)

            @block.scalar
            def _(e):
                run_stream("act", e)

            @block.vector
            def _(e):
                run_stream("dve", e)

            @block.gpsimd
            def _(e):
                run_stream("pool", e)

            @block.sync
            def _(e):
                run_stream("sp", e)


C_IDENT, C_TRI, C_ONES, C_CM, C_PEN, C_TRIL, C_OH = 0, 128, 256, 384, 896, 2944, 3072
C_PIDX, C_A, C_BM, C_BS, C_HM, C_OHC, C_REP, C_BD = 5120, 5121, 5153, 5185, 5217, 5225, 6249, 6377
C_W = 6505


def make_consts():
    c = np.zeros((128, C_W), np.float32)
    i = np.arange(128)
    c[:, C_IDENT:C_IDENT + 128] = np.eye(128)
    c[:, C_TRI:C_TRI + 128] = (i[:, None] >= i[None, :])
    c[:, C_ONES:C_ONES + 128] = 1.0
    t256 = np.arange(256)
    c[:, C_CM:C_CM + 256] = (i[:, None] <= t256[None, :])
    c[:, C_CM + 256:C_CM + 512] = (128 + i[:, None] <= t256[None, :])
    t512 = np.arange(512)
    for j in range(4):
        c[:, C_PEN + 512 * j:C_PEN + 512 * (j + 1)] = np.where(128 * j + i[:, None] < t512[None, :], 0.0, NEGBIG)
    c[:, C_TRIL:C_TRIL + 128] = (i[None, :] <= i[:, None])
    for n in range(16):
        c[n, C_OH + 128 * n:C_OH + 128 * (n + 1)] = 1.0
    col = np.arange(32)
    c[:, C_PIDX] = i
    c[:, C_A:C_A + 32] = (i[:, None] // 4 == col[None, :])
    c[:, C_BM:C_BM + 32] = (i[:, None] % 4 <= col[None, :] // 8)
    c[:, C_BS:C_BS + 32] = (i[:, None] % 4 < col[None, :] // 8)
    c[0:32, C_HM:C_HM + 8] = (col[:, None] % 8 == np.arange(8)[None, :])
    for n in range(32):
        c[:, C_OHC + 32 * n + n] = 1.0
    c[0:4, C_REP:C_REP + 128] = (np.arange(4)[:, None] == i[None, :] % 4)
    c[:, C_BD:C_BD + 128] = (i[:, None] // 4 == i[None, :] // 4) & (i[:, None] % 4 <= i[None, :] % 4)
    return c


def build(do_prompt=True, do_sample=True, n_phys=2560, same_engine_sync=True, nblk=8, stop_at=None, npr=4, nsq=32):
    nc = bass.Bass("TRN2", target_bir_lowering=False)
    dt_in = {}

    def din(name, shape, dt=F32):
        dt_in[name] = nc.dram_tensor(name, shape, dt, kind="ExternalInput").ap()
        return dt_in[name]

    def dout(name, shape, dt=F32):
        return nc.dram_tensor(name, shape, dt, kind="ExternalOutput").ap()

    consts = din("consts", [128, C_W])
    xp = din("xp", [npr * SEQ, D])
    npre_e = din("norm_pre_e", [D]); npost_e = din("norm_post_e", [D])
    npre_o = din("norm_pre_o", [D]); npost_o = din("norm_post_o", [D])
    w_in_e = din("w_in_e", [D, 4096]); w_out_e = din("w_out_e", [D, D])
    w_in_o = din("w_in_o", [D, 3584]); w_out_o = din("w_out_o", [D, D])
    conv_w = din("conv_w", [3, 512])
    gmlp_w = din("gmlp_w", [8, 128, 128]); gmlp_b = din("gmlp_b", [8, 128])

    o_y = dout("y_prompt", [npr * SEQ, D])
    o_conv = dout("conv_prompt", [npr * 2, 512])
    o_kmb = dout("k_moba_prompt", [npr * SEQ, 512]); o_vmb = dout("v_moba_prompt", [npr * SEQ, 512])
    o_ksb = dout("k_sb_prompt", [npr * SEQ, 512]); o_vsb = dout("v_sb_prompt", [npr * SEQ, 512])
    o_gv = dout("gmlp_v_prompt", [npr * 128, 512])
    x1d = nc.dram_tensor("x1_scratch", [npr * SEQ, D], F32, kind="Internal").ap()
    if do_sample:
        xs = din("xs", [128, D])
        stc = din("stc", [64, 512])
        ptab = din("ptab", [nsq * 64], I32)
        ck_m = din("ck_m", [n_phys * 128, 512]); cv_m = din("cv_m", [n_phys * 128, 512])
        ck_s = din("ck_s", [n_phys * 128, 512]); cv_s = din("cv_s", [n_phys * 128, 512])
        o_ys = dout("y_sample", [128, D])
        o_convs = dout("conv_sample", [64, 512])
        o_kms = dout("k_moba_sample", [128, 512]); o_vms = dout("v_moba_sample", [128, 512])
        o_kss = dout("k_sb_sample", [128, 512]); o_vss = dout("v_sb_sample", [128, 512])
        o_gvs = dout("gmlp_v_sample", [128, 512])
        xs1d = nc.dram_tensor("xs1_scratch", [128, D], F32, kind="Internal").ap()
        ybd = nc.dram_tensor("yb_scratch", [128, 512], F32, kind="Internal").ap()

    with ExitStack() as st:
        def sb(name, shape, dt):
            return st.enter_context(nc.sbuf_tensor(name, shape, dt))

        def ps(name, shape, dt):
            return st.enter_context(nc.psum_tensor(name, shape, dt))

        P = Prog(nc, same_engine_sync=same_engine_sync)

        ident = sb("ident", [128, 128], BF16)
        identf = sb("identf", [128, 128], F32)
        tri = sb("tri", [128, 128], BF16)
        onesm = sb("onesm", [128, 128], BF16)
        cm = sb("cm", [128, 512], F32)
        pen = sb("pen", [128, 2048], BF16)
        tril = sb("tril", [128, 128], F32)
        oneh = sb("oneh", [16, 2048], BF16)
        P.dma("pool", lambda e: e.dma_start(out=ident[:], in_=consts[:, C_IDENT:C_IDENT + 128]), writes=["ident"])
        P.dma("sp", lambda e: e.dma_start(out=identf[:], in_=consts[:, C_IDENT:C_IDENT + 128]), writes=["identf"])
        P.dma("pool", lambda e: e.dma_start(out=tri[:], in_=consts[:, C_TRI:C_TRI + 128]), writes=["tri"])
        P.dma("pool", lambda e: e.dma_start(out=onesm[:], in_=consts[:, C_ONES:C_ONES + 128]), writes=["onesm"])
        P.dma("sp", lambda e: e.dma_start(out=cm[:], in_=consts[:, C_CM:C_CM + 512]), writes=["cm"])
        P.dma("pool", lambda e: e.dma_start(out=pen[:], in_=consts[:, C_PEN:C_PEN + 2048]), writes=["pen"])
        P.dma("sp", lambda e: e.dma_start(out=tril[:], in_=consts[:, C_TRIL:C_TRIL + 128]), writes=["tril"])
        P.dma("pool", lambda e: e.dma_start(out=oneh[:], in_=consts[0:16, C_OH:C_OH + 2048]), writes=["oneh"])

        wout = sb("wout", [128, 8, D], BF16)
        NWB = 4
        wbuf = [sb("wbuf%d" % i, [128, 8, 128], BF16) for i in range(NWB)]
        wtok = [sb("wtok%d" % i, [128, 8, 512], BF16) for i in range(1)]
        arena = sb("arena", [128, 16384], F32)
        KT = arena[:, 0:8192].bitcast(BF16).rearrange("p (c k) -> p c k", c=4)
        V = arena[:, 8192:16384].bitcast(BF16).rearrange("p (t n) -> p t n", t=NT)
        xt = sb("xt", [128, D], F32)
        xn = sb("xn", [128, D], BF16)
        xnT = sb("xnT", [128, 8, 512], BF16)
        gpre = sb("gpre", [128, 8], F32)
        gpost = sb("gpost", [128, D], F32)
        ssq = sb("ssq", [128, 1], F32)
        rstd = sb("rstd", [128, 1], F32)
        junk = sb("junk", [128, D], F32)
        qT = sb("qT", [128, 4, 512], BF16)
        qT32 = sb("qT32", [128, 4, 512], F32)
        yT = sb("yT", [128, 8, 512], BF16)
        zs = sb("zs", [128, 4, 512], F32)
        tA = sb("tA", [128, 512], F32)
        tB = sb("tB", [128, 512], F32)
        tC = sb("tC", [128, 512], F32)
        ubuf = sb("ubuf", [128, 514], F32)
        ucarry = sb("ucarry", [128, 4, 2], F32)
        cwT = sb("cwT", [128, 4, 3], F32)
        tokf = [sb("tokf%d" % i, [128, 512], F32) for i in range(2)]
        E = [sb("E%d" % i, [128, 512], F32) for i in range(2)]
        L = [sb("L%d" % i, [128, 512], BF16) for i in range(2)]
        X = [sb("X%d" % i, [128, 512], F32) for i in range(2)]
        PT = [sb("PT%d" % i, [128, 512], BF16) for i in range(2)]
        Lsum = sb("Lsum", [128, 512], F32)
        Lsumb = sb("Lsumb", [128, 512], BF16)
        kmT = sb("kmT", [128, 4, 16], F32)
        gsb = sb("gsb", [128, 16], F32)
        m8 = sb("m8", [128, 8], F32)
        self_ = sb("sel", [128, 16], F32)
        selT = sb("selT", [16, 256], BF16)
        rden = sb("rden", [128, 256], F32)
        onecol = sb("onecol", [128, 1], F32)
        wmT = sb("wmT", [128, 8, 128], BF16)
        wnat = sb("wnat", [128, 128], F32)
        bB = sb("bB", [128, 4, 128], F32)
        cvb = sb("cvb", [128, 512], BF16)

        pAB = ps("pAB", [128, 1024], F32)
        pT = ps("pT", [128, 1024], BF16)
        pS = [ps("pS%d" % i, [128, 512], F32) for i in range(2)]
        pC = ps("pC", [128, 512], F32)
        pACC = [ps("pACC%d" % i, [128, 512], F32) for i in range(2)]

        epsc = sb("epsc", [128, 1], F32)
        P.op("dve", lambda e: e.memset(epsc[:], EPS), writes=["epsc"])
        P.op("dve", lambda e: e.memset(onecol[:], 1.0), writes=["onecol"])
        P.op("dve", lambda e: e.memset(gsb[:], -1e30), writes=["gsb"])

        ctr = {"w": 0, "wt": 0, "ab": 0, "s": 0, "acc": 0, "tok": 0, "e": 0}

        def load_g(npre, npost):
            P.dma("sp", lambda e: e.dma_start(out=gpre[:], in_=npre.rearrange("(c p) -> p c", p=128),
                                              allow_slow_non_contiguous=True), writes=["gpre"])
            P.dma("sp", lambda e: e.dma_start(out=gpost[:], in_=npost.partition_broadcast(128)), writes=["gpost"])

        def load_wout(w):
            P.dma("pool", lambda e: e.dma_start(out=wout[:], in_=w.rearrange("(c p) n -> p c n", p=128)),
                  writes=["wout"])

        def norm_transpose(src_rows, col0, rname=None):
            P.dma("sp", lambda e: e.dma_start(out=xt[:], in_=src_rows), reads=([rname] if rname else []), writes=["xt"])
            P.op("act", lambda e: e.activation(out=junk[:], in_=xt[:], func=AF.Square, accum_out=ssq[:]),
                 reads=["xt"], writes=["junk", "ssq"])
            P.op("act", lambda e: e.activation(out=rstd[:], in_=ssq[:], func=AF.Ln, scale=1.0 / D, bias=epsc[:, 0:1]),
                 reads=["ssq", "epsc"], writes=["rstd"])
            P.op("act", lambda e: e.activation(out=rstd[:], in_=rstd[:], func=AF.Exp, scale=-0.5),
                 reads=["rstd"], writes=["rstd"])
            P.op("dve", lambda e: e.tensor_scalar(out=xn[:], in0=xt[:], scalar1=rstd[:, 0:1], scalar2=None,
                                                  op0=ALU.mult), reads=["xt", "rstd"], writes=["xn"])
            for c in range(8):
                P.op("pe", lambda e, c=c: e.transpose(pT[:, c * 128:(c + 1) * 128], xn[:, c * 128:(c + 1) * 128],
                                                      ident[:]), reads=["xn", "ident"], writes=["pT"])
            P.op("dve", lambda e: e.tensor_tensor(
                out=xnT[:, :, col0:col0 + 128], in0=pT[:].rearrange("p (c t) -> p c t", c=8),
                in1=gpre[:].unsqueeze(2).to_broadcast([128, 8, 128]), op=ALU.mult),
                reads=["pT", "gpre"], writes=["xnT"])

        def proj_feat(w_in, col0, ntok=512):
            wb = wbuf[ctr["w"] % NWB]
            wn = "wbuf%d" % (ctr["w"] % NWB)
            ctr["w"] += 1
            P.dma("pool", lambda e: e.dma_start(
                out=wb[:], in_=w_in[:, col0:col0 + 128].rearrange("(c p) n -> p c n", p=128)), writes=[wn])
            half = ctr["ab"] % 2
            ctr["ab"] += 1
            pn = "pAB%d" % half
            dst = pAB[:, half * 512:half * 512 + ntok]
            for kc in range(8):
                P.op("pe", lambda e, kc=kc: e.matmul(dst, wb[:, kc, :], xnT[:, kc, 0:ntok],
                                                     start=(kc == 0), stop=(kc == 7)),
                     reads=[wn, "xnT"], writes=[pn])
            return dst, pn

        def load_wtok(w_in, col0):
            i = 0
            P.dma("pool", lambda e: e.dma_start(
                out=wtok[i][:], in_=w_in[:, col0:col0 + 512].rearrange("(c p) n -> p c n", p=128)),
                writes=["wtok%d" % i])
            return wtok[i], "wtok%d" % i

        def proj_tok(wt, wtn, tcol0):
            half = ctr["ab"] % 2
            ctr["ab"] += 1
            pn = "pAB%d" % half
            dst = pAB[:, half * 512:(half + 1) * 512]
            for kc in range(8):
                P.op("pe", lambda e, kc=kc: e.matmul(dst, xnT[:, kc, tcol0:tcol0 + 128], wt[:, kc, :],
                                                     start=(kc == 0), stop=(kc == 7)),
                     reads=[wtn, "xnT"], writes=[pn])
            return dst, pn

        def out_proj_tile(blk, ti, src_rows, dst_rows, rname=None, wname="dram_res"):
            tc0 = ti * 128
            for nh in range(2):
                for c in range(8):
                    P.op("pe", lambda e, c=c, nh=nh: e.matmul(
                        pAB[:, nh * 512:(nh + 1) * 512], yT[:, c, tc0:tc0 + 128], wout[:, c, nh * 512:(nh + 1) * 512],
                        start=(c == 0), stop=(c == 7)), reads=["yT", "wout"], writes=["pAB%d" % nh])
            P.dma("sp", lambda e: e.dma_start(out=xt[:], in_=src_rows), reads=([rname] if rname else []), writes=["xt"])
            P.op("act", lambda e: e.activation(out=junk[:], in_=pAB[:], func=AF.Square, accum_out=ssq[:]),
                 reads=["pAB0", "pAB1"], writes=["junk", "ssq"])
            P.op("act", lambda e: e.activation(out=rstd[:], in_=ssq[:], func=AF.Ln, scale=1.0 / D, bias=epsc[:, 0:1]),
                 reads=["ssq", "epsc"], writes=["rstd"])
            P.op("act", lambda e: e.activation(out=rstd[:], in_=rstd[:], func=AF.Exp, scale=-0.5),
                 reads=["rstd"], writes=["rstd"])
            P.op("dve", lambda e: e.scalar_tensor_tensor(out=junk[:], in0=pAB[:], scalar=rstd[:, 0:1], in1=gpost[:],
                                                         op0=ALU.mult, op1=ALU.mult),
                 reads=["pAB0", "pAB1", "rstd", "gpost"], writes=["junk"])
            P.op("dve", lambda e: e.tensor_tensor(out=xt[:], in0=xt[:], in1=junk[:], op=ALU.add),
                 reads=["xt", "junk"], writes=["xt"])
            P.dma("sp", lambda e: e.dma_start(out=dst_rows, in_=xt[:]), reads=["xt"], writes=[wname])

        def tok_store(psrc, pn, dram_rows, extra=None):
            i = ctr["tok"] % 2
            ctr["tok"] += 1
            tn = "tokf%d" % i
            P.op("act", lambda e: e.activation(out=tokf[i][:], in_=psrc, func=AF.Copy), reads=[pn], writes=[tn])
            P.dma("sp", lambda e: e.dma_start(out=dram_rows, in_=tokf[i][:]), reads=[tn], writes=["dram_kv"])
            return tokf[i], tn

        def layer0_setup():
            load_g(npre_e, npost_e)
            load_wout(w_out_e)
            for j in range(3):
                P.dma("sp", lambda e, j=j: e.dma_start(out=cwT[:, :, j], in_=conv_w[j].rearrange("(c p) -> p c", p=128),
                                                       allow_slow_non_contiguous=True), writes=["cwT"])

        def layer0(sq):
            R0 = sq * SEQ
            P.op("dve", lambda e: e.memset(ucarry[:], 0.0), writes=["ucarry"])
            P.op("dve", lambda e: e.memset(gsb[:], -1e30), writes=["gsb"])
            for blk in range(nblk):
                r0 = blk * 512
                for ti in range(4):
                    norm_transpose(xp[R0 + r0 + ti * 128:R0 + r0 + (ti + 1) * 128, :], ti * 128)
                wt, wtn = load_wtok(w_in_e, 2048 + 512)
                for ti in range(4):
                    g = blk * 4 + ti
                    psrc, pn = proj_tok(wt, wtn, ti * 128)
                    tf, tn = tok_store(psrc, pn, o_kmb[R0 + g * 128:R0 + (g + 1) * 128, :])
                    for p in range(4):
                        P.op("pe", lambda e, p=p, ti=ti, tf=tf: e.matmul(
                            pC[:, 256 + 4 * (ti % 2) + p:256 + 4 * (ti % 2) + p + 1], tf[:, p * 128:(p + 1) * 128], onecol[:],
                            start=True, stop=True), reads=[tn, "onecol"], writes=["pC"])
                    if ti % 2 == 1:
                        P.op("dve", lambda e, g=g: e.tensor_copy(out=kmT[:, :, g // 2], in_=pC[:, 256:260]),
                             reads=["pC"], writes=["kmT"])
                        P.op("dve", lambda e, g=g: e.tensor_tensor(out=kmT[:, :, g // 2], in0=kmT[:, :, g // 2],
                                                                   in1=pC[:, 260:264], op=ALU.add),
                             reads=["pC", "kmT"], writes=["kmT"])
                wt, wtn = load_wtok(w_in_e, 2048 + 1024)
                for ti in range(4):
                    g = blk * 4 + ti
                    psrc, pn = proj_tok(wt, wtn, ti * 128)
                    P.op("dve", lambda e, g=g, psrc=psrc: e.tensor_copy(out=V[:, g, :], in_=psrc),
                         reads=[pn], writes=["V%d" % g])
                    tok_store(psrc, pn, o_vmb[R0 + g * 128:R0 + (g + 1) * 128, :])
                for p in range(4):
                    psrc, pn = proj_feat(w_in_e, 2048 + p * 128)
                    P.op("act", lambda e, p=p, psrc=psrc: e.activation(out=qT[:, p, :], in_=psrc, func=AF.Copy),
                         reads=[pn], writes=["qT"])
                    P.op("dve", lambda e, p=p, psrc=psrc: e.tensor_copy(out=qT32[:, p, :], in_=psrc),
                         reads=[pn], writes=["qT32"])
                for p in range(4):
                    psrc, pn = proj_feat(w_in_e, 2048 + 512 + p * 128)
                    P.op("act", lambda e, p=p, psrc=psrc: e.activation(out=KT[:, p, r0:r0 + 512], in_=psrc, func=AF.Copy),
                         reads=[pn], writes=["KT%d" % blk])
                for p in range(4):
                    psrc, pn = proj_feat(w_in_e, 2048 + 1536 + p * 128)
                    P.op("act", lambda e, p=p, psrc=psrc: e.activation(out=zs[:, p, :], in_=psrc, func=AF.Silu),
                         reads=[pn], writes=["zs"])
                for c in range(4):
                    psrc, pn = proj_feat(w_in_e, 512 + c * 128)
                    P.op("act", lambda e, psrc=psrc: e.activation(out=tA[:], in_=psrc, func=AF.Copy),
                         reads=[pn], writes=["tA"])
                    psrc, pn = proj_feat(w_in_e, 1024 + c * 128)
                    P.op("dve", lambda e, c=c: e.tensor_copy(out=ubuf[:, 0:2], in_=ucarry[:, c, :]),
                         reads=["ucarry"], writes=["ubuf"])
                    P.op("dve", lambda e, psrc=psrc: e.tensor_tensor(out=ubuf[:, 2:514], in0=psrc, in1=tA[:], op=ALU.mult),
                         reads=[pn, "tA"], writes=["ubuf"])
                    P.op("dve", lambda e, c=c: e.tensor_copy(out=ucarry[:, c, :], in_=ubuf[:, 512:514]),
                         reads=["ubuf"], writes=["ucarry"])
                    P.op("dve", lambda e, c=c: e.tensor_scalar(out=tB[:], in0=ubuf[:, 0:512], scalar1=cwT[:, c, 0:1],
                                                               scalar2=None, op0=ALU.mult),
                         reads=["ubuf", "cwT"], writes=["tB"])
                    P.op("dve", lambda e, c=c: e.scalar_tensor_tensor(out=tB[:], in0=ubuf[:, 1:513], scalar=cwT[:, c, 1:2],
                                                                      in1=tB[:], op0=ALU.mult, op1=ALU.add),
                         reads=["ubuf", "cwT", "tB"], writes=["tB"])
                    P.op("dve", lambda e, c=c: e.scalar_tensor_tensor(out=tB[:], in0=ubuf[:, 2:514], scalar=cwT[:, c, 2:3],
                                                                      in1=tB[:], op0=ALU.mult, op1=ALU.add),
                         reads=["ubuf", "cwT", "tB"], writes=["tB"])
                    psrc, pn = proj_feat(w_in_e, 1536 + c * 128)
                    P.op("act", lambda e, psrc=psrc: e.activation(out=tC[:], in_=psrc, func=AF.Silu),
                         reads=[pn], writes=["tC"])
                    P.op("dve", lambda e: e.tensor_tensor(out=tB[:], in0=tB[:], in1=tC[:], op=ALU.mult),
                         reads=["tB", "tC"], writes=["tB"])
                    psrc, pn = proj_feat(w_in_e, 0 + c * 128)
                    P.op("dve", lambda e, c=c, psrc=psrc: e.tensor_tensor(out=yT[:, c, :], in0=psrc, in1=tB[:], op=ALU.mult),
                         reads=[pn, "tB"], writes=["yT"])
                if blk == nblk - 1:
                    for j in range(2):
                        P.dma("sp", lambda e, j=j: e.dma_start(out=o_conv[sq * 2 + j].rearrange("(c p) -> p c", p=128), in_=ucarry[:, :, j],
                                                               allow_slow_non_contiguous=True), reads=["ucarry"], writes=["o_conv"])
                for half in range(2 if not SKIP.get("moba") else 0):
                    g = blk * 2 + half
                    q0 = half * 256
                    for h in range(8):
                        p, hr = h // 2, (h % 2) * 64
                        need_sel = g >= 4
                        if need_sel:
                            for qt in range(2):
                                P.op("pe", lambda e, qt=qt: e.matmul(
                                    pC[:, 0:g], qT32[hr:hr + 64, p, q0 + qt * 128:q0 + (qt + 1) * 128],
                                    kmT[hr:hr + 64, p, 0:g], start=True, stop=True),
                                    reads=["qT32", "kmT"], writes=["pC"])
                                P.op("dve", lambda e: e.tensor_copy(out=gsb[:, 0:g], in_=pC[:, 0:g]),
                                     reads=["pC"], writes=["gsb"])
                                P.op("dve", lambda e: e.max(out=m8[:], in_=gsb[:]), reads=["gsb"], writes=["m8"])
                                P.op("dve", lambda e: e.tensor_scalar(out=self_[:], in0=gsb[:], scalar1=m8[:, 2:3],
                                                                      scalar2=None, op0=ALU.is_ge),
                                     reads=["gsb", "m8"], writes=["sel"])
                                P.op("pe", lambda e: e.transpose(pC[0:16, 128:256], self_[:], identf[:]),
                                     reads=["sel", "identf"], writes=["pC"])
                                P.op("dve", lambda e, qt=qt: e.tensor_copy(out=selT[:, qt * 128:(qt + 1) * 128],
                                                                           in_=pC[0:16, 128:256]),
                                     reads=["pC"], writes=["selT"])
                        acc = pACC[ctr["acc"] % 2]
                        accn = "pACC%d" % (ctr["acc"] % 2)
                        ctr["acc"] += 1
                        nkt = 2 * (g + 1)
                        def mobaA(kt):
                            n = kt // 2
                            si = ctr["s"] % 2
                            ctr["s"] += 1
                            pSn = "pS%d" % si
                            P.op("pe", lambda e, kt=kt, si=si: e.matmul(
                                pS[si][:, 0:256], KT[hr:hr + 64, p, kt * 128:(kt + 1) * 128], qT[hr:hr + 64, p, q0:q0 + 256],
                                start=True, stop=True), reads=["KT%d" % (kt // 4), "qT"], writes=[pSn])
                            ei = ctr["e"] % 2
                            ctr["e"] += 1
                            if n == g:
                                P.op("act", lambda e, si=si, ei=ei: e.activation(out=E[ei][:, 0:256], in_=pS[si][:, 0:256],
                                                                                func=AF.Exp, scale=SCALE),
                                     reads=[pSn], writes=["E%d" % ei])
                                j = kt % 2
                                P.op("dve", lambda e, ei=ei, j=j: e.tensor_tensor(out=PT[ei][:, 0:256], in0=E[ei][:, 0:256],
                                                                                  in1=cm[:, j * 256:(j + 1) * 256], op=ALU.mult),
                                     reads=["E%d" % ei, "cm"], writes=["PT%d" % ei])
                            elif need_sel:
                                if kt % 2 == 0:
                                    P.op("pe", lambda e, n=n: e.matmul(pC[:, 256:512], oneh[:, n * 128:(n + 1) * 128], selT[:],
                                                                       start=True, stop=True),
                                         reads=["oneh", "selT"], writes=["pC"])
                                P.op("act", lambda e, si=si, ei=ei: e.activation(out=E[ei][:, 0:256], in_=pS[si][:, 0:256],
                                                                                func=AF.Exp, scale=SCALE),
                                     reads=[pSn], writes=["E%d" % ei])
                                P.op("dve", lambda e, ei=ei: e.tensor_tensor(out=PT[ei][:, 0:256], in0=E[ei][:, 0:256],
                                                                             in1=pC[:, 256:512], op=ALU.mult),
                                     reads=["E%d" % ei, "pC"], writes=["PT%d" % ei])
                            else:
                                P.op("act", lambda e, si=si, ei=ei: e.activation(out=PT[ei][:, 0:256], in_=pS[si][:, 0:256],
                                                                                func=AF.Exp, scale=SCALE),
                                     reads=[pSn], writes=["PT%d" % ei])
                            return ei

                        def mobaB(kt, ei):
                            P.op("pe", lambda e, kt=kt, ei=ei, acc=acc: e.matmul(
                                acc[:, 0:256], V[:, kt, p * 128:(p + 1) * 128], PT[ei][:, 0:256],
                                start=(kt == 0), stop=(kt == nkt - 1), skip_group_check=True), reads=["V%d" % kt, "PT%d" % ei], writes=[accn])
                            P.op("pe", lambda e, kt=kt, ei=ei, acc=acc: e.matmul(
                                acc[:, 256:512], onesm[:], PT[ei][:, 0:256],
                                start=False, stop=(kt == nkt - 1), skip_group_check=True), reads=["onesm", "PT%d" % ei], writes=[accn])

                        eis = {}
                        for i_ in range(nkt + 1):
                            if i_ < nkt:
                                eis[i_] = mobaA(i_)
                            if i_ >= 1:
                                mobaB(i_ - 1, eis[i_ - 1])
                        P.op("dve", lambda e, acc=acc: e.reciprocal(out=rden[hr:hr + 64, :], in_=acc[hr:hr + 64, 256:512]),
                             reads=[accn], writes=["rden"])
                        P.op("dve", lambda e, acc=acc: e.tensor_tensor(out=rden[hr:hr + 64, :], in0=acc[hr:hr + 64, 0:256],
                                                                       in1=rden[hr:hr + 64, :], op=ALU.mult),
                             reads=[accn, "rden"], writes=["rden"])
                        P.op("dve", lambda e: e.tensor_tensor(out=yT[hr:hr + 64, 4 + p, q0:q0 + 256], in0=rden[hr:hr + 64, :],
                                                              in1=zs[hr:hr + 64, p, q0:q0 + 256], op=ALU.mult),
                             reads=["rden", "zs"], writes=["yT"])
                for ti in range(4):
                    rr = r0 + ti * 128
                    out_proj_tile(blk, ti, xp[R0 + rr:R0 + rr + 128, :], x1d[R0 + rr:R0 + rr + 128, :], wname="x1d")

        def layer1_setup():
            load_g(npre_o, npost_o)
            load_wout(w_out_o)
            for g in range(8):
                P.dma("sp", lambda e, g=g: e.dma_start(out=wnat[:], in_=gmlp_w[g]), writes=["wnat"])
                P.op("dve", lambda e: e.tensor_tensor(out=wnat[:], in0=wnat[:], in1=tril[:], op=ALU.mult),
                     reads=["wnat", "tril"], writes=["wnat"])
                P.op("pe", lambda e: e.transpose(pC[:, 0:128], wnat[:], identf[:]), reads=["wnat", "identf"], writes=["pC"])
                P.op("dve", lambda e, g=g: e.tensor_copy(out=wmT[:, g, :], in_=pC[:, 0:128]), reads=["pC"], writes=["wmT"])
                P.dma("sp", lambda e, g=g: e.dma_start(out=bB[(g % 2) * 64:(g % 2) * 64 + 64, g // 2, :],
                                                       in_=gmlp_b[g].partition_broadcast(64)), writes=["bB"])

        def layer1(sq):
            R0 = sq * SEQ
            for blk in range(nblk):
                r0 = blk * 512
                for ti in range(4):
                    norm_transpose(x1d[R0 + r0 + ti * 128:R0 + r0 + (ti + 1) * 128, :], ti * 128, rname="x1d")
                wt, wtn = load_wtok(w_in_o, 1536 + 512)
                for ti in range(4):
                    g = blk * 4 + ti
                    psrc, pn = proj_tok(wt, wtn, ti * 128)
                    tok_store(psrc, pn, o_ksb[R0 + g * 128:R0 + (g + 1) * 128, :])
                wt, wtn = load_wtok(w_in_o, 1536 + 1024)
                for ti in range(4):
                    g = blk * 4 + ti
                    psrc, pn = proj_tok(wt, wtn, ti * 128)
                    P.op("dve", lambda e, g=g, psrc=psrc: e.tensor_copy(out=V[:, g, :], in_=psrc),
                         reads=[pn], writes=["V%d" % g])
                    tok_store(psrc, pn, o_vsb[R0 + g * 128:R0 + (g + 1) * 128, :])
                for p in range(4):
                    psrc, pn = proj_feat(w_in_o, 1536 + p * 128)
                    P.op("act", lambda e, p=p, psrc=psrc: e.activation(out=qT[:, p, :], in_=psrc, func=AF.Copy),
                         reads=[pn], writes=["qT"])
                for p in range(4):
                    psrc, pn = proj_feat(w_in_o, 1536 + 512 + p * 128)
                    P.op("act", lambda e, p=p, psrc=psrc: e.activation(out=KT[:, p, r0:r0 + 512], in_=psrc, func=AF.Copy),
                         reads=[pn], writes=["KT%d" % blk])
                for p in range(4):
                    psrc, pn = proj_feat(w_in_o, 1536 + 1536 + p * 128)
                    P.op("act", lambda e, p=p, psrc=psrc: e.activation(out=zs[:, p, :], in_=psrc, func=AF.Silu),
                         reads=[pn], writes=["zs"])
                wt, wtn = load_wtok(w_in_o, 512)
                for ti in range(4):
                    g = blk * 4 + ti
                    psrc, pn = proj_tok(wt, wtn, ti * 128)
                    P.op("dve", lambda e, psrc=psrc: e.tensor_copy(out=cvb[:], in_=psrc), reads=[pn], writes=["cvb"])
                    if g == NT - 1:
                        tok_store(psrc, pn, o_gv[sq * 128:(sq + 1) * 128, :])
                    for p in range(4):
                        for j in range(2):
                            gg = 2 * p + j
                            P.op("pe", lambda e, p=p, j=j, gg=gg: e.matmul(
                                pC[:, j * 128:(j + 1) * 128], cvb[:, p * 128:(p + 1) * 128], wmT[:, gg, :],
                                start=True, stop=True), reads=["cvb", "wmT"], writes=["pC"])
                        for j in range(2):
                            P.op("dve", lambda e, p=p, j=j, ti=ti: e.tensor_tensor(
                                out=tA[j * 64:(j + 1) * 64, p * 128:(p + 1) * 128],
                                in0=pC[j * 64:(j + 1) * 64, j * 128:(j + 1) * 128],
                                in1=bB[j * 64:(j + 1) * 64, p, :], op=ALU.add), reads=["pC", "bB"], writes=["tA"])
                    P.op("pool", lambda e, ti=ti: e.tensor_copy(
                        out=mixT[:, :, ti * 128:(ti + 1) * 128], in_=tA[:].rearrange("q (p t) -> q p t", p=4)),
                        reads=["tA"], writes=["qT32"])
                for c in range(4):
                    psrc, pn = proj_feat(w_in_o, 1024 + c * 128)
                    P.op("act", lambda e, psrc=psrc: e.activation(out=tC[:], in_=psrc, func=AF.Silu),
                         reads=[pn], writes=["tC"])
                    P.op("dve", lambda e, c=c: e.tensor_tensor(out=tC[:], in0=tC[:], in1=mixT[:, c, :], op=ALU.mult),
                         reads=["tC", "qT32"], writes=["tC"])
                    psrc, pn = proj_feat(w_in_o, 0 + c * 128)
                    P.op("dve", lambda e, c=c, psrc=psrc: e.tensor_tensor(out=yT[:, c, :], in0=psrc, in1=tC[:], op=ALU.mult),
                         reads=[pn, "tC"], writes=["yT"])
                nkt = 4 * (blk + 1)
                for h in range(8 if not SKIP.get("sb") else 0):
                    p, hr = h // 2, (h % 2) * 64
                    acc = pACC[ctr["acc"] % 2]
                    accn = "pACC%d" % (ctr["acc"] % 2)
                    ctr["acc"] += 1
                    cbufs = [(pC, "pC"), (pACC[ctr["acc"] % 2], "pACC%d" % (ctr["acc"] % 2))]
                    kts = list(range(nkt - 1, -1, -1))
                    stt_ = {}

                    def sbA(idx):
                        kt = kts[idx]
                        si = ctr["s"] % 2
                        ctr["s"] += 1
                        pSn = "pS%d" % si
                        ei = ctr["e"] % 2
                        ctr["e"] += 1
                        stt_[idx] = ei
                        P.op("pe", lambda e: e.matmul(
                            pS[si][:], KT[hr:hr + 64, p, kt * 128:(kt + 1) * 128], qT[hr:hr + 64, p, :],
                            start=True, stop=True), reads=["KT%d" % (kt // 4), "qT"], writes=[pSn])
                        if kt >= nkt - 4:
                            j = kt - (nkt - 4)
                            P.op("dve", lambda e: e.scalar_tensor_tensor(
                                out=X[ei][:], in0=pS[si][:], scalar=SCALE, in1=pen[:, j * 512:(j + 1) * 512],
                                op0=ALU.mult, op1=ALU.add), reads=[pSn, "pen"], writes=["X%d" % ei])
                            P.op("act", lambda e: e.activation(out=E[ei][:], in_=X[ei][:], func=AF.Exp),
                                 reads=["X%d" % ei], writes=["E%d" % ei])
                        else:
                            P.op("act", lambda e: e.activation(out=E[ei][:], in_=pS[si][:], func=AF.Exp, scale=SCALE),
                                 reads=[pSn], writes=["E%d" % ei])
                        P.op("act", lambda e: e.activation(out=L[ei][:], in_=E[ei][:], func=AF.Ln, bias=1.0),
                             reads=["E%d" % ei], writes=["L%d" % ei])

                    def sbB(idx):
                        kt = kts[idx]
                        ei = stt_[idx]
                        cb, cbn = cbufs[idx % 2]
                        P.op("pe", lambda e: e.matmul(cb[:], tri[:], L[ei][:], start=True, stop=(idx == 0)),
                             reads=["tri", "L%d" % ei], writes=[cbn])
                        if idx > 0:
                            P.op("pe", lambda e: e.matmul(cb[:], onesm[:], Lsumb[:], start=False, stop=True),
                                 reads=["onesm", "Lsumb"], writes=[cbn])
                        if idx < nkt - 1:
                            if idx == 0:
                                P.op("dve", lambda e: e.tensor_copy(out=Lsumb[:], in_=L[ei][:]), reads=["L%d" % ei], writes=["Lsumb"])
                                P.op("pool", lambda e: e.tensor_copy(out=Lsum[:], in_=L[ei][:]),
                                     reads=["L%d" % ei], writes=["Lsum"])
                            else:
                                P.op("dve", lambda e: e.tensor_tensor(out=Lsumb[:], in0=Lsum[:], in1=L[ei][:], op=ALU.add),
                                     reads=["L%d" % ei, "Lsum"], writes=["Lsumb"])
                                P.op("pool", lambda e: e.tensor_tensor(out=Lsum[:], in0=Lsum[:], in1=L[ei][:], op=ALU.add),
                                     reads=["L%d" % ei, "Lsum"], writes=["Lsum"])
                        P.op("act", lambda e: e.activation(out=X[ei][:], in_=cb[:], func=AF.Exp, scale=-1.0),
                             reads=[cbn], writes=["X%d" % ei])
                        P.op("dve", lambda e: e.tensor_tensor(out=PT[ei][:], in0=E[ei][:], in1=X[ei][:], op=ALU.mult),
                             reads=["E%d" % ei, "X%d" % ei], writes=["PT%d" % ei])
                        P.op("pe", lambda e: e.matmul(
                            acc[:], V[:, kt, p * 128:(p + 1) * 128], PT[ei][:],
                            start=(idx == 0), stop=(idx == nkt - 1)), reads=["V%d" % kt, "PT%d" % ei], writes=[accn])

                    for i_ in range(nkt + 1):
                        if i_ < nkt:
                            sbA(i_)
                        if i_ >= 1:
                            sbB(i_ - 1)
                    P.op("dve", lambda e, acc=acc: e.tensor_tensor(out=yT[hr:hr + 64, 4 + p, :], in0=acc[hr:hr + 64, :],
                                                                   in1=zs[hr:hr + 64, p, :], op=ALU.mult),
                         reads=[accn, "zs"], writes=["yT"])
                for ti in range(4):
                    rr = r0 + ti * 128
                    out_proj_tile(blk, ti, x1d[R0 + rr:R0 + rr + 128, :], o_y[R0 + rr:R0 + rr + 128, :], rname="x1d")

        if do_sample:
            pidx = sb("pidx", [128, 1], F32)
            Acon = sb("Acon", [128, 32], F32)
            BMc = sb("BMc", [128, 32], F32)
            BSc = sb("BSc", [128, 32], F32)
            HMc = sb("HMc", [32, 8], F32)
            ohc = sb("ohc", [128, 1024], BF16)
            repc = sb("repc", [4, 128], F32)
            bdm = sb("bdm", [128, 128], F32)
            trif = sb("trif", [128, 128], F32)
            onesf = sb("onesf", [128, 128], F32)
            for tl, nm, c0, w, q in ((pidx, "pidx", C_PIDX, 1, "sp"), (Acon, "Acon", C_A, 32, "sp"), (BMc, "BMc", C_BM, 32, "sp"),
                                     (BSc, "BSc", C_BS, 32, "sp"), (ohc, "ohc", C_OHC, 1024, "pool"), (bdm, "bdm", C_BD, 128, "sp"),
                                     (trif, "trif", C_TRI, 128, "sp"), (onesf, "onesf", C_ONES, 128, "sp")):
                P.dma(q, lambda e, tl=tl, c0=c0, w=w: e.dma_start(out=tl[:], in_=consts[:, c0:c0 + w], allow_slow_non_contiguous=True), writes=[nm])
            P.dma("sp", lambda e: e.dma_start(out=HMc[:], in_=consts[0:32, C_HM:C_HM + 8]), writes=["HMc"])
            P.dma("sp", lambda e: e.dma_start(out=repc[:], in_=consts[0:4, C_REP:C_REP + 128]), writes=["repc"])
            ueb = sb("ueb", [128, 4, 32, 6], F32)
            QF = sb("QF", [128, 4, 4, 8], F32)
            QB = sb("QB", [128, 4, 4, 8], BF16)
            kssb = junk[0:32, 512:1024]
            ksT = sb("ksT", [128, 4, 32], F32)
            gs = sb("gs", [32, 32], F32)
            m8s = sb("m8s", [32, 8], F32)
            sels = sb("sels", [32, 32], F32)
            selTs = sb("selTs", [32, 32], F32)
            D3 = Lsum[:].bitcast(BF16)[0:32, :].rearrange("n (a b) -> n a b", a=32)
            PTsum = sb("PTsum", [128, 32], F32)
            rdn = sb("rdn", [32, 1], F32)
            ytmp = junk[0:32, 0:512]
            ysb = sb("ysb", [32, 64], F32)
            cvs = Lsumb
            BDg = wmT
            w4 = sb("w4", [4, 4], F32)
            m1 = sb("m1", [4, 128], F32)
            bBs = bB[:].rearrange("p c (s t) -> p c s t", t=4)
            barc = sb("barc", [128, 1], F32)
            NCOL = 65 * 32
            sE = arena[:, 0:2080]
            sL = arena[:, 2080:4160]
            sCw = arena[:, 4160:6240]
            sTA = arena[:, 6240:8320]
            sPT = arena[:, 8320:9360].bitcast(BF16)
            sIdx = arena[:, 9360:11408].bitcast(I32)
            sPtf = arena[:, 11408:13456]
            sPti = arena[:, 13456:15504].bitcast(I32)
            Kpg = [E[0][:].bitcast(BF16)[:, 0:512], E[0][:].bitcast(BF16)[:, 512:1024],
                   E[1][:].bitcast(BF16)[:, 0:512], E[1][:].bitcast(BF16)[:, 512:1024]]
            Vpg = Kpg
            KTp = [X[0][:].bitcast(BF16)[:, 0:512], X[0][:].bitcast(BF16)[:, 512:1024]]

            def do_barrier():
                P.barrier([
                    ("pe", lambda e: e.matmul(pC[0:1, 0:1], onecol[:], onecol[:], start=True, stop=True)),
                    ("act", lambda e: e.activation(out=barc[:], in_=onecol[:], func=AF.Copy)),
                    ("dve", lambda e: e.memset(barc[:], 0.0)),
                    ("pool", lambda e: e.memset(barc[:], 0.0)),
                    ("sp", lambda e: e.dma_start(out=barc[0:1, 0:1], in_=consts[0:1, 0:1])),
                    ("pe", lambda e: e.matmul(pC[0:1, 0:1], onecol[:], onecol[:], start=True, stop=True)),
                    ("act", lambda e: e.activation(out=barc[:], in_=onecol[:], func=AF.Copy)),
                    ("dve", lambda e: e.memset(barc[:], 0.0)),
                    ("pool", lambda e: e.memset(barc[:], 0.0)),
                ])

            def sample_phase(layer):
                do_barrier()
                w_in = w_in_e if layer == 0 else w_in_o
                src = xs if layer == 0 else xs1d
                dst = xs1d if layer == 0 else o_ys
                okd, ovd = (o_kms, o_vms) if layer == 0 else (o_kss, o_vss)
                ckd, cvd = (ck_m, cv_m) if layer == 0 else (ck_s, cv_s)
                qoff = 2048 if layer == 0 else 1536
                norm_transpose(src[0:128, :], 0, rname=("xs1d" if layer == 1 else None))
                nidx = nsq * 64
                P.dma("sp", lambda e: e.dma_start(out=sPti[:, 0:nidx], in_=ptab.partition_broadcast(128)), writes=["sPti"])
                P.op("dve", lambda e: e.tensor_copy(out=sPtf[:, 0:nidx], in_=sPti[:, 0:nidx]), reads=["sPti"], writes=["sPtf"])
                P.op("dve", lambda e: e.tensor_scalar(out=sPtf[:, 0:nidx], in0=sPtf[:, 0:nidx], scalar1=128.0, scalar2=pidx[:, 0:1],
                                                      op0=ALU.mult, op1=ALU.add), reads=["sPtf", "pidx"], writes=["sPtf"])
                P.op("dve", lambda e: e.tensor_copy(out=sIdx[:, 0:nidx], in_=sPtf[:, 0:nidx]), reads=["sPtf"], writes=["sIdx"])
                wt, wtn = load_wtok(w_in, qoff + 512)
                psrc, pn = proj_tok(wt, wtn, 0)
                tok_store(psrc, pn, okd[:, :])
                wt, wtn = load_wtok(w_in, qoff + 1024)
                psrc, pn = proj_tok(wt, wtn, 0)
                P.op("dve", lambda e, psrc=psrc: e.tensor_copy(out=cvb[:], in_=psrc), reads=[pn], writes=["cvb"])
                tok_store(psrc, pn, ovd[:, :])
                for p in range(4):
                    psrc, pn = proj_feat(w_in, qoff + p * 128, ntok=128)
                    P.op("dve", lambda e, p=p, psrc=psrc: e.tensor_copy(out=qT32[:, p, 0:128], in_=psrc), reads=[pn], writes=["qT32"])
                for p in range(4):
                    psrc, pn = proj_feat(w_in, qoff + 512 + p * 128, ntok=128)
                    P.op("act", lambda e, p=p, psrc=psrc: e.activation(out=qT[:, p, 0:128], in_=psrc, func=AF.Copy), reads=[pn], writes=["qT"])
                for p in range(4):
                    psrc, pn = proj_feat(w_in, qoff + 1536 + p * 128, ntok=128)
                    P.op("act", lambda e, p=p, psrc=psrc: e.activation(out=zs[:, p, 0:128], in_=psrc, func=AF.Silu), reads=[pn], writes=["zs"])
                if layer == 0:
                    P.dma("sp", lambda e: e.dma_start(out=tokf[0][0:64, :], in_=stc), writes=["tokf0"])
                    for c in range(4):
                        P.op("pe", lambda e, c=c: e.transpose(pC[:, 0:64], tokf[0][0:64, c * 128:(c + 1) * 128], identf[0:64, 0:64]),
                             reads=["tokf0", "identf"], writes=["pC"])
                        P.op("dve", lambda e, c=c: e.tensor_copy(out=ueb[:, c, :, 0:2], in_=pC[:, 0:64].rearrange("p (s j) -> p s j", j=2)),
                             reads=["pC"], writes=["ueb"])
                    for c in range(4):
                        psrc, pn = proj_feat(w_in, 512 + c * 128, ntok=128)
                        P.op("act", lambda e, psrc=psrc: e.activation(out=tA[:, 0:128], in_=psrc, func=AF.Copy), reads=[pn], writes=["tA"])
                        psrc, pn = proj_feat(w_in, 1024 + c * 128, ntok=128)
                        P.op("dve", lambda e, c=c, psrc=psrc: e.tensor_tensor(
                            out=ueb[:, c, :, 2:6], in0=psrc.rearrange("p (s t) -> p s t", t=4),
                            in1=tA[:, 0:128].rearrange("p (s t) -> p s t", t=4), op=ALU.mult), reads=[pn, "tA"], writes=["ueb"])
                        tBv = tB[:, 0:128].rearrange("p (s t) -> p s t", t=4)
                        P.op("dve", lambda e, c=c, tBv=tBv: e.tensor_scalar(out=tBv, in0=ueb[:, c, :, 0:4], scalar1=cwT[:, c, 0:1],
                                                                            scalar2=None, op0=ALU.mult), reads=["ueb", "cwT"], writes=["tB"])
                        for j in (1, 2):
                            P.op("dve", lambda e, c=c, j=j, tBv=tBv: e.scalar_tensor_tensor(
                                out=tBv, in0=ueb[:, c, :, j:j + 4], scalar=cwT[:, c, j:j + 1], in1=tBv, op0=ALU.mult, op1=ALU.add),
                                reads=["ueb", "cwT", "tB"], writes=["tB"])
                        psrc, pn = proj_feat(w_in, 1536 + c * 128, ntok=128)
                        P.op("act", lambda e, psrc=psrc: e.activation(out=tC[:, 0:128], in_=psrc, func=AF.Silu), reads=[pn], writes=["tC"])
                        P.op("dve", lambda e: e.tensor_tensor(out=tB[:, 0:128], in0=tB[:, 0:128], in1=tC[:, 0:128], op=ALU.mult),
                             reads=["tB", "tC"], writes=["tB"])
                        psrc, pn = proj_feat(w_in, 0 + c * 128, ntok=128)
                        P.op("dve", lambda e, c=c, psrc=psrc: e.tensor_tensor(out=yT[:, c, 0:128], in0=psrc, in1=tB[:, 0:128], op=ALU.mult),
                             reads=[pn, "tB"], writes=["yT"])
                        P.op("dve", lambda e, c=c: e.tensor_copy(out=tA[:, 0:64].rearrange("p (s j) -> p s j", j=2), in_=ueb[:, c, :, 4:6]),
                             reads=["ueb"], writes=["tA"])
                        P.op("pe", lambda e, c=c: e.transpose(pC[0:64, 128:256], tA[:, 0:64], identf[:]),
                             reads=["tA", "identf"], writes=["pC"])
                        P.op("dve", lambda e, c=c: e.tensor_copy(out=tokf[1][0:64, c * 128:(c + 1) * 128], in_=pC[0:64, 128:256]),
                             reads=["pC"], writes=["tokf1"])
                    P.dma("sp", lambda e: e.dma_start(out=o_convs[:, :], in_=tokf[1][0:64, :]), reads=["tokf1"], writes=["o_convs"])
                else:
                    for g in range(8):
                        P.dma("sp", lambda e, g=g: e.dma_start(out=w4[:], in_=gmlp_w[g, 0:4, 0:4]), writes=["w4"])
                        P.op("pe", lambda e: e.matmul(pC[0:4, 0:128], w4[:], repc[:], start=True, stop=True),
                             reads=["w4", "repc"], writes=["pC"])
                        P.op("dve", lambda e: e.tensor_copy(out=m1[:], in_=pC[0:4, 0:128]), reads=["pC"], writes=["m1"])
                        P.op("pe", lambda e: e.matmul(pC[:, 128:256], repc[:], m1[:], start=True, stop=True),
                             reads=["repc", "m1"], writes=["pC"])
                        P.op("dve", lambda e, g=g: e.tensor_tensor(out=BDg[:, g, :], in0=pC[:, 128:256], in1=bdm[:], op=ALU.mult),
                             reads=["pC", "bdm"], writes=["wmT"])
                        P.dma("sp", lambda e, g=g: e.dma_start(
                            out=bBs[(g % 2) * 64:(g % 2) * 64 + 64, g // 2, :, :],
                            in_=gmlp_b[g, 0:4].partition_broadcast(64).unsqueeze(1).to_broadcast([64, 32, 4])), writes=["bB"])
                    wt, wtn = load_wtok(w_in, 512)
                    psrc, pn = proj_tok(wt, wtn, 0)
                    P.op("dve", lambda e, psrc=psrc: e.tensor_copy(out=cvs[:], in_=psrc), reads=[pn], writes=["Lsumb"])
                    tok_store(psrc, pn, o_gvs[:, :])
                    for p in range(4):
                        for j in range(2):
                            P.op("pe", lambda e, p=p, j=j: e.matmul(pC[:, j * 128:(j + 1) * 128], cvs[:, p * 128:(p + 1) * 128], BDg[:, 2 * p + j, :],
                                                                    start=True, stop=True), reads=["Lsumb", "wmT"], writes=["pC"])
                        for j in range(2):
                            P.op("dve", lambda e, p=p, j=j: e.tensor_tensor(
                                out=tA[j * 64:(j + 1) * 64, p * 128:(p + 1) * 128], in0=pC[j * 64:(j + 1) * 64, j * 128:(j + 1) * 128],
                                in1=bBs[j * 64:(j + 1) * 64, p, :, :].rearrange("q s t -> q (s t)"), op=ALU.add),
                                reads=["pC", "bB"], writes=["tA"])
                    for c in range(4):
                        psrc, pn = proj_feat(w_in, 1024 + c * 128, ntok=128)
                        P.op("act", lambda e, psrc=psrc: e.activation(out=tC[:, 0:128], in_=psrc, func=AF.Silu), reads=[pn], writes=["tC"])
                        P.op("dve", lambda e, c=c: e.tensor_tensor(out=tC[:, 0:128], in0=tC[:, 0:128], in1=tA[:, c * 128:(c + 1) * 128], op=ALU.mult),
                             reads=["tC", "tA"], writes=["tC"])
                        psrc, pn = proj_feat(w_in, 0 + c * 128, ntok=128)
                        P.op("dve", lambda e, c=c, psrc=psrc: e.tensor_tensor(out=yT[:, c, 0:128], in0=psrc, in1=tC[:, 0:128], op=ALU.mult),
                             reads=[pn, "tC"], writes=["yT"])
                P.op("dve", lambda e: e.memset(ytmp[:], 0.0), writes=["junk"])
                for r4 in range(4):
                    P.dma("sp", lambda e, r4=r4: e.dma_start(out=ybd[r4 * 32:(r4 + 1) * 32, :], in_=ytmp[:]), reads=["junk"], writes=["ybd"])
                P.op("dve", lambda e: e.memset(QF[:], 0.0), writes=["QF"])
                bank = 0
                for s_ in range(nsq):
                    for p in range(4):
                        for j in range(2):
                            h = 2 * p + j
                            P.op("dve", lambda e, p=p, j=j, h=h, s_=s_: e.tensor_copy(
                                out=QF[j * 64:(j + 1) * 64, p, :, h], in_=qT32[j * 64:(j + 1) * 64, p, s_ * 4:(s_ + 1) * 4]),
                                reads=["qT32"], writes=["QF"])
                    P.op("dve", lambda e: e.tensor_copy(out=QB[:], in_=QF[:]), reads=["QF"], writes=["QB"])
                    for p in range(65):
                        if p < 64:
                            kp, kn = Kpg[p % 4], "sPG%d" % (p % 4)
                            col = s_ * 64 + p
                            P.dma("pool", lambda e, kp=kp, col=col: e.indirect_dma_start(
                                out=kp, out_offset=None, in_=ckd, in_offset=bass.IndirectOffsetOnAxis(ap=sIdx[:, col:col + 1], axis=0)),
                                reads=["sIdx"], writes=[kn])
                            for c in range(4):
                                P.op("pe", lambda e, c=c, kp=kp: e.transpose(pT[:, c * 128:(c + 1) * 128], kp[:, c * 128:(c + 1) * 128], ident[:]),
                                     reads=[kn, "ident"], writes=["pT"])
                            ktp, ktn = KTp[p % 2], "sKT%d" % (p % 2)
                            if p % 2 == 0:
                                P.op("act", lambda e, ktp=ktp: e.activation(out=ktp, in_=pT[:, 0:512], func=AF.Copy), reads=["pT"], writes=[ktn])
                            else:
                                P.op("dve", lambda e, ktp=ktp: e.tensor_copy(out=ktp, in_=pT[:, 0:512]), reads=["pT"], writes=[ktn])
                            if layer == 0:
                                n = p // 2
                                P.op("pe", lambda e, n=n, kp=kp, p=p: e.matmul(pACC[1][0:32, :], ohc[:, n * 32:(n + 1) * 32], kp,
                                                                               start=(p == 0), stop=(p == 63)),
                                     reads=["ohc", kn], writes=["pACC1"])
                        pcol = (p % 16) * 32
                        pSn = "pS%d" % bank
                        for c in range(4):
                            if p < 64:
                                P.op("pe", lambda e, c=c, ktp=ktp, bank=bank, pcol=pcol: e.matmul(
                                    pS[bank][:, pcol:pcol + 32], ktp[:, c * 128:(c + 1) * 128], QB[:, c, :, :].rearrange("p a b -> p (a b)"),
                                    start=(c == 0), stop=(c == 3)), reads=[ktn, "QB"], writes=[pSn])
                            else:
                                P.op("pe", lambda e, c=c, bank=bank, pcol=pcol: e.matmul(
                                    pS[bank][:, pcol:pcol + 32], qT[:, c, 0:128], QB[:, c, :, :].rearrange("p a b -> p (a b)"),
                                    start=(c == 0), stop=(c == 3)), reads=["qT", "QB"], writes=[pSn])
                        if p % 16 == 15 or p == 64:
                            c0 = (p // 16) * 512
                            wdt = pcol + 32
                            P.op("act", lambda e, bank=bank, c0=c0, wdt=wdt: e.activation(out=sE[:, c0:c0 + wdt], in_=pS[bank][:, 0:wdt],
                                                                                         func=AF.Exp, scale=SCALE),
                                 reads=[pSn], writes=["sE"])
                            bank ^= 1
                    P.op("dve", lambda e, s_=s_: e.scalar_tensor_tensor(
                        out=sE[:, 2048:2080], in0=sE[:, 2048:2080], scalar=Acon[:, s_:s_ + 1], in1=(BMc if layer == 0 else BSc)[:],
                        op0=ALU.mult, op1=ALU.mult), reads=["sE", "Acon", "BMc", "BSc"], writes=["sE"])
                    if layer == 0:
                        P.op("act", lambda e: e.activation(out=kssb[:], in_=pACC[1][0:32, :], func=AF.Copy), reads=["pACC1"], writes=["junk"])
                        for c in range(4):
                            P.op("pe", lambda e, c=c: e.transpose(pC[:, c * 32:(c + 1) * 32], kssb[:, c * 128:(c + 1) * 128], identf[0:32, 0:32]),
                                 reads=["junk", "identf"], writes=["pC"])
                        P.op("dve", lambda e: e.tensor_copy(out=ksT[:], in_=pC[:, 0:128].rearrange("p (c n) -> p c n", c=4)),
                             reads=["pC"], writes=["ksT"])
                        for c in range(4):
                            P.op("pe", lambda e, c=c: e.matmul(pC[0:32, 128:160], QF[:, c, :, :].rearrange("p a b -> p (a b)"), ksT[:, c, :],
                                                               start=(c == 0), stop=(c == 3)), reads=["QF", "ksT"], writes=["pC"])
                        P.op("dve", lambda e: e.tensor_copy(out=gs[:], in_=pC[0:32, 128:160]), reads=["pC"], writes=["gs"])
                        P.op("dve", lambda e: e.max(out=m8s[:], in_=gs[:]), reads=["gs"], writes=["m8s"])
                        P.op("dve", lambda e: e.tensor_scalar(out=sels[:], in0=gs[:], scalar1=m8s[:, 2:3], scalar2=None, op0=ALU.is_ge),
                             reads=["gs", "m8s"], writes=["sels"])
                        P.op("pe", lambda e: e.transpose(pC[0:32, 160:192], sels[:], identf[0:32, 0:32]), reads=["sels", "identf"], writes=["pC"])
                        P.op("dve", lambda e: e.tensor_copy(out=selTs[:], in_=pC[0:32, 160:192]), reads=["pC"], writes=["selTs"])
                        P.op("dve", lambda e: e.tensor_tensor(out=D3[:], in0=identf[0:32, 0:32].unsqueeze(2).to_broadcast([32, 32, 32]),
                                                              in1=selTs[:].unsqueeze(1).to_broadcast([32, 32, 32]), op=ALU.mult),
                             reads=["identf", "selTs"], writes=["Lsum"])
                        for hf in range(2):
                            P.op("pe", lambda e, hf=hf: e.matmul(pS[bank][:], onesm[0:32, :], D3[:, hf * 16:(hf + 1) * 16, :].rearrange("n a b -> n (a b)"),
                                                                 start=True, stop=True), reads=["onesm", "Lsum"], writes=["pS%d" % bank])
                            for j in range(2):
                                Ev = sE[:, hf * 1024:(hf + 1) * 1024].rearrange("k (n j c) -> k n j c", j=2, c=32)
                                Pv = sPT[:, hf * 1024:(hf + 1) * 1024].rearrange("k (n j c) -> k n j c", j=2, c=32)
                                P.op("dve", lambda e, j=j, Ev=Ev, Pv=Pv, bank=bank: e.tensor_tensor(
                                    out=Pv[:, :, j, :], in0=Ev[:, :, j, :], in1=pS[bank][:].rearrange("k (n c) -> k n c", c=32), op=ALU.mult),
                                    reads=["sE", "pS%d" % bank], writes=["sPT"])
                            bank ^= 1
                        P.op("dve", lambda e: e.tensor_copy(out=sPT[:, 2048:2080], in_=sE[:, 2048:2080]), reads=["sE"], writes=["sPT"])
                        P.op("dve", lambda e: e.tensor_reduce(out=PTsum[:], in_=sPT[:, 0:NCOL].rearrange("k (p c) -> k c p", c=32),
                                                              axis=AX.X, op=ALU.add), reads=["sPT"], writes=["PTsum"])
                        P.op("pe", lambda e: e.matmul(pC[0:32, 200:201], PTsum[:], onecol[:], start=True, stop=True),
                             reads=["PTsum", "onecol"], writes=["pC"])
                        P.op("dve", lambda e: e.reciprocal(out=rdn[:], in_=pC[0:32, 200:201]), reads=["pC"], writes=["rdn"])
                    else:
                        P.op("act", lambda e: e.activation(out=sL[:, 0:NCOL], in_=sE[:, 0:NCOL], func=AF.Ln, bias=1.0), reads=["sE"], writes=["sL"])
                        for k in range(5):
                            c0 = k * 512
                            wdt = min(512, NCOL - c0)
                            P.op("pe", lambda e, c0=c0, wdt=wdt: e.matmul(pC[:, 0:wdt], trif[:], sL[:, c0:c0 + wdt], start=True, stop=True),
                                 reads=["trif", "sL"], writes=["pC"])
                            P.op("dve", lambda e, c0=c0, wdt=wdt: e.tensor_copy(out=sCw[:, c0:c0 + wdt], in_=pC[:, 0:wdt]), reads=["pC"], writes=["sCw"])
                            P.op("pe", lambda e, c0=c0, wdt=wdt: e.matmul(pC[:, 0:wdt], onesf[:], sL[:, c0:c0 + wdt], start=True, stop=True),
                                 reads=["onesf", "sL"], writes=["pC"])
                            P.op("dve", lambda e, c0=c0, wdt=wdt: e.tensor_copy(out=sTA[:, c0:c0 + wdt], in_=pC[:, 0:wdt]), reads=["pC"], writes=["sTA"])
                        P.op("dve", lambda e: e.tensor_copy(out=sL[:, 0:2048], in_=sTA[:, 32:2080]), reads=["sTA"], writes=["sL"])
                        P.op("dve", lambda e: e.memset(sL[:, 2048:2080], 0.0), reads=["sTA"], writes=["sL"])
                        bufs = [(sL, "sL"), (sTA, "sTA")]
                        cur = 0
                        st_ = 1
                        while st_ < 65:
                            a, an = bufs[cur]
                            b, bn = bufs[1 - cur]
                            nh = (65 - st_) * 32
                            P.op("dve", lambda e, a=a, b=b, nh=nh, st_=st_: e.tensor_tensor(out=b[:, 0:nh], in0=a[:, 0:nh], in1=a[:, st_ * 32:NCOL], op=ALU.add),
                                 reads=[an], writes=[bn])
                            P.op("dve", lambda e, a=a, b=b, nh=nh: e.tensor_copy(out=b[:, nh:NCOL], in_=a[:, nh:NCOL]), reads=[an], writes=[bn])
                            cur = 1 - cur
                            st_ *= 2
                        r_, rn = bufs[cur]
                        o_, on = bufs[1 - cur]
                        P.op("dve", lambda e, r_=r_: e.tensor_tensor(out=sCw[:, 0:NCOL], in0=sCw[:, 0:NCOL], in1=r_[:, 0:NCOL], op=ALU.add),
                             reads=["sCw", rn], writes=["sCw"])
                        P.op("act", lambda e, o_=o_: e.activation(out=o_[:, 0:NCOL], in_=sCw[:, 0:NCOL], func=AF.Exp, scale=-1.0),
                             reads=["sCw"], writes=[on])
                        P.op("dve", lambda e, o_=o_: e.tensor_tensor(out=sPT[:, 0:NCOL], in0=sE[:, 0:NCOL], in1=o_[:, 0:NCOL], op=ALU.mult),
                             reads=["sE", on], writes=["sPT"])
                    for p in range(64):
                        vp, vn = Vpg[p % 4], "sPG%d" % (p % 4)
                        col = s_ * 64 + p
                        P.dma("pool", lambda e, vp=vp, col=col: e.indirect_dma_start(
                            out=vp, out_offset=None, in_=cvd, in_offset=bass.IndirectOffsetOnAxis(ap=sIdx[:, col:col + 1], axis=0)),
                            reads=["sIdx"], writes=[vn])
                        P.op("pe", lambda e, vp=vp, p=p: e.matmul(pACC[0][0:32, :], sPT[:, p * 32:(p + 1) * 32], vp, start=(p == 0), stop=False),
                             reads=["sPT", vn], writes=["pACC0"])
                    P.op("pe", lambda e: e.matmul(pACC[0][0:32, :], sPT[:, 2048:2080], cvb[:], start=False, stop=True),
                         reads=["sPT", "cvb"], writes=["pACC0"])
                    P.op("dve", lambda e: e.tensor_tensor(out=ytmp[:].rearrange("r (h d) -> r h d", h=8),
                                                          in0=pACC[0][0:32, :].rearrange("r (h d) -> r h d", h=8),
                                                          in1=HMc[:].unsqueeze(2).to_broadcast([32, 8, 64]), op=ALU.mult),
                         reads=["pACC0", "HMc"], writes=["junk"])
                    P.op("dve", lambda e: e.tensor_reduce(out=ysb[:], in_=ytmp[:].rearrange("r (h d) -> r d h", h=8), axis=AX.X, op=ALU.add),
                         reads=["junk"], writes=["ysb"])
                    if layer == 0:
                        P.op("dve", lambda e: e.tensor_scalar(out=ysb[:], in0=ysb[:], scalar1=rdn[:, 0:1], scalar2=None, op0=ALU.mult),
                             reads=["ysb", "rdn"], writes=["ysb"])
                    P.dma("sp", lambda e, s_=s_: e.dma_start(out=ybd[s_ * 4:(s_ + 1) * 4, :].rearrange("q (h d) -> (q h) d", d=64), in_=ysb[:]),
                          reads=["ysb"], writes=["ybd"])
                P.dma("sp", lambda e: e.dma_start(out=tokf[0][:], in_=ybd[:, :]), reads=["ybd"], writes=["tokf0"])
                for c in range(4):
                    P.op("pe", lambda e, c=c: e.transpose(pC[:, c * 128:(c + 1) * 128], tokf[0][:, c * 128:(c + 1) * 128], identf[:]),
                         reads=["tokf0", "identf"], writes=["pC"])
                P.op("dve", lambda e: e.tensor_tensor(out=yT[:, 4:8, 0:128], in0=pC[:].rearrange("p (c t) -> p c t", c=4),
                                                      in1=zs[:, :, 0:128], op=ALU.mult), reads=["pC", "zs"], writes=["yT"])
                out_proj_tile(0, 0, src[0:128, :], dst[0:128, :], rname=("xs1d" if layer == 1 else None),
                              wname=("xs1d" if layer == 0 else "dram_res"))
                do_barrier()

        mixT = qT32

        P.stop_at = stop_at
        try:
            layer0_setup()
            if do_prompt:
                for sq in range(npr):
                    layer0(sq)
            if do_sample:
                sample_phase(0)
            layer1_setup()
            if do_prompt:
                for sq in range(npr):
                    layer1(sq)
            if do_sample:
                sample_phase(1)
        except StopBuild:
            pass
        try:
            print("sbuf bytes remaining", nc.sbuf_bytes_remaining, "n_ops", len(P.ops))
        except Exception:
            pass
        P.emit()
    return nc


NCORES = 1


def kernel(**inputs):
    f32 = np.float32
    n = NCORES
    npr, nsq = 4 // n, 32 // n
    consts = make_consts()
    shared = {k: np.ascontiguousarray(inputs[k][0], dtype=f32) for k in
              ("norm_pre_e", "norm_post_e", "norm_pre_o", "norm_post_o", "w_in_e", "w_out_e",
               "w_in_o", "w_out_o", "conv_w", "gmlp_w", "gmlp_b")}
    n_phys = inputs["cache_k_moba"].shape[1]
    pools = {"ck_m": "cache_k_moba", "cv_m": "cache_v_moba", "ck_s": "cache_k_sb", "cv_s": "cache_v_sb"}
    for k, src in pools.items():
        shared[k] = np.ascontiguousarray(inputs[src][0], dtype=f32).reshape(n_phys * 128, 512)
    in_maps = []
    for c in range(n):
        m = dict(shared)
        m["consts"] = consts
        m["xp"] = np.ascontiguousarray(inputs["x_prompt"][c * npr:(c + 1) * npr], dtype=f32).reshape(npr * SEQ, D)
        xs = np.zeros((128, D), f32)
        xs[:4 * nsq] = np.asarray(inputs["x_sample"][c * nsq:(c + 1) * nsq], dtype=f32).reshape(4 * nsq, D)
        m["xs"] = xs
        stc = np.zeros((64, 512), f32)
        stc[:2 * nsq] = np.asarray(inputs["state_conv"][0, c * nsq:(c + 1) * nsq], dtype=f32).reshape(2 * nsq, 512)
        m["stc"] = stc
        m["ptab"] = np.ascontiguousarray(inputs["page_table"][c * nsq:(c + 1) * nsq], dtype=np.int32).reshape(-1)
        in_maps.append(m)
    nc = build(do_prompt=True, do_sample=True, n_phys=n_phys, npr=npr, nsq=nsq)
    res = run_bass_kernel_spmd(nc, in_maps, core_ids=list(range(n))).results

    def cat(name, rows, shape):
        return np.concatenate([np.asarray(r[name])[:rows] for r in res], axis=0).reshape(shape).astype(f32)

    nb, nd = 4, 32
    y_prompt = cat("y_prompt", npr * SEQ, (nb, SEQ, D))
    y_sample = cat("y_sample", 4 * nsq, (nd, 4, D))
    conv_prompt = cat("conv_prompt", npr * 2, (1, nb, 2, 512))
    conv_sample = cat("conv_sample", 2 * nsq, (1, nd, 2, 512))
    kvp = lambda name: cat(name, npr * SEQ, (1, nb, SEQ, 8, 64))
    kvs = lambda name: cat(name, 4 * nsq, (1, nd, 4, 8, 64))
    return (y_prompt, y_sample, conv_prompt, conv_sample,
            kvp("k_moba_prompt"), kvp("v_moba_prompt"), kvs("k_moba_sample"), kvs("v_moba_sample"),
            kvp("k_sb_prompt"), kvp("v_sb_prompt"), kvs("k_sb_sample"), kvs("v_sb_sample"),
            cat("gmlp_v_prompt", npr * 128, (1, nb, 128, 512)), cat("gmlp_v_sample", 4 * nsq, (1, nd, 4, 512)))
```

```python
import types
import numpy as np
from contextlib import ExitStack
import concourse.bass as bass
import concourse.mybir as mybir
from concourse.bass_utils import run_bass_kernel_spmd

F32 = mybir.dt.float32
BF16 = mybir.dt.bfloat16
I32 = mybir.dt.int32
AF = mybir.ActivationFunctionType
ALU = mybir.AluOpType
AX = mybir.AxisListType

COMPUTE = ("pe", "act", "dve", "pool")
NDMA_SEMS = 6
SEM_EPOCH = 30000
SKIP = {}

D = 1024
SEQ = 4096
NT = SEQ // 128
EPS = 1e-6
SCALE = 0.125
NEGBIG = -30000.0


class Op:
    __slots__ = ("eng", "fn", "reads", "writes", "deps", "is_dma", "queue", "signal",
                 "sem", "val", "prewait", "dk")

    def __init__(self, eng, fn, reads, writes, is_dma=False, queue=None):
        self.eng = eng
        self.fn = fn
        self.reads = reads
        self.writes = writes
        self.deps = []
        self.is_dma = is_dma
        self.queue = queue
        self.signal = False
        self.sem = None
        self.val = 0
        self.prewait = None


def _freeze(fn):
    if fn.__closure__ is None:
        return fn
    cells = []
    for c in fn.__closure__:
        try:
            cells.append(types.CellType(c.cell_contents))
        except ValueError:
            cells.append(c)
    return types.FunctionType(fn.__code__, fn.__globals__, fn.__name__, fn.__defaults__, tuple(cells))


class StopBuild(Exception):
    pass


class Prog:
    stop_at = None

    def __init__(self, nc, same_engine_sync=True):
        self.nc = nc
        self.ops = []
        self.last_writer = {}
        self.readers = {}
        self.dma_writers = {}
        self.dcnt = {"sp": 0, "act": 0, "pool": 0}
        self.same_engine_sync = same_engine_sync

    def _add(self, op):
        deps = set()
        for r in op.reads:
            w = self.last_writer.get(r)
            if w is not None:
                deps.add(w)
            deps.update(self.dma_writers.get(r, {}).values())
        for w_ in op.writes:
            w = self.last_writer.get(w_)
            if w is not None:
                deps.add(w)
            deps.update(self.dma_writers.get(w_, {}).values())
            for rd in self.readers.get(w_, ()):
                deps.add(rd)
        i = len(self.ops)
        if self.stop_at is not None and i >= self.stop_at:
            raise StopBuild()
        op.fn = _freeze(op.fn)
        op.deps = sorted(deps)
        self.ops.append(op)
        for r in op.reads:
            self.readers.setdefault(r, []).append(i)
        for w_ in op.writes:
            self.last_writer[w_] = i
            self.readers[w_] = []
            if op.is_dma:
                self.dma_writers.setdefault(w_, {})[(op.queue, op.dk % NDMA_SEMS)] = i
            else:
                self.dma_writers.pop(w_, None)
        return i

    def barrier(self, fns):
        names = sorted(set(self.last_writer) | set(self.readers))
        for eng, fn in fns:
            if eng in ("sp",):
                self.dma(eng, fn, writes=names)
            else:
                self._add(Op(eng, fn, (), tuple(names)))

    def op(self, eng, fn, reads=(), writes=()):
        writes = tuple(writes) + tuple(r for r in reads if r[0] == "p" and r[1].isupper() and r not in writes)
        return self._add(Op(eng, fn, tuple(reads), tuple(writes)))

    def dma(self, queue, fn, reads=(), writes=()):
        o = Op("dma_" + queue, fn, tuple(reads), tuple(writes), is_dma=True, queue=queue)
        o.dk = self.dcnt[queue]
        self.dcnt[queue] += 1
        return self._add(o)

    def emit(self):
        nc = self.nc
        ops = self.ops
        for o in ops:
            for d in o.deps:
                p = ops[d]
                if p.is_dma:
                    p.signal = True
                    continue
                same = (p.eng == o.eng) and not o.is_dma
                if same and (p.eng == "pe" or not self.same_engine_sync):
                    continue
                p.signal = True
        for o in ops:
            if o.is_dma:
                o.signal = True
        with ExitStack() as st:
            sems = {e: st.enter_context(nc.semaphore("s_" + e)) for e in COMPUTE}
            dsem = {q: [st.enter_context(nc.semaphore("d_%s%d" % (q, k))) for k in range(NDMA_SEMS)]
                    for q in ("sp", "act", "pool")}
            cnt = {e: 0 for e in COMPUTE}
            ep = {e: 0 for e in COMPUTE}
            for o in ops:
                if o.is_dma:
                    k = o.dk
                    o.sem = dsem[o.queue][k % NDMA_SEMS]
                    o.val = 16 * (k // NDMA_SEMS + 1)
                    o.prewait = (o.sem, o.val - 16) if o.val > 16 else None
                elif o.signal:
                    if cnt[o.eng] >= SEM_EPOCH:
                        ep[o.eng] += 1
                        sems[o.eng] = st.enter_context(nc.semaphore("s_%s_%d" % (o.eng, ep[o.eng])))
                        cnt[o.eng] = 0
                    cnt[o.eng] += 1
                    o.sem = sems[o.eng]
                    o.val = cnt[o.eng]
            issue_eng = {"pe": "pe", "act": "act", "dve": "dve", "pool": "pool",
                         "dma_sp": "sp", "dma_act": "act", "dma_pool": "pool"}
            streams = {"pe": [], "act": [], "dve": [], "pool": [], "sp": []}
            for i, o in enumerate(ops):
                streams[issue_eng[o.eng]].append(i)
            block = st.enter_context(nc.Block())

            def run_stream(name, eng):
                waited = {}
                for i in streams[name]:
                    o = ops[i]
                    need = {}
                    for d in o.deps:
                        p = ops[d]
                        if p.sem is None:
                            continue
                        if need.get(p.sem, (0, None))[0] < p.val:
                            need[p.sem] = (p.val, p.sem)
                    if o.prewait is not None:
                        s, v = o.prewait
                        if need.get(s, (0, None))[0] < v:
                            need[s] = (v, s)
                    for key, (v, s) in need.items():
                        if waited.get(key, 0) < v:
                            eng.wait_ge(s, v)
                            waited[key] = v
                    ins = o.fn(eng)
                    if o.signal:
                        ins.then_inc(o.sem, 16 if o.is_dma else 1)
                if name == "sp":
                    last = {}
                    for o in ops:
                        if o.sem is not None and last.get(o.sem, (0,))[0] < o.val:
                            last[o.sem] = (o.val, o.sem)
                    for key, (v, s) in last.items():
                        if waited.get(key, 0) < v:
                            eng.wait_ge(s, v)

            @block.tensor
            def _(e):
                run_stream("pe", e)

            @block.scalar
            def _(e):
                run_stream("act", e)

            @block.vector
            def _(e):
                run_stream("dve", e)

            @block.gpsimd
            def _(e):
                run_stream("pool", e)

            @block.sync
            def _(e):
                run_stream("sp", e)


C_IDENT, C_TRI, C_ONES, C_CM, C_PEN, C_TRIL, C_OH = 0, 128, 256, 384, 896, 2944, 3072
C_PIDX, C_A, C_BM, C_BS, C_HM, C_OHC, C_REP, C_BD = 5120, 5121, 5153, 5185, 5217, 5225, 6249, 6377
C_W = 6505


def make_consts():
    c = np.zeros((128, C_W), np.float32)
    i = np.arange(128)
    c[:, C_IDENT:C_IDENT + 128] = np.eye(128)
    c[:, C_TRI:C_TRI + 128] = (i[:, None] >= i[None, :])
    c[:, C_ONES:C_ONES + 128] = 1.0
    t256 = np.arange(256)
    c[:, C_CM:C_CM + 256] = (i[:, None] <= t256[None, :])
    c[:, C_CM + 256:C_CM + 512] = (128 + i[:, None] <= t256[None, :])
    t512 = np.arange(512)
    for j in range(4):
        c[:, C_PEN + 512 * j:C_PEN + 512 * (j + 1)] = np.where(128 * j + i[:, None] < t512[None, :], 0.0, NEGBIG)
    c[:, C_TRIL:C_TRIL + 128] = (i[None, :] <= i[:, None])
    for n in range(16):
        c[n, C_OH + 128 * n:C_OH + 128 * (n + 1)] = 1.0
    col = np.arange(32)
    c[:, C_PIDX] = i
    c[:, C_A:C_A + 32] = (i[:, None] // 4 == col[None, :])
    c[:, C_BM:C_BM + 32] = (i[:, None] % 4 <= col[None, :] // 8)
    c[:, C_BS:C_BS + 32] = (i[:, None] % 4 < col[None, :] // 8)
    c[0:32, C_HM:C_HM + 8] = (col[:, None] % 8 == np.arange(8)[None, :])
    for n in range(32):
        c[:, C_OHC + 32 * n + n] = 1.0
    c[0:4, C_REP:C_REP + 128] = (np.arange(4)[:, None] == i[None, :] % 4)
    c[:, C_BD:C_BD + 128] = (i[:, None] // 4 == i[None, :] // 4) & (i[:, None] % 4 <= i[None, :] % 4)
    return c


def build(do_prompt=True, do_sample=True, n_phys=2560, same_engine_sync=True, nblk=8, stop_at=None, npr=4, nsq=32):
    nc = bass.Bass("TRN2", target_bir_lowering=False)
    dt_in = {}

    def din(name, shape, dt=F32):
        dt_in[name] = nc.dram_tensor(name, shape, dt, kind="ExternalInput").ap()
        return dt_in[name]

    def dout(name, shape, dt=F32):
        return nc.dram_tensor(name, shape, dt, kind="ExternalOutput").ap()

    consts = din("consts", [128, C_W])
    xp = din("xp", [npr * SEQ, D])
    npre_e = din("norm_pre_e", [D]); npost_e = din("norm_post_e", [D])
    npre_o = din("norm_pre_o", [D]); npost_o = din("norm_post_o", [D])
    w_in_e = din("w_in_e", [D, 4096]); w_out_e = din("w_out_e", [D, D])
    w_in_o = din("w_in_o", [D, 3584]); w_out_o = din("w_out_o", [D, D])
    conv_w = din("conv_w", [3, 512])
    gmlp_w = din("gmlp_w", [8, 128, 128]); gmlp_b = din("gmlp_b", [8, 128])

    o_y = dout("y_prompt", [npr * SEQ, D])
    o_conv = dout("conv_prompt", [npr * 2, 512])
    o_kmb = dout("k_moba_prompt", [npr * SEQ, 512]); o_vmb = dout("v_moba_prompt", [npr * SEQ, 512])
    o_ksb = dout("k_sb_prompt", [npr * SEQ, 512]); o_vsb = dout("v_sb_prompt", [npr * SEQ, 512])
    o_gv = dout("gmlp_v_prompt", [npr * 128, 512])
    x1d = nc.dram_tensor("x1_scratch", [npr * SEQ, D], F32, kind="Internal").ap()
    if do_sample:
        xs = din("xs", [128, D])
        stc = din("stc", [64, 512])
        ptab = din("ptab", [nsq * 64], I32)
        ck_m = din("ck_m", [n_phys * 128, 512]); cv_m = din("cv_m", [n_phys * 128, 512])
        ck_s = din("ck_s", [n_phys * 128, 512]); cv_s = din("cv_s", [n_phys * 128, 512])
        o_ys = dout("y_sample", [128, D])
        o_convs = dout("conv_sample", [64, 512])
        o_kms = dout("k_moba_sample", [128, 512]); o_vms = dout("v_moba_sample", [128, 512])
        o_kss = dout("k_sb_sample", [128, 512]); o_vss = dout("v_sb_sample", [128, 512])
        o_gvs = dout("gmlp_v_sample", [128, 512])
        xs1d = nc.dram_tensor("xs1_scratch", [128, D], F32, kind="Internal").ap()
        ybd = nc.dram_tensor("yb_scratch", [128, 512], F32, kind="Internal").ap()

    with ExitStack() as st:
        def sb(name, shape, dt):
            return st.enter_context(nc.sbuf_tensor(name, shape, dt))

        def ps(name, shape, dt):
            return st.enter_context(nc.psum_tensor(name, shape, dt))

        P = Prog(nc, same_engine_sync=same_engine_sync)

        ident = sb("ident", [128, 128], BF16)
        identf = sb("identf", [128, 128], F32)
        tri = sb("tri", [128, 128], BF16)
        onesm = sb("onesm", [128, 128], BF16)
        cm = sb("cm", [128, 512], F32)
        pen = sb("pen", [128, 2048], BF16)
        tril = sb("tril", [128, 128], F32)
        oneh = sb("oneh", [16, 2048], BF16)
        P.dma("pool", lambda e: e.dma_start(out=ident[:], in_=consts[:, C_IDENT:C_IDENT + 128]), writes=["ident"])
        P.dma("sp", lambda e: e.dma_start(out=identf[:], in_=consts[:, C_IDENT:C_IDENT + 128]), writes=["identf"])
        P.dma("pool", lambda e: e.dma_start(out=tri[:], in_=consts[:, C_TRI:C_TRI + 128]), writes=["tri"])
        P.dma("pool", lambda e: e.dma_start(out=onesm[:], in_=consts[:, C_ONES:C_ONES + 128]), writes=["onesm"])
        P.dma("sp", lambda e: e.dma_start(out=cm[:], in_=consts[:, C_CM:C_CM + 512]), writes=["cm"])
        P.dma("pool", lambda e: e.dma_start(out=pen[:], in_=consts[:, C_PEN:C_PEN + 2048]), writes=["pen"])
        P.dma("sp", lambda e: e.dma_start(out=tril[:], in_=consts[:, C_TRIL:C_TRIL + 128]), writes=["tril"])
        P.dma("pool", lambda e: e.dma_start(out=oneh[:], in_=consts[0:16, C_OH:C_OH + 2048]), writes=["oneh"])

        wout = sb("wout", [128, 8, D], BF16)
        NWB = 4
        wbuf = [sb("wbuf%d" % i, [128, 8, 128], BF16) for i in range(NWB)]
        wtok = [sb("wtok%d" % i, [128, 8, 512], BF16) for i in range(1)]
        arena = sb("arena", [128, 16384], F32)
        KT = arena[:, 0:8192].bitcast(BF16).rearrange("p (c k) -> p c k", c=4)
        V = arena[:, 8192:16384].bitcast(BF16).rearrange("p (t n) -> p t n", t=NT)
        xt = sb("xt", [128, D], F32)
        xn = sb("xn", [128, D], BF16)
        xnT = sb("xnT", [128, 8, 512], BF16)
        gpre = sb("gpre", [128, 8], F32)
        gpost = sb("gpost", [128, D], F32)
        ssq = sb("ssq", [128, 1], F32)
        rstd = sb("rstd", [128, 1], F32)
        junk = sb("junk", [128, D], F32)
        qT = sb("qT", [128, 4, 512], BF16)
        qT32 = sb("qT32", [128, 4, 512], F32)
        yT = sb("yT", [128, 8, 512], BF16)
        zs = sb("zs", [128, 4, 512], F32)
        tA = sb("tA", [128, 512], F32)
        tB = sb("tB", [128, 512], F32)
        tC = sb("tC", [128, 512], F32)
        ubuf = sb("ubuf", [128, 514], F32)
        ucarry = sb("ucarry", [128, 4, 2], F32)
        cwT = sb("cwT", [128, 4, 3], F32)
        tokf = [sb("tokf%d" % i, [128, 512], F32) for i in range(2)]
        E = [sb("E%d" % i, [128, 512], F32) for i in range(2)]
        L = [sb("L%d" % i, [128, 512], BF16) for i in range(2)]
        X = [sb("X%d" % i, [128, 512], F32) for i in range(2)]
        PT = [sb("PT%d" % i, [128, 512], BF16) for i in range(2)]
        Lsum = sb("Lsum", [128, 512], F32)
        Lsumb = sb("Lsumb", [128, 512], BF16)
        kmT = sb("kmT", [128, 4, 16], F32)
        gsb = sb("gsb", [128, 16], F32)
        m8 = sb("m8", [128, 8], F32)
        self_ = sb("sel", [128, 16], F32)
        selT = sb("selT", [16, 256], BF16)
        rden = sb("rden", [128, 256], F32)
        onecol = sb("onecol", [128, 1], F32)
        wmT = sb("wmT", [128, 8, 128], BF16)
        wnat = sb("wnat", [128, 128], F32)
        bB = sb("bB", [128, 4, 128], F32)
        cvb = sb("cvb", [128, 512], BF16)

        pAB = ps("pAB", [128, 1024], F32)
        pT = ps("pT", [128, 1024], BF16)
        pS = [ps("pS%d" % i, [128, 512], F32) for i in range(2)]
        pC = ps("pC", [128, 512], F32)
        pACC = [ps("pACC%d" % i, [128, 512], F32) for i in range(2)]

        epsc = sb("epsc", [128, 1], F32)
        P.op("dve", lambda e: e.memset(epsc[:], EPS), writes=["epsc"])
        P.op("dve", lambda e: e.memset(onecol[:], 1.0), writes=["onecol"])
        P.op("dve", lambda e: e.memset(gsb[:], -1e30), writes=["gsb"])

        ctr = {"w": 0, "wt": 0, "ab": 0, "s": 0, "acc": 0, "tok": 0, "e": 0}

        def load_g(npre, npost):
            P.dma("sp", lambda e: e.dma_start(out=gpre[:], in_=npre.rearrange("(c p) -> p c", p=128),
                                              allow_slow_non_contiguous=True), writes=["gpre"])
            P.dma("sp", lambda e: e.dma_start(out=gpost[:], in_=npost.partition_broadcast(128)), writes=["gpost"])

        def load_wout(w):
            P.dma("pool", lambda e: e.dma_start(out=wout[:], in_=w.rearrange("(c p) n -> p c n", p=128)),
                  writes=["wout"])

        def norm_transpose(src_rows, col0, rname=None):
            P.dma("sp", lambda e: e.dma_start(out=xt[:], in_=src_rows), reads=([rname] if rname else []), writes=["xt"])
            P.op("act", lambda e: e.activation(out=junk[:], in_=xt[:], func=AF.Square, accum_out=ssq[:]),
                 reads=["xt"], writes=["junk", "ssq"])
            P.op("act", lambda e: e.activation(out=rstd[:], in_=ssq[:], func=AF.Ln, scale=1.0 / D, bias=epsc[:, 0:1]),
                 reads=["ssq", "epsc"], writes=["rstd"])
            P.op("act", lambda e: e.activation(out=rstd[:], in_=rstd[:], func=AF.Exp, scale=-0.5),
                 reads=["rstd"], writes=["rstd"])
            P.op("dve", lambda e: e.tensor_scalar(out=xn[:], in0=xt[:], scalar1=rstd[:, 0:1], scalar2=None,
                                                  op0=ALU.mult), reads=["xt", "rstd"], writes=["xn"])
            for c in range(8):
                P.op("pe", lambda e, c=c: e.transpose(pT[:, c * 128:(c + 1) * 128], xn[:, c * 128:(c + 1) * 128],
                                                      ident[:]), reads=["xn", "ident"], writes=["pT"])
            P.op("dve", lambda e: e.tensor_tensor(
                out=xnT[:, :, col0:col0 + 128], in0=pT[:].rearrange("p (c t) -> p c t", c=8),
                in1=gpre[:].unsqueeze(2).to_broadcast([128, 8, 128]), op=ALU.mult),
                reads=["pT", "gpre"], writes=["xnT"])

        def proj_feat(w_in, col0, ntok=512):
            wb = wbuf[ctr["w"] % NWB]
            wn = "wbuf%d" % (ctr["w"] % NWB)
            ctr["w"] += 1
            P.dma("pool", lambda e: e.dma_start(
                out=wb[:], in_=w_in[:, col0:col0 + 128].rearrange("(c p) n -> p c n", p=128)), writes=[wn])
            half = ctr["ab"] % 2
            ctr["ab"] += 1
            pn = "pAB%d" % half
            dst = pAB[:, half * 512:half * 512 + ntok]
            for kc in range(8):
                P.op("pe", lambda e, kc=kc: e.matmul(dst, wb[:, kc, :], xnT[:, kc, 0:ntok],
                                                     start=(kc == 0), stop=(kc == 7)),
                     reads=[wn, "xnT"], writes=[pn])
            return dst, pn

        def load_wtok(w_in, col0):
            i = 0
            P.dma("pool", lambda e: e.dma_start(
                out=wtok[i][:], in_=w_in[:, col0:col0 + 512].rearrange("(c p) n -> p c n", p=128)),
                writes=["wtok%d" % i])
            return wtok[i], "wtok%d" % i

        def proj_tok(wt, wtn, tcol0):
            half = ctr["ab"] % 2
            ctr["ab"] += 1
            pn = "pAB%d" % half
            dst = pAB[:, half * 512:(half + 1) * 512]
            for kc in range(8):
                P.op("pe", lambda e, kc=kc: e.matmul(dst, xnT[:, kc, tcol0:tcol0 + 128], wt[:, kc, :],
                                                     start=(kc == 0), stop=(kc == 7)),
                     reads=[wtn, "xnT"], writes=[pn])
            return dst, pn

        def out_proj_tile(blk, ti, src_rows, dst_rows, rname=None, wname="dram_res"):
            tc0 = ti * 128
            for nh in range(2):
                for c in range(8):
                    P.op("pe", lambda e, c=c, nh=nh: e.matmul(
                        pAB[:, nh * 512:(nh + 1) * 512], yT[:, c, tc0:tc0 + 128], wout[:, c, nh * 512:(nh + 1) * 512],
                        start=(c == 0), stop=(c == 7)), reads=["yT", "wout"], writes=["pAB%d" % nh])
            P.dma("sp", lambda e: e.dma_start(out=xt[:], in_=src_rows), reads=([rname] if rname else []), writes=["xt"])
            P.op("act", lambda e: e.activation(out=junk[:], in_=pAB[:], func=AF.Square, accum_out=ssq[:]),
                 reads=["pAB0", "pAB1"], writes=["junk", "ssq"])
            P.op("act", lambda e: e.activation(out=rstd[:], in_=ssq[:], func=AF.Ln, scale=1.0 / D, bias=epsc[:, 0:1]),
                 reads=["ssq", "epsc"], writes=["rstd"])
            P.op("act", lambda e: e.activation(out=rstd[:], in_=rstd[:], func=AF.Exp, scale=-0.5),
                 reads=["rstd"], writes=["rstd"])
            P.op("dve", lambda e: e.scalar_tensor_tensor(out=junk[:], in0=pAB[:], scalar=rstd[:, 0:1], in1=gpost[:],
                                                         op0=ALU.mult, op1=ALU.mult),
                 reads=["pAB0", "pAB1", "rstd", "gpost"], writes=["junk"])
            P.op("dve", lambda e: e.tensor_tensor(out=xt[:], in0=xt[:], in1=junk[:], op=ALU.add),
                 reads=["xt", "junk"], writes=["xt"])
            P.dma("sp", lambda e: e.dma_start(out=dst_rows, in_=xt[:]), reads=["xt"], writes=[wname])

        def tok_store(psrc, pn, dram_rows, extra=None):
            i = ctr["tok"] % 2
            ctr["tok"] += 1
            tn = "tokf%d" % i
            P.op("act", lambda e: e.activation(out=tokf[i][:], in_=psrc, func=AF.Copy), reads=[pn], writes=[tn])
            P.dma("sp", lambda e: e.dma_start(out=dram_rows, in_=tokf[i][:]), reads=[tn], writes=["dram_kv"])
            return tokf[i], tn

        def layer0_setup():
            load_g(npre_e, npost_e)
            load_wout(w_out_e)
            for j in range(3):
                P.dma("sp", lambda e, j=j: e.dma_start(out=cwT[:, :, j], in_=conv_w[j].rearrange("(c p) -> p c", p=128),
                                                       allow_slow_non_contiguous=True), writes=["cwT"])

        def layer0(sq):
            R0 = sq * SEQ
            P.op("dve", lambda e: e.memset(ucarry[:], 0.0), writes=["ucarry"])
            P.op("dve", lambda e: e.memset(gsb[:], -1e30), writes=["gsb"])
            for blk in range(nblk):
                r0 = blk * 512
                for ti in range(4):
                    norm_transpose(xp[R0 + r0 + ti * 128:R0 + r0 + (ti + 1) * 128, :], ti * 128)
                wt, wtn = load_wtok(w_in_e, 2048 + 512)
                for ti in range(4):
                    g = blk * 4 + ti
                    psrc, pn = proj_tok(wt, wtn, ti * 128)
                    tf, tn = tok_store(psrc, pn, o_kmb[R0 + g * 128:R0 + (g + 1) * 128, :])
                    for p in range(4):
                        P.op("pe", lambda e, p=p, ti=ti, tf=tf: e.matmul(
                            pC[:, 256 + 4 * (ti % 2) + p:256 + 4 * (ti % 2) + p + 1], tf[:, p * 128:(p + 1) * 128], onecol[:],
                            start=True, stop=True), reads=[tn, "onecol"], writes=["pC"])
                    if ti % 2 == 1:
                        P.op("dve", lambda e, g=g: e.tensor_copy(out=kmT[:, :, g // 2], in_=pC[:, 256:260]),
                             reads=["pC"], writes=["kmT"])
                        P.op("dve", lambda e, g=g: e.tensor_tensor(out=kmT[:, :, g // 2], in0=kmT[:, :, g // 2],
                                                                   in1=pC[:, 260:264], op=ALU.add),
                             reads=["pC", "kmT"], writes=["kmT"])
                wt, wtn = load_wtok(w_in_e, 2048 + 1024)
                for ti in range(4):
                    g = blk * 4 + ti
                    psrc, pn = proj_tok(wt, wtn, ti * 128)
                    P.op("dve", lambda e, g=g, psrc=psrc: e.tensor_copy(out=V[:, g, :], in_=psrc),
                         reads=[pn], writes=["V%d" % g])
                    tok_store(psrc, pn, o_vmb[R0 + g * 128:R0 + (g + 1) * 128, :])
                for p in range(4):
                    psrc, pn = proj_feat(w_in_e, 2048 + p * 128)
                    P.op("act", lambda e, p=p, psrc=psrc: e.activation(out=qT[:, p, :], in_=psrc, func=AF.Copy),
                         reads=[pn], writes=["qT"])
                    P.op("dve", lambda e, p=p, psrc=psrc: e.tensor_copy(out=qT32[:, p, :], in_=psrc),
                         reads=[pn], writes=["qT32"])
                for p in range(4):
                    psrc, pn = proj_feat(w_in_e, 2048 + 512 + p * 128)
                    P.op("act", lambda e, p=p, psrc=psrc: e.activation(out=KT[:, p, r0:r0 + 512], in_=psrc, func=AF.Copy),
                         reads=[pn], writes=["KT%d" % blk])
                for p in range(4):
                    psrc, pn = proj_feat(w_in_e, 2048 + 1536 + p * 128)
                    P.op("act", lambda e, p=p, psrc=psrc: e.activation(out=zs[:, p, :], in_=psrc, func=AF.Silu),
                         reads=[pn], writes=["zs"])
                for c in range(4):
                    psrc, pn = proj_feat(w_in_e, 512 + c * 128)
                    P.op("act", lambda e, psrc=psrc: e.activation(out=tA[:], in_=psrc, func=AF.Copy),
                         reads=[pn], writes=["tA"])
                    psrc, pn = proj_feat(w_in_e, 1024 + c * 128)
                    P.op("dve", lambda e, c=c: e.tensor_copy(out=ubuf[:, 0:2], in_=ucarry[:, c, :]),
                         reads=["ucarry"], writes=["ubuf"])
                    P.op("dve", lambda e, psrc=psrc: e.tensor_tensor(out=ubuf[:, 2:514], in0=psrc, in1=tA[:], op=ALU.mult),
                         reads=[pn, "tA"], writes=["ubuf"])
                    P.op("dve", lambda e, c=c: e.tensor_copy(out=ucarry[:, c, :], in_=ubuf[:, 512:514]),
                         reads=["ubuf"], writes=["ucarry"])
                    P.op("dve", lambda e, c=c: e.tensor_scalar(out=tB[:], in0=ubuf[:, 0:512], scalar1=cwT[:, c, 0:1],
                                                               scalar2=None, op0=ALU.mult),
                         reads=["ubuf", "cwT"], writes=["tB"])
                    P.op("dve", lambda e, c=c: e.scalar_tensor_tensor(out=tB[:], in0=ubuf[:, 1:513], scalar=cwT[:, c, 1:2],
                                                                      in1=tB[:], op0=ALU.mult, op1=ALU.add),
                         reads=["ubuf", "cwT", "tB"], writes=["tB"])
                    P.op("dve", lambda e, c=c: e.scalar_tensor_tensor(out=tB[:], in0=ubuf[:, 2:514], scalar=cwT[:, c, 2:3],
                                                                      in1=tB[:], op0=ALU.mult, op1=ALU.add),
                         reads=["ubuf", "cwT", "tB"], writes=["tB"])
                    psrc, pn = proj_feat(w_in_e, 1536 + c * 128)
                    P.op("act", lambda e, psrc=psrc: e.activation(out=tC[:], in_=psrc, func=AF.Silu),
                         reads=[pn], writes=["tC"])
                    P.op("dve", lambda e: e.tensor_tensor(out=tB[:], in0=tB[:], in1=tC[:], op=ALU.mult),
                         reads=["tB", "tC"], writes=["tB"])
                    psrc, pn = proj_feat(w_in_e, 0 + c * 128)
                    P.op("dve", lambda e, c=c, psrc=psrc: e.tensor_tensor(out=yT[:, c, :], in0=psrc, in1=tB[:], op=ALU.mult),
                         reads=[pn, "tB"], writes=["yT"])
                if blk == nblk - 1:
                    for j in range(2):
                        P.dma("sp", lambda e, j=j: e.dma_start(out=o_conv[sq * 2 + j].rearrange("(c p) -> p c", p=128), in_=ucarry[:, :, j],
                                                               allow_slow_non_contiguous=True), reads=["ucarry"], writes=["o_conv"])
                for half in range(2 if not SKIP.get("moba") else 0):
                    g = blk * 2 + half
                    q0 = half * 256
                    for h in range(8):
                        p, hr = h // 2, (h % 2) * 64
                        need_sel = g >= 4
                        if need_sel:
                            for qt in range(2):
                                P.op("pe", lambda e, qt=qt: e.matmul(
                                    pC[:, 0:g], qT32[hr:hr + 64, p, q0 + qt * 128:q0 + (qt + 1) * 128],
                                    kmT[hr:hr + 64, p, 0:g], start=True, stop=True),
                                    reads=["qT32", "kmT"], writes=["pC"])
                                P.op("dve", lambda e: e.tensor_copy(out=gsb[:, 0:g], in_=pC[:, 0:g]),
                                     reads=["pC"], writes=["gsb"])
                                P.op("dve", lambda e: e.max(out=m8[:], in_=gsb[:]), reads=["gsb"], writes=["m8"])
                                P.op("dve", lambda e: e.tensor_scalar(out=self_[:], in0=gsb[:], scalar1=m8[:, 2:3],
                                                                      scalar2=None, op0=ALU.is_ge),
                                     reads=["gsb", "m8"], writes=["sel"])
                                P.op("pe", lambda e: e.transpose(pC[0:16, 128:256], self_[:], identf[:]),
                                     reads=["sel", "identf"], writes=["pC"])
                                P.op("dve", lambda e, qt=qt: e.tensor_copy(out=selT[:, qt * 128:(qt + 1) * 128],
                                                                           in_=pC[0:16, 128:256]),
                                     reads=["pC"], writes=["selT"])
                        acc = pACC[ctr["acc"] % 2]
                        accn = "pACC%d" % (ctr["acc"] % 2)
                        ctr["acc"] += 1
                        nkt = 2 * (g + 1)
                        def mobaA(kt):
                            n = kt // 2
                            si = ctr["s"] % 2
                            ctr["s"] += 1
                            pSn = "pS%d" % si
                            P.op("pe", lambda e, kt=kt, si=si: e.matmul(
                                pS[si][:, 0:256], KT[hr:hr + 64, p, kt * 128:(kt + 1) * 128], qT[hr:hr + 64, p, q0:q0 + 256],
                                start=True, stop=True), reads=["KT%d" % (kt // 4), "qT"], writes=[pSn])
                            ei = ctr["e"] % 2
                            ctr["e"] += 1
                            if n == g:
                                P.op("act", lambda e, si=si, ei=ei: e.activation(out=E[ei][:, 0:256], in_=pS[si][:, 0:256],
                                                                                func=AF.Exp, scale=SCALE),
                                     reads=[pSn], writes=["E%d" % ei])
                                j = kt % 2
                                P.op("dve", lambda e, ei=ei, j=j: e.tensor_tensor(out=PT[ei][:, 0:256], in0=E[ei][:, 0:256],
                                                                                  in1=cm[:, j * 256:(j + 1) * 256], op=ALU.mult),
                                     reads=["E%d" % ei, "cm"], writes=["PT%d" % ei])
                            elif need_sel:
                                if kt % 2 == 0:
                                    P.op("pe", lambda e, n=n: e.matmul(pC[:, 256:512], oneh[:, n * 128:(n + 1) * 128], selT[:],
                                                                       start=True, stop=True),
                                         reads=["oneh", "selT"], writes=["pC"])
                                P.op("act", lambda e, si=si, ei=ei: e.activation(out=E[ei][:, 0:256], in_=pS[si][:, 0:256],
                                                                                func=AF.Exp, scale=SCALE),
                                     reads=[pSn], writes=["E%d" % ei])
                                P.op("dve", lambda e, ei=ei: e.tensor_tensor(out=PT[ei][:, 0:256], in0=E[ei][:, 0:256],
                                                                             in1=pC[:, 256:512], op=ALU.mult),
                                     reads=["E%d" % ei, "pC"], writes=["PT%d" % ei])
                            else:
                                P.op("act", lambda e, si=si, ei=ei: e.activation(out=PT[ei][:, 0:256], in_=pS[si][:, 0:256],
                                                                                func=AF.Exp, scale=SCALE),
                                     reads=[pSn], writes=["PT%d" % ei])
                            return ei

                        def mobaB(kt, ei):
                            P.op("pe", lambda e, kt=kt, ei=ei, acc=acc: e.matmul(
                                acc[:, 0:256], V[:, kt, p * 128:(p + 1) * 128], PT[ei][:, 0:256],
                                start=(kt == 0), stop=(kt == nkt - 1), skip_group_check=True), reads=["V%d" % kt, "PT%d" % ei], writes=[accn])
                            P.op("pe", lambda e, kt=kt, ei=ei, acc=acc: e.matmul(
                                acc[:, 256:512], onesm[:], PT[ei][:, 0:256],
                                start=False, stop=(kt == nkt - 1), skip_group_check=True), reads=["onesm", "PT%d" % ei], writes=[accn])

                        eis = {}
                        for i_ in range(nkt + 1):
                            if i_ < nkt:
                                eis[i_] = mobaA(i_)
                            if i_ >= 1:
                                mobaB(i_ - 1, eis[i_ - 1])
                        P.op("dve", lambda e, acc=acc: e.reciprocal(out=rden[hr:hr + 64, :], in_=acc[hr:hr + 64, 256:512]),
                             reads=[accn], writes=["rden"])
                        P.op("dve", lambda e, acc=acc: e.tensor_tensor(out=rden[hr:hr + 64, :], in0=acc[hr:hr + 64, 0:256],
                                                                       in1=rden[hr:hr + 64, :], op=ALU.mult),
                             reads=[accn, "rden"], writes=["rden"])
                        P.op("dve", lambda e: e.tensor_tensor(out=yT[hr:hr + 64, 4 + p, q0:q0 + 256], in0=rden[hr:hr + 64, :],
                                                              in1=zs[hr:hr + 64, p, q0:q0 + 256], op=ALU.mult),
                             reads=["rden", "zs"], writes=["yT"])
                for ti in range(4):
                    rr = r0 + ti * 128
                    out_proj_tile(blk, ti, xp[R0 + rr:R0 + rr + 128, :], x1d[R0 + rr:R0 + rr + 128, :], wname="x1d")

        def layer1_setup():
            load_g(npre_o, npost_o)
            load_wout(w_out_o)
            for g in range(8):
                P.dma("sp", lambda e, g=g: e.dma_start(out=wnat[:], in_=gmlp_w[g]), writes=["wnat"])
                P.op("dve", lambda e: e.tensor_tensor(out=wnat[:], in0=wnat[:], in1=tril[:], op=ALU.mult),
                     reads=["wnat", "tril"], writes=["wnat"])
                P.op("pe", lambda e: e.transpose(pC[:, 0:128], wnat[:], identf[:]), reads=["wnat", "identf"], writes=["pC"])
                P.op("dve", lambda e, g=g: e.tensor_copy(out=wmT[:, g, :], in_=pC[:, 0:128]), reads=["pC"], writes=["wmT"])
                P.dma("sp", lambda e, g=g: e.dma_start(out=bB[(g % 2) * 64:(g % 2) * 64 + 64, g // 2, :],
                                                       in_=gmlp_b[g].partition_broadcast(64)), writes=["bB"])

        def layer1(sq):
            R0 = sq * SEQ
            for blk in range(nblk):
                r0 = blk * 512
                for ti in range(4):
                    norm_transpose(x1d[R0 + r0 + ti * 128:R0 + r0 + (ti + 1) * 128, :], ti * 128, rname="x1d")
                wt, wtn = load_wtok(w_in_o, 1536 + 512)
                for ti in range(4):
                    g = blk * 4 + ti
                    psrc, pn = proj_tok(wt, wtn, ti * 128)
                    tok_store(psrc, pn, o_ksb[R0 + g * 128:R0 + (g + 1) * 128, :])
                wt, wtn = load_wtok(w_in_o, 1536 + 1024)
                for ti in range(4):
                    g = blk * 4 + ti
                    psrc, pn = proj_tok(wt, wtn, ti * 128)
                    P.op("dve", lambda e, g=g, psrc=psrc: e.tensor_copy(out=V[:, g, :], in_=psrc),
                         reads=[pn], writes=["V%d" % g])
                    tok_store(psrc, pn, o_vsb[R0 + g * 128:R0 + (g + 1) * 128, :])
                for p in range(4):
                    psrc, pn = proj_feat(w_in_o, 1536 + p * 128)
                    P.op("act", lambda e, p=p, psrc=psrc: e.activation(out=qT[:, p, :], in_=psrc, func=AF.Copy),
                         reads=[pn], writes=["qT"])
                for p in range(4):
                    psrc, pn = proj_feat(w_in_o, 1536 + 512 + p * 128)
                    P.op("act", lambda e, p=p, psrc=psrc: e.activation(out=KT[:, p, r0:r0 + 512], in_=psrc, func=AF.Copy),
                         reads=[pn], writes=["KT%d" % blk])
                for p in range(4):
                    psrc, pn = proj_feat(w_in_o, 1536 + 1536 + p * 128)
                    P.op("act", lambda e, p=p, psrc=psrc: e.activation(out=zs[:, p, :], in_=psrc, func=AF.Silu),
                         reads=[pn], writes=["zs"])
                wt, wtn = load_wtok(w_in_o, 512)
                for ti in range(4):
                    g = blk * 4 + ti
                    psrc, pn = proj_tok(wt, wtn, ti * 128)
                    P.op("dve", lambda e, psrc=psrc: e.tensor_copy(out=cvb[:], in_=psrc), reads=[pn], writes=["cvb"])
                    if g == NT - 1:
                        tok_store(psrc, pn, o_gv[sq * 128:(sq + 1) * 128, :])
                    for p in range(4):
                        for j in range(2):
                            gg = 2 * p + j
                            P.op("pe", lambda e, p=p, j=j, gg=gg: e.matmul(
                                pC[:, j * 128:(j + 1) * 128], cvb[:, p * 128:(p + 1) * 128], wmT[:, gg, :],
                                start=True, stop=True), reads=["cvb", "wmT"], writes=["pC"])
                        for j in range(2):
                            P.op("dve", lambda e, p=p, j=j, ti=ti: e.tensor_tensor(
                                out=tA[j * 64:(j + 1) * 64, p * 128:(p + 1) * 128],
                                in0=pC[j * 64:(j + 1) * 64, j * 128:(j + 1) * 128],
                                in1=bB[j * 64:(j + 1) * 64, p, :], op=ALU.add), reads=["pC", "bB"], writes=["tA"])
                    P.op("pool", lambda e, ti=ti: e.tensor_copy(
                        out=mixT[:, :, ti * 128:(ti + 1) * 128], in_=tA[:].rearrange("q (p t) -> q p t", p=4)),
                        reads=["tA"], writes=["qT32"])
                for c in range(4):
                    psrc, pn = proj_feat(w_in_o, 1024 + c * 128)
                    P.op("act", lambda e, psrc=psrc: e.activation(out=tC[:], in_=psrc, func=AF.Silu),
                         reads=[pn], writes=["tC"])
                    P.op("dve", lambda e, c=c: e.tensor_tensor(out=tC[:], in0=tC[:], in1=mixT[:, c, :], op=ALU.mult),
                         reads=["tC", "qT32"], writes=["tC"])
                    psrc, pn = proj_feat(w_in_o, 0 + c * 128)
                    P.op("dve", lambda e, c=c, psrc=psrc: e.tensor_tensor(out=yT[:, c, :], in0=psrc, in1=tC[:], op=ALU.mult),
                         reads=[pn, "tC"], writes=["yT"])
                nkt = 4 * (blk + 1)
                for h in range(8 if not SKIP.get("sb") else 0):
                    p, hr = h // 2, (h % 2) * 64
                    acc = pACC[ctr["acc"] % 2]
                    accn = "pACC%d" % (ctr["acc"] % 2)
                    ctr["acc"] += 1
                    cbufs = [(pC, "pC"), (pACC[ctr["acc"] % 2], "pACC%d" % (ctr["acc"] % 2))]
                    kts = list(range(nkt - 1, -1, -1))
                    stt_ = {}

                    def sbA(idx):
                        kt = kts[idx]
                        si = ctr["s"] % 2
                        ctr["s"] += 1
                        pSn = "pS%d" % si
                        ei = ctr["e"] % 2
                        ctr["e"] += 1
                        stt_[idx] = ei
                        P.op("pe", lambda e: e.matmul(
                            pS[si][:], KT[hr:hr + 64, p, kt * 128:(kt + 1) * 128], qT[hr:hr + 64, p, :],
                            start=True, stop=True), reads=["KT%d" % (kt // 4), "qT"], writes=[pSn])
                        if kt >= nkt - 4:
                            j = kt - (nkt - 4)
                            P.op("dve", lambda e: e.scalar_tensor_tensor(
                                out=X[ei][:], in0=pS[si][:], scalar=SCALE, in1=pen[:, j * 512:(j + 1) * 512],
                                op0=ALU.mult, op1=ALU.add), reads=[pSn, "pen"], writes=["X%d" % ei])
                            P.op("act", lambda e: e.activation(out=E[ei][:], in_=X[ei][:], func=AF.Exp),
                                 reads=["X%d" % ei], writes=["E%d" % ei])
                        else:
                            P.op("act", lambda e: e.activation(out=E[ei][:], in_=pS[si][:], func=AF.Exp, scale=SCALE),
                                 reads=[pSn], writes=["E%d" % ei])
                        P.op("act", lambda e: e.activation(out=L[ei][:], in_=E[ei][:], func=AF.Ln, bias=1.0),
                             reads=["E%d" % ei], writes=["L%d" % ei])

                    def sbB(idx):
                        kt = kts[idx]
                        ei = stt_[idx]
                        cb, cbn = cbufs[idx % 2]
                        P.op("pe", lambda e: e.matmul(cb[:], tri[:], L[ei][:], start=True, stop=(idx == 0)),
                             reads=["tri", "L%d" % ei], writes=[cbn])
                        if idx > 0:
                            P.op("pe", lambda e: e.matmul(cb[:], onesm[:], Lsumb[:], start=False, stop=True),
                                 reads=["onesm", "Lsumb"], writes=[cbn])
                        if idx < nkt - 1:
                            if idx == 0:
                                P.op("dve", lambda e: e.tensor_copy(out=Lsumb[:], in_=L[ei][:]), reads=["L%d" % ei], writes=["Lsumb"])
                                P.op("pool", lambda e: e.tensor_copy(out=Lsum[:], in_=L[ei][:]),
                                     reads=["L%d" % ei], writes=["Lsum"])
                            else:
                                P.op("dve", lambda e: e.tensor_tensor(out=Lsumb[:], in0=Lsum[:], in1=L[ei][:], op=ALU.add),
                                     reads=["L%d" % ei, "Lsum"], writes=["Lsumb"])
                                P.op("pool", lambda e: e.tensor_tensor(out=Lsum[:], in0=Lsum[:], in1=L[ei][:], op=ALU.add),
                                     reads=["L%d" % ei, "Lsum"], writes=["Lsum"])
                        P.op("act", lambda e: e.activation(out=X[ei][:], in_=cb[:], func=AF.Exp, scale=-1.0),
                             reads=[cbn], writes=["X%d" % ei])
                        P.op("dve", lambda e: e.tensor_tensor(out=PT[ei][:], in0=E[ei][:], in1=X[ei][:], op=ALU.mult),
                             reads=["E%d" % ei, "X%d" % ei], writes=["PT%d" % ei])
                        P.op("pe", lambda e: e.matmul(
                            acc[:], V[:, kt, p * 128:(p + 1) * 128], PT[ei][:],
                            start=(idx == 0), stop=(idx == nkt - 1)), reads=["V%d" % kt, "PT%d" % ei], writes=[accn])

                    for i_ in range(nkt + 1):
                        if i_ < nkt:
                            sbA(i_)
                        if i_ >= 1:
                            sbB(i_ - 1)
                    P.op("dve", lambda e, acc=acc: e.tensor_tensor(out=yT[hr:hr + 64, 4 + p, :], in0=acc[hr:hr + 64, :],
                                                                   in1=zs[hr:hr + 64, p, :], op=ALU.mult),
                         reads=[accn, "zs"], writes=["yT"])
                for ti in range(4):
                    rr = r0 + ti * 128
                    out_proj_tile(blk, ti, x1d[R0 + rr:R0 + rr + 128, :], o_y[R0 + rr:R0 + rr + 128, :], rname="x1d")

        if do_sample:
            pidx = sb("pidx", [128, 1], F32)
            Acon = sb("Acon", [128, 32], F32)
            BMc = sb("BMc", [128, 32], F32)
            BSc = sb("BSc", [128, 32], F32)
            HMc = sb("HMc", [32, 8], F32)
            ohc = sb("ohc", [128, 1024], BF16)
            repc = sb("repc", [4, 128], F32)
            bdm = sb("bdm", [128, 128], F32)
            trif = sb("trif", [128, 128], F32)
            onesf = sb("onesf", [128, 128], F32)
            for tl, nm, c0, w, q in ((pidx, "pidx", C_PIDX, 1, "sp"), (Acon, "Acon", C_A, 32, "sp"), (BMc, "BMc", C_BM, 32, "sp"),
                                     (BSc, "BSc", C_BS, 32, "sp"), (ohc, "ohc", C_OHC, 1024, "pool"), (bdm, "bdm", C_BD, 128, "sp"),
                                     (trif, "trif", C_TRI, 128, "sp"), (onesf, "onesf", C_ONES, 128, "sp")):
                P.dma(q, lambda e, tl=tl, c0=c0, w=w: e.dma_start(out=tl[:], in_=consts[:, c0:c0 + w], allow_slow_non_contiguous=True), writes=[nm])
            P.dma("sp", lambda e: e.dma_start(out=HMc[:], in_=consts[0:32, C_HM:C_HM + 8]), writes=["HMc"])
            P.dma("sp", lambda e: e.dma_start(out=repc[:], in_=consts[0:4, C_REP:C_REP + 128]), writes=["repc"])
            ueb = sb("ueb", [128, 4, 32, 6], F32)
            QF = sb("QF", [128, 4, 4, 8], F32)
            QB = sb("QB", [128, 4, 4, 8], BF16)
            kssb = junk[0:32, 512:1024]
            ksT = sb("ksT", [128, 4, 32], F32)
            gs = sb("gs", [32, 32], F32)
            m8s = sb("m8s", [32, 8], F32)
            sels = sb("sels", [32, 32], F32)
            selTs = sb("selTs", [32, 32], F32)
            D3 = Lsum[:].bitcast(BF16)[0:32, :].rearrange("n (a b) -> n a b", a=32)
            PTsum = sb("PTsum", [128, 32], F32)
            rdn = sb("rdn", [32, 1], F32)
            ytmp = junk[0:32, 0:512]
            ysb = sb("ysb", [32, 64], F32)
            cvs = Lsumb
            BDg = wmT
            w4 = sb("w4", [4, 4], F32)
            m1 = sb("m1", [4, 128], F32)
            bBs = bB[:].rearrange("p c (s t) -> p c s t", t=4)
            barc = sb("barc", [128, 1], F32)
            NCOL = 65 * 32
            sE = arena[:, 0:2080]
            sL = arena[:, 2080:4160]
            sCw = arena[:, 4160:6240]
            sTA = arena[:, 6240:8320]
            sPT = arena[:, 8320:9360].bitcast(BF16)
            sIdx = arena[:, 9360:11408].bitcast(I32)
            sPtf = arena[:, 11408:13456]
            sPti = arena[:, 13456:15504].bitcast(I32)
            Kpg = [E[0][:].bitcast(BF16)[:, 0:512], E[0][:].bitcast(BF16)[:, 512:1024],
                   E[1][:].bitcast(BF16)[:, 0:512], E[1][:].bitcast(BF16)[:, 512:1024]]
            Vpg = Kpg
            KTp = [X[0][:].bitcast(BF16)[:, 0:512], X[0][:].bitcast(BF16)[:, 512:1024]]

            def do_barrier():
                P.barrier([
                    ("pe", lambda e: e.matmul(pC[0:1, 0:1], onecol[:], onecol[:], start=True, stop=True)),
                    ("act", lambda e: e.activation(out=barc[:], in_=onecol[:], func=AF.Copy)),
                    ("dve", lambda e: e.memset(barc[:], 0.0)),
                    ("pool", lambda e: e.memset(barc[:], 0.0)),
                    ("sp", lambda e: e.dma_start(out=barc[0:1, 0:1], in_=consts[0:1, 0:1])),
                    ("pe", lambda e: e.matmul(pC[0:1, 0:1], onecol[:], onecol[:], start=True, stop=True)),
                    ("act", lambda e: e.activation(out=barc[:], in_=onecol[:], func=AF.Copy)),
                    ("dve", lambda e: e.memset(barc[:], 0.0)),
                    ("pool", lambda e: e.memset(barc[:], 0.0)),
                ])

            def sample_phase(layer):
                do_barrier()
                w_in = w_in_e if layer == 0 else w_in_o
                src = xs if layer == 0 else xs1d
                dst = xs1d if layer == 0 else o_ys
                okd, ovd = (o_kms, o_vms) if layer == 0 else (o_kss, o_vss)
                ckd, cvd = (ck_m, cv_m) if layer == 0 else (ck_s, cv_s)
                qoff = 2048 if layer == 0 else 1536
                norm_transpose(src[0:128, :], 0, rname=("xs1d" if layer == 1 else None))
                nidx = nsq * 64
                P.dma("sp", lambda e: e.dma_start(out=sPti[:, 0:nidx], in_=ptab.partition_broadcast(128)), writes=["sPti"])
                P.op("dve", lambda e: e.tensor_copy(out=sPtf[:, 0:nidx], in_=sPti[:, 0:nidx]), reads=["sPti"], writes=["sPtf"])
                P.op("dve", lambda e: e.tensor_scalar(out=sPtf[:, 0:nidx], in0=sPtf[:, 0:nidx], scalar1=128.0, scalar2=pidx[:, 0:1],
                                                      op0=ALU.mult, op1=ALU.add), reads=["sPtf", "pidx"], writes=["sPtf"])
                P.op("dve", lambda e: e.tensor_copy(out=sIdx[:, 0:nidx], in_=sPtf[:, 0:nidx]), reads=["sPtf"], writes=["sIdx"])
                wt, wtn = load_wtok(w_in, qoff + 512)
                psrc, pn = proj_tok(wt, wtn, 0)
                tok_store(psrc, pn, okd[:, :])
                wt, wtn = load_wtok(w_in, qoff + 1024)
                psrc, pn = proj_tok(wt, wtn, 0)
                P.op("dve", lambda e, psrc=psrc: e.tensor_copy(out=cvb[:], in_=psrc), reads=[pn], writes=["cvb"])
                tok_store(psrc, pn, ovd[:, :])
                for p in range(4):
                    psrc, pn = proj_feat(w_in, qoff + p * 128, ntok=128)
                    P.op("dve", lambda e, p=p, psrc=psrc: e.tensor_copy(out=qT32[:, p, 0:128], in_=psrc), reads=[pn], writes=["qT32"])
                for p in range(4):
                    psrc, pn = proj_feat(w_in, qoff + 512 + p * 128, ntok=128)
                    P.op("act", lambda e, p=p, psrc=psrc: e.activation(out=qT[:, p, 0:128], in_=psrc, func=AF.Copy), reads=[pn], writes=["qT"])
                for p in range(4):
                    psrc, pn = proj_feat(w_in, qoff + 1536 + p * 128, ntok=128)
                    P.op("act", lambda e, p=p, psrc=psrc: e.activation(out=zs[:, p, 0:128], in_=psrc, func=AF.Silu), reads=[pn], writes=["zs"])
                if layer == 0:
                    P.dma("sp", lambda e: e.dma_start(out=tokf[0][0:64, :], in_=stc), writes=["tokf0"])
                    for c in range(4):
                        P.op("pe", lambda e, c=c: e.transpose(pC[:, 0:64], tokf[0][0:64, c * 128:(c + 1) * 128], identf[0:64, 0:64]),
                             reads=["tokf0", "identf"], writes=["pC"])
                        P.op("dve", lambda e, c=c: e.tensor_copy(out=ueb[:, c, :, 0:2], in_=pC[:, 0:64].rearrange("p (s j) -> p s j", j=2)),
                             reads=["pC"], writes=["ueb"])
                    for c in range(4):
                        psrc, pn = proj_feat(w_in, 512 + c * 128, ntok=128)
                        P.op("act", lambda e, psrc=psrc: e.activation(out=tA[:, 0:128], in_=psrc, func=AF.Copy), reads=[pn], writes=["tA"])
                        psrc, pn = proj_feat(w_in, 1024 + c * 128, ntok=128)
                        P.op("dve", lambda e, c=c, psrc=psrc: e.tensor_tensor(
                            out=ueb[:, c, :, 2:6], in0=psrc.rearrange("p (s t) -> p s t", t=4),
                            in1=tA[:, 0:128].rearrange("p (s t) -> p s t", t=4), op=ALU.mult), reads=[pn, "tA"], writes=["ueb"])
                        tBv = tB[:, 0:128].rearrange("p (s t) -> p s t", t=4)
                        P.op("dve", lambda e, c=c, tBv=tBv: e.tensor_scalar(out=tBv, in0=ueb[:, c, :, 0:4], scalar1=cwT[:, c, 0:1],
                                                                            scalar2=None, op0=ALU.mult), reads=["ueb", "cwT"], writes=["tB"])
                        for j in (1, 2):
                            P.op("dve", lambda e, c=c, j=j, tBv=tBv: e.scalar_tensor_tensor(
                                out=tBv, in0=ueb[:, c, :, j:j + 4], scalar=cwT[:, c, j:j + 1], in1=tBv, op0=ALU.mult, op1=ALU.add),
                                reads=["ueb", "cwT", "tB"], writes=["tB"])
                        psrc, pn = proj_feat(w_in, 1536 + c * 128, ntok=128)
                        P.op("act", lambda e, psrc=psrc: e.activation(out=tC[:, 0:128], in_=psrc, func=AF.Silu), reads=[pn], writes=["tC"])
                        P.op("dve", lambda e: e.tensor_tensor(out=tB[:, 0:128], in0=tB[:, 0:128], in1=tC[:, 0:128], op=ALU.mult),
                             reads=["tB", "tC"], writes=["tB"])
                        psrc, pn = proj_feat(w_in, 0 + c * 128, ntok=128)
                        P.op("dve", lambda e, c=c, psrc=psrc: e.tensor_tensor(out=yT[:, c, 0:128], in0=psrc, in1=tB[:, 0:128], op=ALU.mult),
                             reads=[pn, "tB"], writes=["yT"])
                        P.op("dve", lambda e, c=c: e.tensor_copy(out=tA[:, 0:64].rearrange("p (s j) -> p s j", j=2), in_=ueb[:, c, :, 4:6]),
                             reads=["ueb"], writes=["tA"])
                        P.op("pe", lambda e, c=c: e.transpose(pC[0:64, 128:256], tA[:, 0:64], identf[:]),
                             reads=["tA", "identf"], writes=["pC"])
                        P.op("dve", lambda e, c=c: e.tensor_copy(out=tokf[1][0:64, c * 128:(c + 1) * 128], in_=pC[0:64, 128:256]),
                             reads=["pC"], writes=["tokf1"])
                    P.dma("sp", lambda e: e.dma_start(out=o_convs[:, :], in_=tokf[1][0:64, :]), reads=["tokf1"], writes=["o_convs"])
                else:
                    for g in range(8):
                        P.dma("sp", lambda e, g=g: e.dma_start(out=w4[:], in_=gmlp_w[g, 0:4, 0:4]), writes=["w4"])
                        P.op("pe", lambda e: e.matmul(pC[0:4, 0:128], w4[:], repc[:], start=True, stop=True),
                             reads=["w4", "repc"], writes=["pC"])
                        P.op("dve", lambda e: e.tensor_copy(out=m1[:], in_=pC[0:4, 0:128]), reads=["pC"], writes=["m1"])
                        P.op("pe", lambda e: e.matmul(pC[:, 128:256], repc[:], m1[:], start=True, stop=True),
                             reads=["repc", "m1"], writes=["pC"])
                        P.op("dve", lambda e, g=g: e.tensor_tensor(out=BDg[:, g, :], in0=pC[:, 128:256], in1=bdm[:], op=ALU.mult),
                             reads=["pC", "bdm"], writes=["wmT"])
                        P.dma("sp", lambda e, g=g: e.dma_start(
                            out=bBs[(g % 2) * 64:(g % 2) * 64 + 64, g // 2, :, :],
                            in_=gmlp_b[g, 0:4].partition_broadcast(64).unsqueeze(1).to_broadcast([64, 32, 4])), writes=["bB"])
                    wt, wtn = load_wtok(w_in, 512)
                    psrc, pn = proj_tok(wt, wtn, 0)
                    P.op("dve", lambda e, psrc=psrc: e.tensor_copy(out=cvs[:], in_=psrc), reads=[pn], writes=["Lsumb"])
                    tok_store(psrc, pn, o_gvs[:, :])
                    for p in range(4):
                        for j in range(2):
                            P.op("pe", lambda e, p=p, j=j: e.matmul(pC[:, j * 128:(j + 1) * 128], cvs[:, p * 128:(p + 1) * 128], BDg[:, 2 * p + j, :],
                                                                    start=True, stop=True), reads=["Lsumb", "wmT"], writes=["pC"])
                        for j in range(2):
                            P.op("dve", lambda e, p=p, j=j: e.tensor_tensor(
                                out=tA[j * 64:(j + 1) * 64, p * 128:(p + 1) * 128], in0=pC[j * 64:(j + 1) * 64, j * 128:(j + 1) * 128],
                                in1=bBs[j * 64:(j + 1) * 64, p, :, :].rearrange("q s t -> q (s t)"), op=ALU.add),
                                reads=["pC", "bB"], writes=["tA"])
                    for c in range(4):
                        psrc, pn = proj_feat(w_in, 1024 + c * 128, ntok=128)
                        P.op("act", lambda e, psrc=psrc: e.activation(out=tC[:, 0:128], in_=psrc, func=AF.Silu), reads=[pn], writes=["tC"])
                        P.op("dve", lambda e, c=c: e.tensor_tensor(out=tC[:, 0:128], in0=tC[:, 0:128], in1=tA[:, c * 128:(c + 1) * 128], op=ALU.mult),
                             reads=["tC", "tA"], writes=["tC"])
                        psrc, pn = proj_feat(w_in, 0 + c * 128, ntok=128)
                        P.op("dve", lambda e, c=c, psrc=psrc: e.tensor_tensor(out=yT[:, c, 0:128], in0=psrc, in1=tC[:, 0:128], op=ALU.mult),
                             reads=[pn, "tC"], writes=["yT"])
                P.op("dve", lambda e: e.memset(ytmp[:], 0.0), writes=["junk"])
                for r4 in range(4):
                    P.dma("sp", lambda e, r4=r4: e.dma_start(out=ybd[r4 * 32:(r4 + 1) * 32, :], in_=ytmp[:]), reads=["junk"], writes=["ybd"])
                P.op("dve", lambda e: e.memset(QF[:], 0.0), writes=["QF"])
                bank = 0
                for s_ in range(nsq):
                    for p in range(4):
                        for j in range(2):
                            h = 2 * p + j
                            P.op("dve", lambda e, p=p, j=j, h=h, s_=s_: e.tensor_copy(
                                out=QF[j * 64:(j + 1) * 64, p, :, h], in_=qT32[j * 64:(j + 1) * 64, p, s_ * 4:(s_ + 1) * 4]),
                                reads=["qT32"], writes=["QF"])
                    P.op("dve", lambda e: e.tensor_copy(out=QB[:], in_=QF[:]), reads=["QF"], writes=["QB"])
                    for p in range(65):
                        if p < 64:
                            kp, kn = Kpg[p % 4], "sPG%d" % (p % 4)
                            col = s_ * 64 + p
                            P.dma("pool", lambda e, kp=kp, col=col: e.indirect_dma_start(
                                out=kp, out_offset=None, in_=ckd, in_offset=bass.IndirectOffsetOnAxis(ap=sIdx[:, col:col + 1], axis=0)),
                                reads=["sIdx"], writes=[kn])
                            for c in range(4):
                                P.op("pe", lambda e, c=c, kp=kp: e.transpose(pT[:, c * 128:(c + 1) * 128], kp[:, c * 128:(c + 1) * 128], ident[:]),
                                     reads=[kn, "ident"], writes=["pT"])
                            ktp, ktn = KTp[p % 2], "sKT%d" % (p % 2)
                            if p % 2 == 0:
                                P.op("act", lambda e, ktp=ktp: e.activation(out=ktp, in_=pT[:, 0:512], func=AF.Copy), reads=["pT"], writes=[ktn])
                            else:
                                P.op("dve", lambda e, ktp=ktp: e.tensor_copy(out=ktp, in_=pT[:, 0:512]), reads=["pT"], writes=[ktn])
                            if layer == 0:
                                n = p // 2
                                P.op("pe", lambda e, n=n, kp=kp, p=p: e.matmul(pACC[1][0:32, :], ohc[:, n * 32:(n + 1) * 32], kp,
                                                                               start=(p == 0), stop=(p == 63)),
                                     reads=["ohc", kn], writes=["pACC1"])
                        pcol = (p % 16) * 32
                        pSn = "pS%d" % bank
                        for c in range(4):
                            if p < 64:
                                P.op("pe", lambda e, c=c, ktp=ktp, bank=bank, pcol=pcol: e.matmul(
                                    pS[bank][:, pcol:pcol + 32], ktp[:, c * 128:(c + 1) * 128], QB[:, c, :, :].rearrange("p a b -> p (a b)"),
                                    start=(c == 0), stop=(c == 3)), reads=[ktn, "QB"], writes=[pSn])
                            else:
                                P.op("pe", lambda e, c=c, bank=bank, pcol=pcol: e.matmul(
                                    pS[bank][:, pcol:pcol + 32], qT[:, c, 0:128], QB[:, c, :, :].rearrange("p a b -> p (a b)"),
                                    start=(c == 0), stop=(c == 3)), reads=["qT", "QB"], writes=[pSn])
                        if p % 16 == 15 or p == 64:
                            c0 = (p // 16) * 512
                            wdt = pcol + 32
                            P.op("act", lambda e, bank=bank, c0=c0, wdt=wdt: e.activation(out=sE[:, c0:c0 + wdt], in_=pS[bank][:, 0:wdt],
                                                                                         func=AF.Exp, scale=SCALE),
                                 reads=[pSn], writes=["sE"])
                            bank ^= 1
                    P.op("dve", lambda e, s_=s_: e.scalar_tensor_tensor(
                        out=sE[:, 2048:2080], in0=sE[:, 2048:2080], scalar=Acon[:, s_:s_ + 1], in1=(BMc if layer == 0 else BSc)[:],
                        op0=ALU.mult, op1=ALU.mult), reads=["sE", "Acon", "BMc", "BSc"], writes=["sE"])
                    if layer == 0:
                        P.op("act", lambda e: e.activation(out=kssb[:], in_=pACC[1][0:32, :], func=AF.Copy), reads=["pACC1"], writes=["junk"])
                        for c in range(4):
                            P.op("pe", lambda e, c=c: e.transpose(pC[:, c * 32:(c + 1) * 32], kssb[:, c * 128:(c + 1) * 128], identf[0:32, 0:32]),
                                 reads=["junk", "identf"], writes=["pC"])
                        P.op("dve", lambda e: e.tensor_copy(out=ksT[:], in_=pC[:, 0:128].rearrange("p (c n) -> p c n", c=4)),
                             reads=["pC"], writes=["ksT"])
                        for c in range(4):
                            P.op("pe", lambda e, c=c: e.matmul(pC[0:32, 128:160], QF[:, c, :, :].rearrange("p a b -> p (a b)"), ksT[:, c, :],
                                                               start=(c == 0), stop=(c == 3)), reads=["QF", "ksT"], writes=["pC"])
                        P.op("dve", lambda e: e.tensor_copy(out=gs[:], in_=pC[0:32, 128:160]), reads=["pC"], writes=["gs"])
                        P.op("dve", lambda e: e.max(out=m8s[:], in_=gs[:]), reads=["gs"], writes=["m8s"])
                        P.op("dve", lambda e: e.tensor_scalar(out=sels[:], in0=gs[:], scalar1=m8s[:, 2:3], scalar2=None, op0=ALU.is_ge),
                             reads=["gs", "m8s"], writes=["sels"])
                        P.op("pe", lambda e: e.transpose(pC[0:32, 160:192], sels[:], identf[0:32, 0:32]), reads=["sels", "identf"], writes=["pC"])
                        P.op("dve", lambda e: e.tensor_copy(out=selTs[:], in_=pC[0:32, 160:192]), reads=["pC"], writes=["selTs"])
                        P.op("dve", lambda e: e.tensor_tensor(out=D3[:], in0=identf[0:32, 0:32].unsqueeze(2).to_broadcast([32, 32, 32]),
                                                              in1=selTs[:].unsqueeze(1).to_broadcast([32, 32, 32]), op=ALU.mult),
                             reads=["identf", "selTs"], writes=["Lsum"])
                        for hf in range(2):
                            P.op("pe", lambda e, hf=hf: e.matmul(pS[bank][:], onesm[0:32, :], D3[:, hf * 16:(hf + 1) * 16, :].rearrange("n a b -> n (a b)"),
                                                                 start=True, stop=True), reads=["onesm", "Lsum"], writes=["pS%d" % bank])
                            for j in range(2):
                                Ev = sE[:, hf * 1024:(hf + 1) * 1024].rearrange("k (n j c) -> k n j c", j=2, c=32)
                                Pv = sPT[:, hf * 1024:(hf + 1) * 1024].rearrange("k (n j c) -> k n j c", j=2, c=32)
                                P.op("dve", lambda e, j=j, Ev=Ev, Pv=Pv, bank=bank: e.tensor_tensor(
                                    out=Pv[:, :, j, :], in0=Ev[:, :, j, :], in1=pS[bank][:].rearrange("k (n c) -> k n c", c=32), op=ALU.mult),
                                    reads=["sE", "pS%d" % bank], writes=["sPT"])
                            bank ^= 1
                        P.op("dve", lambda e: e.tensor_copy(out=sPT[:, 2048:2080], in_=sE[:, 2048:2080]), reads=["sE"], writes=["sPT"])
                        P.op("dve", lambda e: e.tensor_reduce(out=PTsum[:], in_=sPT[:, 0:NCOL].rearrange("k (p c) -> k c p", c=32),
                                                              axis=AX.X, op=ALU.add), reads=["sPT"], writes=["PTsum"])
                        P.op("pe", lambda e: e.matmul(pC[0:32, 200:201], PTsum[:], onecol[:], start=True, stop=True),
                             reads=["PTsum", "onecol"], writes=["pC"])
                        P.op("dve", lambda e: e.reciprocal(out=rdn[:], in_=pC[0:32, 200:201]), reads=["pC"], writes=["rdn"])
                    else:
                        P.op("act", lambda e: e.activation(out=sL[:, 0:NCOL], in_=sE[:, 0:NCOL], func=AF.Ln, bias=1.0), reads=["sE"], writes=["sL"])
                        for k in range(5):
                            c0 = k * 512
                            wdt = min(512, NCOL - c0)
                            P.op("pe", lambda e, c0=c0, wdt=wdt: e.matmul(pC[:, 0:wdt], trif[:], sL[:, c0:c0 + wdt], start=True, stop=True),
                                 reads=["trif", "sL"], writes=["pC"])
                            P.op("dve", lambda e, c0=c0, wdt=wdt: e.tensor_copy(out=sCw[:, c0:c0 + wdt], in_=pC[:, 0:wdt]), reads=["pC"], writes=["sCw"])
                            P.op("pe", lambda e, c0=c0, wdt=wdt: e.matmul(pC[:, 0:wdt], onesf[:], sL[:, c0:c0 + wdt], start=True, stop=True),
                                 reads=["onesf", "sL"], writes=["pC"])
                            P.op("dve", lambda e, c0=c0, wdt=wdt: e.tensor_copy(out=sTA[:, c0:c0 + wdt], in_=pC[:, 0:wdt]), reads=["pC"], writes=["sTA"])
                        P.op("dve", lambda e: e.tensor_copy(out=sL[:, 0:2048], in_=sTA[:, 32:2080]), reads=["sTA"], writes=["sL"])
                        P.op("dve", lambda e: e.memset(sL[:, 2048:2080], 0.0), reads=["sTA"], writes=["sL"])
                        bufs = [(sL, "sL"), (sTA, "sTA")]
                        cur = 0
                        st_ = 1
                        while st_ < 65:
                            a, an = bufs[cur]
                            b, bn = bufs[1 - cur]
                            nh = (65 - st_) * 32
                            P.op("dve", lambda e, a=a, b=b, nh=nh, st_=st_: e.tensor_tensor(out=b[:, 0:nh], in0=a[:, 0:nh], in1=a[:, st_ * 32:NCOL], op=ALU.add),
                                 reads=[an], writes=[bn])
                            P.op("dve", lambda e, a=a, b=b, nh=nh: e.tensor_copy(out=b[:, nh:NCOL], in_=a[:, nh:NCOL]), reads=[an], writes=[bn])
                            cur = 1 - cur
                            st_ *= 2
                        r_, rn = bufs[cur]
                        o_, on = bufs[1 - cur]
                        P.op("dve", lambda e, r_=r_: e.tensor_tensor(out=sCw[:, 0:NCOL], in0=sCw[:, 0:NCOL], in1=r_[:, 0:NCOL], op=ALU.add),
                             reads=["sCw", rn], writes=["sCw"])
                        P.op("act", lambda e, o_=o_: e.activation(out=o_[:, 0:NCOL], in_=sCw[:, 0:NCOL], func=AF.Exp, scale=-1.0),
                             reads=["sCw"], writes=[on])
                        P.op("dve", lambda e, o_=o_: e.tensor_tensor(out=sPT[:, 0:NCOL], in0=sE[:, 0:NCOL], in1=o_[:, 0:NCOL], op=ALU.mult),
                             reads=["sE", on], writes=["sPT"])
                    for p in range(64):
                        vp, vn = Vpg[p % 4], "sPG%d" % (p % 4)
                        col = s_ * 64 + p
                        P.dma("pool", lambda e, vp=vp, col=col: e.indirect_dma_start(
                            out=vp, out_offset=None, in_=cvd, in_offset=bass.IndirectOffsetOnAxis(ap=sIdx[:, col:col + 1], axis=0)),
                            reads=["sIdx"], writes=[vn])
                        P.op("pe", lambda e, vp=vp, p=p: e.matmul(pACC[0][0:32, :], sPT[:, p * 32:(p + 1) * 32], vp, start=(p == 0), stop=False),
                             reads=["sPT", vn], writes=["pACC0"])
                    P.op("pe", lambda e: e.matmul(pACC[0][0:32, :], sPT[:, 2048:2080], cvb[:], start=False, stop=True),
                         reads=["sPT", "cvb"], writes=["pACC0"])
                    P.op("dve", lambda e: e.tensor_tensor(out=ytmp[:].rearrange("r (h d) -> r h d", h=8),
                                                          in0=pACC[0][0:32, :].rearrange("r (h d) -> r h d", h=8),
                                                          in1=HMc[:].unsqueeze(2).to_broadcast([32, 8, 64]), op=ALU.mult),
                         reads=["pACC0", "HMc"], writes=["junk"])
                    P.op("dve", lambda e: e.tensor_reduce(out=ysb[:], in_=ytmp[:].rearrange("r (h d) -> r d h", h=8), axis=AX.X, op=ALU.add),
                         reads=["junk"], writes=["ysb"])
                    if layer == 0:
                        P.op("dve", lambda e: e.tensor_scalar(out=ysb[:], in0=ysb[:], scalar1=rdn[:, 0:1], scalar2=None, op0=ALU.mult),
                             reads=["ysb", "rdn"], writes=["ysb"])
                    P.dma("sp", lambda e, s_=s_: e.dma_start(out=ybd[s_ * 4:(s_ + 1) * 4, :].rearrange("q (h d) -> (q h) d", d=64), in_=ysb[:]),
                          reads=["ysb"], writes=["ybd"])
                P.dma("sp", lambda e: e.dma_start(out=tokf[0][:], in_=ybd[:, :]), reads=["ybd"], writes=["tokf0"])
                for c in range(4):
                    P.op("pe", lambda e, c=c: e.transpose(pC[:, c * 128:(c + 1) * 128], tokf[0][:, c * 128:(c + 1) * 128], identf[:]),
                         reads=["tokf0", "identf"], writes=["pC"])
                P.op("dve", lambda e: e.tensor_tensor(out=yT[:, 4:8, 0:128], in0=pC[:].rearrange("p (c t) -> p c t", c=4),
                                                      in1=zs[:, :, 0:128], op=ALU.mult), reads=["pC", "zs"], writes=["yT"])
                out_proj_tile(0, 0, src[0:128, :], dst[0:128, :], rname=("xs1d" if layer == 1 else None),
                              wname=("xs1d" if layer == 0 else "dram_res"))
                do_barrier()

        mixT = qT32

        P.stop_at = stop_at
        try:
            layer0_setup()
            if do_prompt:
                for sq in range(npr):
                    layer0(sq)
            if do_sample:
                sample_phase(0)
            layer1_setup()
            if do_prompt:
                for sq in range(npr):
                    layer1(sq)
            if do_sample:
                sample_phase(1)
        except StopBuild:
            pass
        try:
            print("sbuf bytes remaining", nc.sbuf_bytes_remaining, "n_ops", len(P.ops))
        except Exception:
            pass
        P.emit()
    return nc


NCORES = 2


def kernel(**inputs):
    f32 = np.float32
    n = NCORES
    npr, nsq = 4 // n, 32 // n
    consts = make_consts()
    shared = {k: np.ascontiguousarray(inputs[k][0], dtype=f32) for k in
              ("norm_pre_e", "norm_post_e", "norm_pre_o", "norm_post_o", "w_in_e", "w_out_e",
               "w_in_o", "w_out_o", "conv_w", "gmlp_w", "gmlp_b")}
    n_phys = inputs["cache_k_moba"].shape[1]
    pools = {"ck_m": "cache_k_moba", "cv_m": "cache_v_moba", "ck_s": "cache_k_sb", "cv_s": "cache_v_sb"}
    for k, src in pools.items():
        shared[k] = np.ascontiguousarray(inputs[src][0], dtype=f32).reshape(n_phys * 128, 512)
    in_maps = []
    for c in range(n):
        m = dict(shared)
        m["consts"] = consts
        m["xp"] = np.ascontiguousarray(inputs["x_prompt"][c * npr:(c + 1) * npr], dtype=f32).reshape(npr * SEQ, D)
        xs = np.zeros((128, D), f32)
        xs[:4 * nsq] = np.asarray(inputs["x_sample"][c * nsq:(c + 1) * nsq], dtype=f32).reshape(4 * nsq, D)
        m["xs"] = xs
        stc = np.zeros((64, 512), f32)
        stc[:2 * nsq] = np.asarray(inputs["state_conv"][0, c * nsq:(c + 1) * nsq], dtype=f32).reshape(2 * nsq, 512)
        m["stc"] = stc
        m["ptab"] = np.ascontiguousarray(inputs["page_table"][c * nsq:(c + 1) * nsq], dtype=np.int32).reshape(-1)
        in_maps.append(m)
    nc = build(do_prompt=True, do_sample=True, n_phys=n_phys, npr=npr, nsq=nsq)
    res = run_bass_kernel_spmd(nc, in_maps, core_ids=list(range(n))).results

    def cat(name, rows, shape):
        return np.concatenate([np.asarray(r[name])[:rows] for r in res], axis=0).reshape(shape).astype(f32)

    nb, nd = 4, 32
    y_prompt = cat("y_prompt", npr * SEQ, (nb, SEQ, D))
    y_sample = cat("y_sample", 4 * nsq, (nd, 4, D))
    conv_prompt = cat("conv_prompt", npr * 2, (1, nb, 2, 512))
    conv_sample = cat("conv_sample", 2 * nsq, (1, nd, 2, 512))
    kvp = lambda name: cat(name, npr * SEQ, (1, nb, SEQ, 8, 64))
    kvs = lambda name: cat(name, 4 * nsq, (1, nd, 4, 8, 64))
    return (y_prompt, y_sample, conv_prompt, conv_sample,
            kvp("k_moba_prompt"), kvp("v_moba_prompt"), kvs("k_moba_sample"), kvs("v_moba_sample"),
            kvp("k_sb_prompt"), kvp("v_sb_prompt"), kvs("k_sb_sample"), kvs("v_sb_sample"),
            cat("gmlp_v_prompt", npr * 128, (1, nb, 128, 512)), cat("gmlp_v_sample", 4 * nsq, (1, nd, 4, 512)))
```
